# Optimizing a Trainium2 kernel written in Bass

```python
import math
import jax, jax.numpy as jnp
from jax import lax
import numpy as np

D_MODEL = 1024
BATCH = 8
SEQ = 4096
DEPTH = 4

HEAD_DIM = 64
D_ATTN = D_MODEL // 2
ATTN_HEADS = D_ATTN // HEAD_DIM
Q_BLOCK = 128
D_CONV = D_MODEL // 4
CONV_K = 3
D_POOL = D_MODEL // 4
POOL_WINDOWS = (2, 4, 8, 16)
POOL_GROUPS = len(POOL_WINDOWS)
POOL_GROUP_DIM = D_POOL // POOL_GROUPS
POOL_OUT_DIM = D_MODEL // POOL_GROUPS
N_BRANCHES = 3
D_FF = -(-8 * D_MODEL // (3 * 256)) * 256
EPS = 1e-6

IN_SIZES = (D_ATTN, D_ATTN, D_ATTN, ATTN_HEADS, D_CONV, D_CONV, D_CONV, D_POOL, N_BRANCHES * D_MODEL)
D_IN = sum(IN_SIZES)
IN_SPLITS = tuple(int(v) for v in np.cumsum(IN_SIZES)[:-1])

kernel_name = "fox_conv_pool_gated_hybrid"


def rmsnorm(x, g):
    xf = x.astype(jnp.float32)
    y = xf * lax.rsqrt(jnp.mean(xf * xf, axis=-1, keepdims=True) + EPS)
    return (y * g.astype(jnp.float32)).astype(x.dtype)


def forgetting_attention(q, k, v, logf):
    S = q.shape[2]
    c = jnp.cumsum(logf, axis=-1)
    scale = HEAD_DIM ** -0.5
    outs = []
    for i in range(S // Q_BLOCK):
        q0, q1 = i * Q_BLOCK, (i + 1) * Q_BLOCK
        qb = q[:, :, q0:q1]
        kb = k[:, :, :q1]
        vb = v[:, :, :q1]
        logits = jnp.einsum('bhqd,bhkd->bhqk', qb, kb, preferred_element_type=jnp.float32) * scale
        logits = logits + (c[:, :, q0:q1, None] - c[:, :, None, :q1])
        causal = (q0 + jnp.arange(Q_BLOCK))[:, None] >= jnp.arange(q1)[None, :]
        logits = jnp.where(causal, logits, -jnp.inf)
        p = jax.nn.softmax(logits, axis=-1)
        outs.append(jnp.einsum('bhqk,bhkd->bhqd', p.astype(vb.dtype), vb))
    return jnp.concatenate(outs, axis=2)


def short_conv_mixer(u, b_gate, c_gate, w):
    S = u.shape[1]
    z = c_gate * u
    zp = jnp.pad(z, ((0, 0), (CONV_K - 1, 0), (0, 0)))
    conv = w[0] * zp[:, 0:S]
    for j in range(1, CONV_K):
        conv = conv + w[j] * zp[:, j:j + S]
    return b_gate * conv


def pooling_mixer(u, w_grp, scale):
    Bsz, S, _ = u.shape
    uf = u.astype(jnp.float32)
    cs = jnp.cumsum(uf, axis=1)
    t = jnp.arange(S, dtype=jnp.float32)
    groups = []
    for g, w in enumerate(POOL_WINDOWS):
        sl = slice(g * POOL_GROUP_DIM, (g + 1) * POOL_GROUP_DIM)
        csg = cs[:, :, sl]
        lagged = jnp.pad(csg[:, :S - w], ((0, 0), (w, 0), (0, 0)))
        counts = jnp.minimum(t + 1.0, float(w))[None, :, None]
        groups.append((csg - lagged) / counts - uf[:, :, sl])
    d = jnp.stack(groups, axis=2).astype(u.dtype)
    out = jnp.einsum('bsgc,gcd->bsgd', d, w_grp).reshape(Bsz, S, D_MODEL)
    return out * scale


def setup_inputs(seed: int = 0) -> dict:
    key = jax.random.key(seed)
    ks = jax.random.split(key, 16)
    nrm = lambda k, shape, fan_in: jax.random.normal(k, shape, jnp.float32) * (fan_in ** -0.5)
    return {
        "x": jax.random.normal(ks[0], (BATCH, SEQ, D_MODEL), jnp.float32),
        "norm_mix_g": 1.0 + 0.02 * jax.random.normal(ks[1], (DEPTH, D_MODEL), jnp.float32),
        "w_in": nrm(ks[2], (DEPTH, D_MODEL, D_IN), D_MODEL),
        "forget_b": jax.random.uniform(ks[3], (DEPTH, ATTN_HEADS), jnp.float32, 2.0, 5.0),
        "q_norm_g": 1.0 + 0.02 * jax.random.normal(ks[4], (DEPTH, HEAD_DIM), jnp.float32),
        "k_norm_g": 1.0 + 0.02 * jax.random.normal(ks[5], (DEPTH, HEAD_DIM), jnp.float32),
        "w_attn_out": nrm(ks[6], (DEPTH, D_ATTN, D_MODEL), D_ATTN),
        "conv_w": nrm(ks[7], (DEPTH, CONV_K, D_CONV), CONV_K),
        "w_conv_out": nrm(ks[8], (DEPTH, D_CONV, D_MODEL), D_CONV),
        "pool_w": nrm(ks[9], (DEPTH, POOL_GROUPS, POOL_GROUP_DIM, POOL_OUT_DIM), POOL_GROUP_DIM),
        "pool_scale": 1.0 + 0.1 * jax.random.normal(ks[10], (DEPTH, D_MODEL), jnp.float32),
        "w_o": nrm(ks[11], (DEPTH, D_MODEL, D_MODEL), D_MODEL),
        "norm_ffn_g": 1.0 + 0.02 * jax.random.normal(ks[12], (DEPTH, D_MODEL), jnp.float32),
        "w_ffn_in": nrm(ks[13], (DEPTH, D_MODEL, 2 * D_FF), D_MODEL),
        "w_ffn_out": nrm(ks[14], (DEPTH, D_FF, D_MODEL), D_FF),
    }


def reference(x, norm_mix_g, w_in, forget_b, q_norm_g, k_norm_g, w_attn_out, conv_w,
              w_conv_out, pool_w, pool_scale, w_o, norm_ffn_g, w_ffn_in, w_ffn_out):
    Bsz, S, _ = x.shape

    def heads(t):
        return t.reshape(Bsz, S, ATTN_HEADS, HEAD_DIM).transpose(0, 2, 1, 3)

    for l in range(DEPTH):
        h = rmsnorm(x, norm_mix_g[l])
        proj = h @ w_in[l]
        q, k, v, f_logit, cx, cb, cc, px, gate_logit = jnp.split(proj, IN_SPLITS, axis=-1)

        qh = rmsnorm(heads(q), q_norm_g[l])
        kh = rmsnorm(heads(k), k_norm_g[l])
        vh = heads(v)
        logf = jax.nn.log_sigmoid((f_logit + forget_b[l]).astype(jnp.float32)).transpose(0, 2, 1)
        a = forgetting_attention(qh, kh, vh, logf).transpose(0, 2, 1, 3).reshape(Bsz, S, D_ATTN)
        y_attn = a @ w_attn_out[l]

        y_conv = short_conv_mixer(cx, cb, cc, conv_w[l]) @ w_conv_out[l]

        y_pool = pooling_mixer(px, pool_w[l], pool_scale[l])

        g = jax.nn.sigmoid(gate_logit).reshape(Bsz, S, N_BRANCHES, D_MODEL)
        merged = g[:, :, 0] * y_attn + g[:, :, 1] * y_conv + g[:, :, 2] * y_pool
        x = x + merged @ w_o[l]

        h = rmsnorm(x, norm_ffn_g[l])
        gt, up = jnp.split(h @ w_ffn_in[l], 2, axis=-1)
        x = x + (jax.nn.silu(gt) * up) @ w_ffn_out[l]
    return x
```

```python
import contextlib
import numpy as np
import concourse.bass as bass
import concourse.mybir as mybir
from concourse.bass_utils import run_bass_kernel_spmd

F32 = mybir.dt.float32
BF16 = mybir.dt.bfloat16
AF = mybir.ActivationFunctionType
ALU = mybir.AluOpType

D = 1024
S = 4096
NCH = 8
CH = 512
KC = 8
D_IN = 5640
D_FF = 2816
NHB = 22
EPS = 1e-6
ENGS = ("pe", "act", "dve", "pool", "sp")

OQ, OK_, OV, OF, OCX, OCB, OCC, OPX, OG = 0, 512, 1024, 1536, 1544, 1800, 2056, 2312, 2568


class Op:
    __slots__ = ("eng", "fn", "deps", "is_dma", "slot", "epoch", "signal",
                 "ordinal", "cum", "ndma", "pos", "marker")


class Prog:
    def __init__(self):
        self.streams = {e: [] for e in ENGS}
        self.lastw = {}
        self.readers = {}
        self.epoch = 0
        self.slot_cum = {}
        self.regions = {}
        self.bufkeys = {}
        self.buf_fence = {}

    @staticmethod
    def _compress(deps):
        best_c = {}
        best_d = {}
        for w in deps:
            if w.is_dma:
                k = (w.slot, w.epoch)
                o = best_d.get(k)
                if o is None or w.cum > o.cum:
                    best_d[k] = w
            else:
                o = best_c.get(w.eng)
                if o is None or w.pos > o.pos:
                    best_c[w.eng] = w
        return list(best_c.values()) + list(best_d.values())

    def region(self, name, lo, hi):
        self.regions[name] = (lo, hi)

    def recycle(self, new_names):
        olds = set()
        for n in new_names:
            lo, hi = self.regions[n]
            for m, (l2, h2) in self.regions.items():
                if l2 < hi and lo < h2:
                    olds.add(m)
        deps = {}
        for n in olds:
            for k in self.bufkeys.get(n, ()):
                w = self.lastw.pop(k, None)
                if w is not None:
                    deps[id(w)] = w
                for r in self.readers.pop(k, ()):
                    deps[id(r)] = r
            self.bufkeys[n] = set()
            m = self.buf_fence.pop(n, None)
            if m is not None:
                for w in m.deps:
                    deps[id(w)] = w
        M = Op()
        M.marker = True
        M.deps = self._compress(deps.values())
        for n in new_names:
            self.buf_fence[n] = M

    def marker(self, reads=(), writes=()):
        deps = {}

        def put(w):
            if w.marker:
                for x in w.deps:
                    deps[id(x)] = x
            else:
                deps[id(w)] = w
        for k in reads:
            w = self.lastw.get(k)
            if w is not None:
                put(w)
        for k in writes:
            w = self.lastw.get(k)
            if w is not None:
                put(w)
            for r in self.readers.get(k, ()):
                put(r)
        M = Op()
        M.marker = True
        M.deps = self._compress(deps.values())
        for k in writes:
            self.lastw[k] = M
            self.readers[k] = []
        return M

    def add(self, eng, fn, reads=(), writes=(), dma=None, ndma=1):
        op = Op()
        op.marker = False
        op.eng = eng
        op.fn = fn
        op.is_dma = dma is not None
        op.slot = dma
        op.epoch = 0 if dma is not None else self.epoch
        op.signal = op.is_dma
        op.ordinal = 0
        op.ndma = ndma
        op.pos = len(self.streams[eng])
        deps = {}

        def put(w, raw):
            if w.marker:
                for x in w.deps:
                    if id(x) not in deps:
                        deps[id(x)] = (x, True)
            else:
                o = deps.get(id(w))
                if o is None or (raw and not o[1]):
                    deps[id(w)] = (w, raw)

        for k in reads:
            w = self.lastw.get(k)
            if w is None:
                w = self.buf_fence.get(k[0])
            if w is not None:
                put(w, True)
        for k in writes:
            w = self.lastw.get(k)
            if w is None:
                w = self.buf_fence.get(k[0])
            if w is not None:
                put(w, False)
            for r in self.readers.get(k, ()):
                put(r, False)
        final = []
        for w, raw in deps.values():
            if w is op:
                continue
            if (not w.is_dma) and (not op.is_dma) and w.eng == eng == "pe":
                continue
            final.append(w)
        final = self._compress(final)
        for w in final:
            w.signal = True
        op.deps = final
        if op.is_dma:
            key = (dma, 0)
            c = self.slot_cum.get(key, 0) + ndma
            self.slot_cum[key] = c
            op.cum = c
        for k in reads:
            self.readers.setdefault(k, []).append(op)
            self.bufkeys.setdefault(k[0], set()).add(k)
        for k in writes:
            self.lastw[k] = op
            self.readers[k] = []
            self.bufkeys.setdefault(k[0], set()).add(k)
        self.streams[eng].append(op)
        return op

    def emit(self, nc, stack):
        for e in ENGS:
            cnt = {}
            for op in self.streams[e]:
                if op.is_dma or not op.signal:
                    continue
                c = cnt.get(op.epoch, 0) + 1
                cnt[op.epoch] = c
                op.ordinal = c
        sems = {}

        def sem_for(key):
            if key not in sems:
                sems[key] = stack.enter_context(nc.semaphore("s%d" % len(sems)))
            return sems[key]

        for e in ENGS:
            for op in self.streams[e]:
                if op.is_dma:
                    sem_for(("d", op.slot, op.epoch))
                elif op.signal:
                    sem_for(("e", op.eng, op.epoch))
        self.nsems = len(sems)
        block = stack.enter_context(nc.Block())
        streams = self.streams

        def run(eng_name):
            def body(engine):
                seen = {}
                for op in streams[eng_name]:
                    need = {}
                    for d in op.deps:
                        if d.is_dma:
                            k = ("d", d.slot, d.epoch)
                            v = 16 * d.cum
                        else:
                            k = ("e", d.eng, d.epoch)
                            v = d.ordinal
                        if v > need.get(k, 0):
                            need[k] = v
                    for k, v in need.items():
                        if v > seen.get(k, 0):
                            engine.wait_ge(sems[k], v)
                            seen[k] = v
                    if op.fn is None:
                        continue
                    r = op.fn(engine)
                    if op.is_dma:
                        s = sems[("d", op.slot, op.epoch)]
                        if not isinstance(r, (list, tuple)):
                            r = [r]
                        assert len(r) == op.ndma
                        for inst in r:
                            inst.then_inc(s, 16)
                    elif op.signal:
                        if isinstance(r, (list, tuple)):
                            r = r[-1]
                        r.then_inc(sems[("e", op.eng, op.epoch)], 1)
            return body

        block.tensor(run("pe"))
        block.scalar(run("act"))
        block.vector(run("dve"))
        block.gpsimd(run("pool"))
        block.sync(run("sp"))


OFF_HT = 0
OFF_AT = 65536
OFF_C = 98304
OFF_D = 131072
OFF_E = 164352
AR_BYTES = 201728
NCST = 290
USE_PAIR = True
NVEC = 132


class _Stop(Exception):
    pass


def build(depth=4, dbg=(), stop=None):
    nc = bass.Bass("TRN2", target_bir_lowering=False)
    dram = lambda name, shape, dt=F32, kind="ExternalInput": nc.dram_tensor(name, shape, dt, kind=kind).ap()
    xT = dram("xT", [D, S])
    w_in = dram("w_in", [4, D, D_IN])
    w_ao = dram("w_attn_out", [4, 512, D])
    w_co = dram("w_conv_out", [4, 256, D])
    w_po = dram("pool_w", [4, 4, 64, 256])
    w_o = dram("w_o", [4, D, D])
    w_fi = dram("w_ffn_in", [4, D, 2 * D_FF])
    w_fo = dram("w_ffn_out", [4, D_FF, D])
    vecs_d = dram("vecs", [128, NVEC])
    cst_d = dram("cst", [128, NCST])
    yT = dram("yT", [D, S], kind="ExternalOutput")
    xs = dram("xs_scr", [D, S], kind="Internal")
    mT = dram("mT_scr", [D, S], BF16, kind="Internal")
    cpd = dram("cp_scr", [8, 4, S], BF16, kind="Internal")
    dbg_out = {}
    for nm, shape, dt in (("d_hT", [D, S], BF16), ("d_aT", [512, S], BF16), ("d_cbz", [256, S], BF16),
                          ("d_dT", [256, S], BF16), ("d_c", [8, S], F32), ("d_qa", [68, S], BF16),
                          ("d_ka", [68, S], BF16), ("d_v", [128, 32 * 8 * 65], BF16)):
        if nm in dbg:
            dbg_out[nm] = dram(nm, shape, dt, kind="ExternalOutput")

    P = Prog()
    st = contextlib.ExitStack()
    with st:
        sbt = lambda name, shape, dt: st.enter_context(nc.sbuf_tensor(name, shape, dt))
        arena = sbt("arena", [128, AR_BYTES // 2], BF16)
        cst = sbt("cstf", [128, NCST], F32)
        vec = sbt("vecf", [128, NVEC], F32)
        ident = sbt("ident", [128, 128], BF16)
        maskb = sbt("maskb", [128, 128], BF16)
        ones = sbt("onesb", [128, 128], BF16)
        sel = sbt("sel", [65, 64], BF16)
        gk8 = sbt("gk8", [128, 4], F32)
        nfb = sbt("nfb", [8, 4], F32)
        onef = sbt("onef", [8, 1], F32)
        bd = sbt("bd", [128, 128], BF16)
        PS = [st.enter_context(nc.psum_tensor("ps%d" % i, [128, 512], F32)) for i in range(8)]

        def V(name, off, nbytes, p0=0, p1=128, dt=BF16, pat=None, **kw):
            P.region(name, off, off + nbytes)
            ap = arena[p0:p1, off // 2:(off + nbytes) // 2]
            if dt == F32:
                ap = ap.bitcast(F32)
            if pat:
                ap = ap.rearrange(pat, **kw)
            return ap

        g1 = vec[:, 0:32].rearrange("p (l k) -> p l k", k=8)
        g2 = vec[:, 32:64].rearrange("p (l k) -> p l k", k=8)
        psc = vec[:, 64:96].rearrange("p (l k) -> p l k", k=8)
        cw = vec[:, 96:120].rearrange("p (l j b) -> p l j b", j=3, b=2)
        gq2 = vec[:, 120:124]
        gk = vec[:, 124:128]
        fb = vec[0:8, 128:132]
        invtab = [cst[:, 256:272], cst[:, 272:288]]
        invw = cst[:, 288:290]

        hT = V("hT", OFF_HT, 65536, pat="p (k n) -> p k n", n=S)
        aT = V("aT", OFF_AT, 32768, pat="p (k n) -> p k n", n=S)
        QA = [V("QA%d" % i, OFF_C + i * 8192, 8192) for i in range(2)]
        KA = [V("KA%d" % i, OFF_C + 16384 + i * 8192, 8192) for i in range(2)]
        VA = V("VA", OFF_D, 33280, pat="p (t h d) -> p t h d", h=8, d=65)

        pscnt = [0]

        def bank():
            b = pscnt[0] % 8
            pscnt[0] += 1
            return b

        def mm(ps_ap, pskey, lhsT, rhs, start, stop, reads):
            P.add("pe", lambda e: e.matmul(ps_ap, lhsT=lhsT, rhs=rhs, start=start, stop=stop),
                  reads=reads, writes=[pskey])

        def gemm(ps_ap, b, pairs, reads):
            n = len(pairs)
            for i, (l, r) in enumerate(pairs):
                mm(ps_ap, ("ps", b), l, r, i == 0, i == n - 1, reads)

        def wdma(out_ap, in_ap, key, slot, n=1):
            P.add("pool", lambda e: e.dma_start(out=out_ap, in_=in_ap), writes=[key], dma=slot)

        P.add("sp", lambda e: e.dma_start(out=cst[:], in_=cst_d), writes=[("cst",)], dma="cst")
        P.add("sp", lambda e: e.dma_start(out=vec[:], in_=vecs_d), writes=[("vec",)], dma="vec")
        P.add("dve", lambda e: e.tensor_copy(out=ident[:], in_=cst[:, 0:128]), reads=[("cst",)], writes=[("ident",)])
        P.add("dve", lambda e: e.tensor_copy(out=maskb[:], in_=cst[:, 128:256]), reads=[("cst",)], writes=[("maskb",)])
        P.add("pool", lambda e: e.memset(ones[:], 1.0), writes=[("ones",)])
        P.add("pool", lambda e: e.memset(sel[:], 0.0), writes=[("sel",)])
        P.add("pool", lambda e: e.memset(sel[64:65, :], 1.0), writes=[("sel",)])
        P.add("pool", lambda e: e.memset(onef[:], 1.0), writes=[("onef",)])
        P.add("pool", lambda e: e.memset(bd[:], 0.0), writes=[("bd",)])
        P.add("pool", lambda e: e.memset(bd[0:64, 0:64], 1.0), writes=[("bd",)])
        P.add("pool", lambda e: e.memset(bd[64:128, 64:128], 1.0), writes=[("bd",)])
        P.add("dve", lambda e: e.tensor_scalar(out=gk8[:], in0=gk, scalar1=0.125, scalar2=None, op0=ALU.mult),
              reads=[("vec",)], writes=[("gk8",)])
        P.add("dve", lambda e: e.tensor_scalar(out=nfb[:], in0=fb, scalar1=-1.0, scalar2=None, op0=ALU.mult),
              reads=[("vec",)], writes=[("nfb",)])

        def norm_a(xc_ap, xckeys, sq_ap, sqkey):
            P.add("act", lambda e: e.activation(out=sq_ap, in_=xc_ap, func=AF.Square),
                  reads=xckeys, writes=[sqkey])

        def norm_b(l, c, xc_ap, xckeys, sq_ap, sqkey, gvec, sd_ap, sdkey):
            b = bank()
            gemm(PS[b][:, :], b, [(ones[:, :], sq_ap[:, k, :]) for k in range(KC)], [sqkey, ("ones",)])
            P.add("act", lambda e: e.activation(out=sd_ap, in_=PS[b][:, :], func=AF.Ln, scale=1.0 / D, bias=EPS),
                  reads=[("ps", b)], writes=[sdkey])
            P.add("act", lambda e: e.activation(out=sd_ap, in_=sd_ap, func=AF.Exp, scale=-0.5),
                  reads=[sdkey], writes=[sdkey])
            for k in range(KC):
                P.add("dve", (lambda k: lambda e: e.scalar_tensor_tensor(
                    out=hT[:, k, c * CH:(c + 1) * CH], in0=xc_ap[:, k, :], scalar=gvec[:, l, k:k + 1],
                    in1=sd_ap, op0=ALU.mult, op1=ALU.mult))(k),
                    reads=[xckeys[k], sdkey, ("vec",)], writes=[("hT", c, k)])

        def hk(c):
            return [("hT", c, k) for k in range(KC)]

        def chk(name):
            if stop == name:
                raise _Stop()

        def layer(l):
            P.epoch = l
            src = xT if l == 0 else xs
            srckey = "xT" if l == 0 else "xs"
            last = (l == depth - 1)
            dst = yT if last else xs
            dstkey = "yT" if last else "xs"

            xc = [V("xc%d" % i, OFF_AT + i * 16384, 16384, dt=F32, pat="p (k n) -> p k n", n=CH) for i in range(2)]
            sqx = [V("sqx%d" % i, OFF_AT + 32768 + i * 8192, 8192, pat="p (k n) -> p k n", n=CH) for i in range(2)]
            sdt = [V("sdt%d" % i, OFF_AT + 49152 + i * 2048, 2048, dt=F32) for i in range(2)]
            rst = [V("rst%d" % i, OFF_AT + 53248 + i * 2048, 2048, dt=F32) for i in range(2)]
            P.recycle(["xc0", "xc1", "sqx0", "sqx1", "sdt0", "sdt1", "rst0", "rst1"])
            for c in range(NCH):
                i = c % 2
                P.add("sp", (lambda c, i: lambda e: e.dma_start(
                    out=xc[i], in_=src[:, c * CH:(c + 1) * CH].rearrange("(k p) n -> p k n", p=128)))(c, i),
                    reads=[(srckey, c)], writes=[("xc%d" % i, o) for o in range(KC)], dma="xc%d" % i)
                xk = [("xc%d" % i, o) for o in range(KC)]
                norm_a(xc[i], xk, sqx[i], ("sqx%d" % i,))
                norm_b(l, c, xc[i], xk, sqx[i], ("sqx%d" % i,), g1, sdt[i], ("sdt%d" % i,))
            if l == 0 and "d_hT" in dbg_out:
                P.add("sp", lambda e: e.dma_start(out=dbg_out["d_hT"].rearrange("(k p) n -> p k n", p=128), in_=hT),
                      reads=[k_ for c in range(NCH) for k_ in hk(c)], writes=[("d_hT",)], dma="dbg")

            chk('P2a')
            fE = V("fE", OFF_C, 16384, p0=0, p1=8, dt=F32)
            fR = V("fR", OFF_C + 16384, 16384, p0=0, p1=8, dt=F32)
            cp4 = V("cp4", OFF_AT, 32768, p0=0, p1=8, pat="p (j n) -> p j n", n=S)
            wf = V("wf", OFF_E, 128, pat="p (k n) -> p k n", n=8)
            wv = V("wv", OFF_E + 128, 8192, pat="p (k n) -> p k n", n=512)
            P.recycle(["fE", "fR", "cp4", "wf", "wv"])
            wdma(wf, w_in[l, :, OF:OF + 8].rearrange("(k p) n -> p k n", p=128), ("wf",), "wf")
            wdma(wv, w_in[l, :, OV:OV + 512].rearrange("(k p) n -> p k n", p=128), ("wv",), "wv")
            for c in range(NCH):
                b = bank()
                gemm(PS[b][0:8, :], b, [(wf[:, k, :], hT[:, k, c * CH:(c + 1) * CH]) for k in range(KC)],
                     [("wf",)] + hk(c))
                P.add("act", (lambda c, b: lambda e: e.activation(
                    out=fE[:, c * CH:(c + 1) * CH], in_=PS[b][0:8, :], func=AF.Exp, scale=-1.0, bias=nfb[:, l:l + 1]))(c, b),
                    reads=[("ps", b), ("nfb",)], writes=[("fE", c)])
            P.add("act", lambda e: e.activation(out=fE, in_=fE, func=AF.Ln, scale=1.0, bias=1.0),
                  reads=[("fE", c) for c in range(NCH)], writes=[("fE", "ln")])
            P.add("dve", lambda e: e.tensor_tensor_scan(out=fR, data0=onef[:, 0:1].to_broadcast([8, S]), data1=fE,
                                                        initial=0.0, op0=ALU.mult, op1=ALU.subtract),
                  reads=[("fE", "ln"), ("onef",)], writes=[("fR",)])
            if l == 0 and "d_c" in dbg_out:
                P.add("sp", lambda e: e.dma_start(out=dbg_out["d_c"], in_=fR), reads=[("fR",)], writes=[("d_c",)], dma="dbg")
            P.add("dve", lambda e: e.tensor_copy(out=cp4[:, 0, :], in_=fR), reads=[("fR",)], writes=[("cp4", 0)])
            P.add("dve", lambda e: e.tensor_tensor(out=fE, in0=fR, in1=cp4[:, 0, :], op=ALU.subtract),
                  reads=[("fR",), ("cp4", 0)], writes=[("fE", "res")])
            P.add("dve", lambda e: e.tensor_copy(out=cp4[:, 1, :], in_=fE), reads=[("fE", "res")], writes=[("cp4", 1)])
            P.add("dve", lambda e: e.tensor_scalar(out=cp4[:, 2:4, :], in0=cp4[:, 0:2, :], scalar1=-1.0, scalar2=None,
                                                   op0=ALU.mult),
                  reads=[("cp4", 0), ("cp4", 1)], writes=[("cp4", 2)])
            P.add("sp", lambda e: e.dma_start(out=cpd, in_=cp4), reads=[("cp4", 0), ("cp4", 1), ("cp4", 2)],
                  writes=[("cpd",)], dma="cpd")

            chk('P2b')
            P.recycle(["VA"])
            P.add("pool", lambda e: e.memset(VA[:, :, :, 64:65], 1.0), writes=[("VA", "ones")])
            for tt in range(32):
                b = bank()
                gemm(PS[b][:, :], b, [(hT[:, k, tt * 128:(tt + 1) * 128], wv[:, k, :]) for k in range(KC)],
                     [("wv",)] + hk(tt // 4))
                P.add("dve", (lambda tt, b: lambda e: e.tensor_copy(
                    out=VA[:, tt, :, 0:64], in_=PS[b][:, :].rearrange("p (h d) -> p h d", d=64)))(tt, b),
                    reads=[("ps", b)], writes=[("VA", tt)])
            if l == 0 and "d_v" in dbg_out:
                P.add("sp", lambda e: e.dma_start(out=dbg_out["d_v"], in_=VA.rearrange("p t h d -> p (t h d)")),
                      reads=[("VA", tt) for tt in range(32)] + [("VA", "ones")], writes=[("d_v",)], dma="dbg")

            chk('P2c')
            E0 = OFF_E + 8320
            PT = [V("PT%d" % i, E0 + i * 1024, 1024) for i in range(4)]
            wqk = [V("wqk%d" % i, E0 + 4096 + i * 4096, 4096, pat="p (a k n) -> p a k n", a=2, n=128) for i in range(2)]
            sqh = [V("sqh%d" % i, E0 + 12288 + i * 1024, 1024) for i in range(2)]
            lnh = [V("lnh%d" % i, E0 + 14336 + i * 2048, 2048, dt=F32) for i in range(2)]
            recf = V("recf", E0 + 18432, 2048, p0=0, p1=65, dt=F32)
            rdh = V("rdh", E0 + 20480, 1024, p0=0, p1=65)
            rdl = V("rdl", E0 + 21504, 1024, p0=0, p1=65)
            bcs = [V("bcs%d" % i, E0 + 22528 + i * 2048, 2048, p0=0, p1=64, dt=F32) for i in range(2)]
            atm = [V("atm%d" % i, E0 + 26624 + i * 1024, 1024, p0=0, p1=64) for i in range(2)]
            qtm = [V("qtm%d" % i, E0 + 26624 + i * 1024, 1024, p0=64, p1=128) for i in range(2)]
            assert E0 + 28672 <= AR_BYTES
            P.recycle(["PT%d" % i for i in range(4)] + ["wqk0", "wqk1", "sqh0", "sqh1", "lnh0", "lnh1",
                                                       "recf", "rdh", "rdl", "bcs0", "bcs1", "atm0", "atm1", "qtm0", "qtm1"])
            P.recycle(["QA0", "QA1", "KA0", "KA1", "aT"])
            P.add("pool", lambda e: e.memset(rdh[:, :], 0.0), writes=[("rdh",)])
            P.add("pool", lambda e: e.memset(rdl[:, :], 0.0), writes=[("rdl",)])
            if USE_PAIR:
                P.add("pool", lambda e: e.memset(QA[0][64:68, :], 1.0), writes=[("QA0", "aug")])
                P.add("pool", lambda e: e.memset(KA[0][64:68, :], 1.0), writes=[("KA0", "aug")])
                P.add("pool", lambda e: e.memset(QA[1][0:64, :], 0.0), writes=[("QA1", "aug")])
                P.add("pool", lambda e: e.memset(KA[1][0:64, :], 0.0), writes=[("KA1", "aug")])
                P.add("pool", lambda e: e.memset(QA[1][32:34, :], 1.0), writes=[("QA1", "aug")])
                P.add("pool", lambda e: e.memset(KA[1][0:2, :], 1.0), writes=[("KA1", "aug")])
            else:
                for i in range(2):
                    P.add("pool", (lambda i: lambda e: e.memset(QA[i][64:68, :], 1.0))(i), writes=[("QA%d" % i, "aug")])
                    P.add("pool", (lambda i: lambda e: e.memset(KA[i][64:68, :], 1.0))(i), writes=[("KA%d" % i, "aug")])

            ucnt = [0]

            def load_wqk(p):
                pb_ = p % 2
                wdma(wqk[pb_][:, 0], w_in[l, :, OQ + p * 128:OQ + (p + 1) * 128].rearrange("(k p) n -> p k n", p=128),
                     ("wqk%d" % pb_, 0), "wq%d" % pb_)
                wdma(wqk[pb_][:, 1], w_in[l, :, OK_ + p * 128:OK_ + (p + 1) * 128].rearrange("(k p) n -> p k n", p=128),
                     ("wqk%d" % pb_, 1), "wk%d" % pb_)

            def proj_pair(p):
                pb_ = p % 2
                for hh, (qr, kr) in enumerate(((64, 66), (0, 32))):
                    h = 2 * p + hh
                    P.add("sp", (lambda hh, h, qr: lambda e: e.dma_start(out=QA[hh][qr:qr + 2, :], in_=cpd[h, 0:2, :]))(hh, h, qr),
                          reads=[("cpd",)], writes=[("QA%d" % hh, "aug")], dma="qaug%d" % hh)
                    P.add("sp", (lambda hh, h, kr: lambda e: e.dma_start(out=KA[hh][kr:kr + 2, :], in_=cpd[h, 2:4, :]))(hh, h, kr),
                          reads=[("cpd",)], writes=[("KA%d" % hh, "aug")], dma="kaug%d" % hh)
                units = [(c, a) for c in range(NCH) for a in range(2)]
                st_ = {}

                def stage_a(ui):
                    c, a = units[ui]
                    u = ucnt[0] % 2
                    ucnt[0] += 1
                    b = bank()
                    st_[ui] = (u, b)
                    gemm(PS[b][:, :], b, [(wqk[pb_][:, a, k, :], hT[:, k, c * CH:(c + 1) * CH]) for k in range(KC)],
                         [("wqk%d" % pb_, a)] + hk(c))
                    P.add("act", lambda e: e.activation(out=sqh[u], in_=PS[b][:, :], func=AF.Square),
                          reads=[("ps", b)], writes=[("sqh%d" % u,)])

                def stage_b(ui):
                    c, a = units[ui]
                    u, b = st_[ui]
                    b2 = bank()
                    mm(PS[b2][:, :], ("ps", b2), bd[:, :], sqh[u], True, True, [("sqh%d" % u,), ("bd",)])
                    P.add("act", lambda e: e.activation(out=lnh[u], in_=PS[b2][:, :], func=AF.Ln, scale=1.0 / 64, bias=EPS),
                          reads=[("ps", b2)], writes=[("lnh%d" % u,)])
                    P.add("act", lambda e: e.activation(out=lnh[u], in_=lnh[u], func=AF.Exp, scale=-0.5),
                          reads=[("lnh%d" % u,)], writes=[("lnh%d" % u,)])
                    dt_ = (QA, KA)[a]
                    dn = ("QA", "KA")[a]
                    gv = (gq2, gk8)[a]
                    P.add("dve", lambda e: e.scalar_tensor_tensor(
                        out=dt_[0][0:64, c * CH:(c + 1) * CH], in0=PS[b][0:64, :], scalar=gv[0:64, l:l + 1],
                        in1=lnh[u][0:64, :], op0=ALU.mult, op1=ALU.mult),
                        reads=[("ps", b), ("lnh%d" % u,), ("vec",), ("gk8",)], writes=[(dn + "0", c)])
                    P.add("dve", lambda e: e.scalar_tensor_tensor(
                        out=dt_[1][64:128, c * CH:(c + 1) * CH], in0=PS[b][64:128, :], scalar=gv[64:128, l:l + 1],
                        in1=lnh[u][64:128, :], op0=ALU.mult, op1=ALU.mult),
                        reads=[("ps", b), ("lnh%d" % u,), ("vec",), ("gk8",)], writes=[(dn + "1", c)])

                nU = len(units)
                stage_a(0)
                for ui in range(nU):
                    if ui + 1 < nU:
                        stage_a(ui + 1)
                    stage_b(ui)

            def proj_head(h, pb_):
                hh = h % 2
                units = [(c, a) for c in range(NCH) for a in range(2)]
                st_ = {}

                def stage_a(ui):
                    c, a = units[ui]
                    u = ucnt[0] % 2
                    ucnt[0] += 1
                    b = bank()
                    st_[ui] = (u, b)
                    gemm(PS[b][0:64, :], b, [(wqk[pb_][:, a, k, hh * 64:(hh + 1) * 64], hT[:, k, c * CH:(c + 1) * CH]) for k in range(KC)],
                         [("wqk%d" % pb_, a)] + hk(c))
                    P.add("act", lambda e: e.activation(out=sqh[u][0:64, :], in_=PS[b][0:64, :], func=AF.Square),
                          reads=[("ps", b)], writes=[("sqh%d" % u,)])

                def stage_b(ui):
                    c, a = units[ui]
                    u, b = st_[ui]
                    b2 = bank()
                    mm(PS[b2][0:64, :], ("ps", b2), ones[0:64, 0:64], sqh[u][0:64, :], True, True, [("sqh%d" % u,), ("ones",)])
                    P.add("act", lambda e: e.activation(out=lnh[u][0:64, :], in_=PS[b2][0:64, :], func=AF.Ln, scale=1.0 / 64, bias=EPS),
                          reads=[("ps", b2)], writes=[("lnh%d" % u,)])
                    P.add("act", lambda e: e.activation(out=lnh[u][0:64, :], in_=lnh[u][0:64, :], func=AF.Exp, scale=-0.5),
                          reads=[("lnh%d" % u,)], writes=[("lnh%d" % u,)])
                    dt_ = (QA, KA)[a]
                    dn = ("QA", "KA")[a]
                    gv = (gq2, gk8)[a]
                    P.add("dve", lambda e: e.scalar_tensor_tensor(
                        out=dt_[hh][0:64, c * CH:(c + 1) * CH], in0=PS[b][0:64, :], scalar=gv[0:64, l:l + 1],
                        in1=lnh[u][0:64, :], op0=ALU.mult, op1=ALU.mult),
                        reads=[("ps", b), ("lnh%d" % u,), ("vec",), ("gk8",)], writes=[(dn + "%d" % hh, c)])

                nU = len(units)
                stage_a(0)
                for ui in range(nU):
                    if ui + 1 < nU:
                        stage_a(ui + 1)
                    stage_b(ui)

            def aug_rows(p):
                for hh in range(2):
                    h = 2 * p + hh
                    P.add("sp", (lambda hh, h: lambda e: e.dma_start(out=QA[hh][64:66, :], in_=cpd[h, 0:2, :]))(hh, h),
                          reads=[("cpd",)], writes=[("QA%d" % hh, "aug")], dma="qaug%d" % hh)
                    P.add("sp", (lambda hh, h: lambda e: e.dma_start(out=KA[hh][66:68, :], in_=cpd[h, 2:4, :]))(hh, h),
                          reads=[("cpd",)], writes=[("KA%d" % hh, "aug")], dma="kaug%d" % hh)

            ocnt = [0]

            def attn_head(h):
                hb = h % 2
                qa, ka = QA[hb], KA[hb]
                qn, kn = "QA%d" % hb, "KA%d" % hb
                tiles = []
                for c in range(NCH):
                    for j in range(4 * c + 4):
                        tiles.append((c, j))
                n = len(tiles)
                state = {}
                pending = []

                def emit_S(i):
                    c, j = tiles[i]
                    n0 = max(0, j - 4 * c) * 128
                    diag = j >= 4 * c
                    sbk = i % 4
                    rd = [(qn, c), (kn, j // 4), (qn, "aug"), (kn, "aug")]
                    kk = 128 if (USE_PAIR and hb == 1) else 68
                    mm(PS[sbk][:, n0:CH], ("ps", sbk), ka[0:kk, j * 128:(j + 1) * 128],
                       qa[0:kk, c * CH + n0:(c + 1) * CH], True, not diag, rd)
                    if diag:
                        mm(PS[sbk][:, n0:n0 + 128], ("ps", sbk), ident[:, :], maskb[:, :], False, True,
                           [("ident",), ("maskb",)])
                    P.add("act", lambda e: e.activation(out=PT[sbk][:, n0:CH], in_=PS[sbk][:, n0:CH], func=AF.Exp),
                          reads=[("ps", sbk)], writes=[("PT%d" % sbk,)])

                def emit_PV(i):
                    c, j = tiles[i]
                    n0 = max(0, j - 4 * c) * 128
                    sbk = i % 4
                    lastj = 4 * c + 3
                    if j == 0:
                        state["ob"] = 4 + (ocnt[0] % 2)
                        ocnt[0] += 1
                    ob = state["ob"]
                    mm(PS[ob][0:65, n0:CH], ("ps", ob), VA[:, j, h, 0:65], PT[sbk][:, n0:CH], j == 0, j == lastj,
                       [("VA", j), ("VA", "ones"), ("PT%d" % sbk,)])
                    if j == lastj:
                        u = c % 2
                        P.add("dve", lambda e: e.reciprocal(out=recf[64:65, :], in_=PS[ob][64:65, :]),
                              reads=[("ps", ob)], writes=[("recf",)])
                        P.add("dve", lambda e: e.tensor_copy(out=rdh[64:65, :], in_=recf[64:65, :]),
                              reads=[("recf",)], writes=[("rdh",)])
                        P.add("dve", lambda e: e.tensor_tensor(out=rdl[64:65, :], in0=recf[64:65, :], in1=rdh[64:65, :],
                                                               op=ALU.subtract),
                              reads=[("recf",), ("rdh",)], writes=[("rdl",)])

                        def part2():
                            mm(PS[6][0:64, :], ("ps", 6), sel[:, :], rdh[:, :], True, False, [("sel",), ("rdh",)])
                            mm(PS[6][0:64, :], ("ps", 6), sel[:, :], rdl[:, :], False, True, [("sel",), ("rdl",)])
                            P.add("act", lambda e: e.activation(out=bcs[u], in_=PS[6][0:64, :], func=AF.Copy),
                                  reads=[("ps", 6)], writes=[("bcs%d" % u,)])
                            if h % 2 == 0:
                                P.add("dve", lambda e: e.tensor_tensor(out=aT[0:64, h // 2, c * CH:(c + 1) * CH],
                                                                       in0=PS[ob][0:64, :], in1=bcs[u], op=ALU.mult),
                                      reads=[("ps", ob), ("bcs%d" % u,)], writes=[("aT", h // 2, c, 0)])
                            else:
                                P.add("dve", lambda e: e.tensor_tensor(out=atm[u], in0=PS[ob][0:64, :], in1=bcs[u], op=ALU.mult),
                                      reads=[("ps", ob), ("bcs%d" % u,)], writes=[("atm%d" % u,)])
                                P.add("sp", lambda e: e.dma_start(out=aT[64:128, h // 2, c * CH:(c + 1) * CH], in_=atm[u]),
                                      reads=[("atm%d" % u,)], writes=[("aT", h // 2, c, 1)], dma="atm%d" % u)
                        pending.append((i + 3, part2))

                LA = 3
                for i in range(min(LA, n)):
                    emit_S(i)
                for i in range(n):
                    if i + LA < n:
                        emit_S(i + LA)
                    emit_PV(i)
                    while pending and pending[0][0] <= i:
                        pending.pop(0)[1]()
                while pending:
                    pending.pop(0)[1]()

            pscnt[0] = 0
            load_wqk(0)
            for p in range(4):
                if p + 1 < 4:
                    load_wqk(p + 1)
                if USE_PAIR:
                    proj_pair(p)
                else:
                    aug_rows(p)
                    proj_head(2 * p, p % 2)
                    proj_head(2 * p + 1, p % 2)
                attn_head(2 * p)
                attn_head(2 * p + 1)
            if l == 0 and "d_aT" in dbg_out:
                P.add("sp", lambda e: e.dma_start(out=dbg_out["d_aT"].rearrange("(k p) n -> p k n", p=128), in_=aT),
                      reads=[("aT", k, c, q) for k in range(4) for c in range(NCH) for q in range(2)],
                      writes=[("d_aT",)], dma="dbg")

            chk('P2d')
            NPB = 16448
            ub = V("ub", OFF_D, NPB, dt=F32)
            pa = V("pa", OFF_D + NPB, NPB, dt=F32)
            pb = V("pb", OFF_D + 2 * NPB, NPB, dt=F32)
            wpx = [V("wpx%d" % i, OFF_D + 3 * NPB + i * 2048, 2048, pat="p (k n) -> p k n", n=128) for i in range(2)]
            dT = V("dT", OFF_C + 16384, 16384, pat="p (k n) -> p k n", n=S)
            cbz = V("cbz", OFF_C, 16384, pat="p (k n) -> p k n", n=S)
            assert OFF_D + 3 * NPB + 4096 <= AR_BYTES
            P.recycle(["ub", "pa", "pb", "wpx0", "wpx1", "dT", "cbz"])
            for t_, nm in ((ub, "ub"), (pa, "pa"), (pb, "pb")):
                P.add("pool", (lambda t_: lambda e: e.memset(t_[:, 0:16], 0.0))(t_), writes=[(nm, "halo")])
            H = 16
            for blk in range(2):
                wdma(wpx[blk], w_in[l, :, OPX + blk * 128:OPX + (blk + 1) * 128].rearrange("(k p) n -> p k n", p=128),
                     ("wpx%d" % blk,), "wpx%d" % blk)
            for blk in range(2):
                for c in range(NCH):
                    b = bank()
                    gemm(PS[b][:, :], b, [(wpx[blk][:, k, :], hT[:, k, c * CH:(c + 1) * CH]) for k in range(KC)],
                         [("wpx%d" % blk,)] + hk(c))
                    P.add("act", (lambda b, c: lambda e: e.activation(out=ub[:, H + c * CH:H + (c + 1) * CH], in_=PS[b][:, :],
                                                                       func=AF.Copy))(b, c),
                          reads=[("ps", b)], writes=[("ub", c)])
                allu = [("ub", c) for c in range(NCH)] + [("ub", "halo")]
                P.add("dve", lambda e: e.tensor_tensor(out=pa[:, H:H + S], in0=ub[:, H:H + S], in1=ub[:, H - 1:H - 1 + S], op=ALU.add),
                      reads=allu + [("pa", "halo")], writes=[("pa", "w")])
                if blk == 0:
                    P.add("dve", lambda e: e.tensor_tensor(out=pb[64:128, H:H + S], in0=pa[64:128, H:H + S],
                                                           in1=pa[64:128, H - 2:H - 2 + S], op=ALU.add),
                          reads=[("pa", "w"), ("pa", "halo"), ("pb", "halo")], writes=[("pb", "w")])
                    resA, resB = pa, pb
                    rk = [("pa", "w"), ("pb", "w")]
                else:
                    P.add("dve", lambda e: e.tensor_tensor(out=pb[:, H:H + S], in0=pa[:, H:H + S], in1=pa[:, H - 2:H - 2 + S], op=ALU.add),
                          reads=[("pa", "w"), ("pa", "halo"), ("pb", "halo")], writes=[("pb", "w")])
                    P.add("dve", lambda e: e.tensor_tensor(out=pa[:, H:H + S], in0=pb[:, H:H + S], in1=pb[:, H - 4:H - 4 + S], op=ALU.add),
                          reads=[("pb", "w"), ("pb", "halo"), ("pa", "halo")], writes=[("pa", "w")])
                    P.add("dve", lambda e: e.tensor_tensor(out=pb[64:128, H:H + S], in0=pa[64:128, H:H + S],
                                                           in1=pa[64:128, H - 8:H - 8 + S], op=ALU.add),
                          reads=[("pa", "w"), ("pa", "halo"), ("pb", "w")], writes=[("pb", "w2")])
                    resA, resB = pa, pb
                    rk = [("pa", "w"), ("pb", "w2"), ("pb", "w")]
                for (r_, p0, p1) in ((resA, 0, 64), (resB, 64, 128)):
                    P.add("dve", (lambda r_, p0, p1, blk: lambda e: e.scalar_tensor_tensor(
                        out=dT[p0:p1, blk, :], in0=r_[p0:p1, H:H + S], scalar=invw[p0:p1, blk:blk + 1],
                        in1=ub[p0:p1, H:H + S], op0=ALU.mult, op1=ALU.subtract))(r_, p0, p1, blk),
                        reads=rk + allu + [("cst",)], writes=[("dT", blk, p0)])
                    P.add("dve", (lambda r_, p0, p1, blk: lambda e: e.tensor_tensor(
                        out=r_[p0:p1, H:H + 16], in0=r_[p0:p1, H:H + 16], in1=invtab[blk][p0:p1, :], op=ALU.mult))(r_, p0, p1, blk),
                        reads=rk + [("dT", blk, p0), ("cst",)], writes=[("ptmp", blk, p0)])
                    P.add("dve", (lambda r_, p0, p1, blk: lambda e: e.tensor_tensor(
                        out=dT[p0:p1, blk, 0:16], in0=r_[p0:p1, H:H + 16], in1=ub[p0:p1, H:H + 16], op=ALU.subtract))(r_, p0, p1, blk),
                        reads=[("ptmp", blk, p0)] + allu, writes=[("dT", blk, p0)])
                if blk == 0:
                    for nm in ("pa", "pb"):
                        pass
            if l == 0 and "d_dT" in dbg_out:
                P.add("sp", lambda e: e.dma_start(out=dbg_out["d_dT"].rearrange("(k p) n -> p k n", p=128), in_=dT),
                      reads=[("dT", b_, p_) for b_ in range(2) for p_ in (0, 64)], writes=[("d_dT",)], dma="dbg")

            chk('P2e')
            NZ = 16400
            zb = V("zb", OFF_D, NZ, dt=F32)
            cxs = [V("cxs%d" % i, OFF_D + NZ + i * 2048, 2048, dt=F32) for i in range(2)]
            acc = [V("acc%d" % i, OFF_D + NZ + 4096 + i * 2048, 2048, dt=F32) for i in range(2)]
            wc3 = [V("wc3%d" % i, OFF_D + NZ + 8192 + i * 6144, 6144, pat="p (k a n) -> p k a n", a=3, n=128) for i in range(2)]
            P.recycle(["zb", "cxs0", "cxs1", "acc0", "acc1", "wc30", "wc31"])
            P.add("pool", lambda e: e.memset(zb[:, 0:2], 0.0), writes=[("zb", "halo")])
            for blk in range(2):
                for a, off in enumerate((OCX, OCB, OCC)):
                    wdma(wc3[blk][:, :, a, :], w_in[l, :, off + blk * 128:off + (blk + 1) * 128].rearrange("(k p) n -> p k n", p=128),
                         ("wc3%d" % blk, a), "wc3%d_%d" % (blk, a))
            for blk in range(2):
                for c in range(NCH):
                    u = c % 2
                    bx, bb, bc_ = bank(), bank(), bank()
                    for a, b in ((0, bx), (1, bb), (2, bc_)):
                        gemm(PS[b][:, :], b, [(wc3[blk][:, k, a, :], hT[:, k, c * CH:(c + 1) * CH]) for k in range(KC)],
                             [("wc3%d" % blk, a)] + hk(c))
                    P.add("act", (lambda bx, u: lambda e: e.activation(out=cxs[u], in_=PS[bx][:, :], func=AF.Copy))(bx, u),
                          reads=[("ps", bx)], writes=[("cxs%d" % u,)])
                    P.add("dve", (lambda bc_, u, c: lambda e: e.tensor_tensor(out=zb[:, 2 + c * CH:2 + (c + 1) * CH], in0=PS[bc_][:, :],
                                                                              in1=cxs[u], op=ALU.mult))(bc_, u, c),
                          reads=[("ps", bc_), ("cxs%d" % u,)], writes=[("zb", c)])
                    zr = [("zb", c), ("zb", "halo")] + ([("zb", c - 1)] if c > 0 else [])
                    P.add("dve", (lambda u, c, blk: lambda e: e.tensor_scalar(out=acc[u], in0=zb[:, c * CH:(c + 1) * CH],
                                                                              scalar1=cw[:, l, 0, blk:blk + 1], scalar2=None, op0=ALU.mult))(u, c, blk),
                          reads=zr + [("vec",)], writes=[("acc%d" % u,)])
                    for jj in (1, 2):
                        P.add("dve", (lambda u, c, blk, jj: lambda e: e.scalar_tensor_tensor(
                            out=acc[u], in0=zb[:, jj + c * CH:jj + (c + 1) * CH], scalar=cw[:, l, jj, blk:blk + 1],
                            in1=acc[u], op0=ALU.mult, op1=ALU.add))(u, c, blk, jj),
                            reads=zr + [("acc%d" % u,), ("vec",)], writes=[("acc%d" % u,)])
                    P.add("dve", (lambda u, c, blk, bb: lambda e: e.tensor_tensor(out=cbz[:, blk, c * CH:(c + 1) * CH], in0=acc[u],
                                                                                  in1=PS[bb][:, :], op=ALU.mult))(u, c, blk, bb),
                          reads=[("acc%d" % u,), ("ps", bb)], writes=[("cbz", blk, c)])
            if l == 0 and "d_cbz" in dbg_out:
                P.add("sp", lambda e: e.dma_start(out=dbg_out["d_cbz"].rearrange("(k p) n -> p k n", p=128), in_=cbz),
                      reads=[("cbz", b_, c_) for b_ in range(2) for c_ in range(NCH)], writes=[("d_cbz",)], dma="dbg")

            chk('P3a')
            WS = 7936
            wg = [V("wg%d" % i, OFF_D + i * WS, 6144, pat="p (k a n) -> p k a n", a=3, n=128) for i in range(2)]
            wao = [V("wao%d" % i, OFF_D + i * WS + 6144, 1024, pat="p (k n) -> p k n", n=128) for i in range(2)]
            wco = [V("wco%d" % i, OFF_D + i * WS + 7168, 512, pat="p (k n) -> p k n", n=128) for i in range(2)]
            wpo = [V("wpo%d" % i, OFF_D + i * WS + 7680, 256) for i in range(2)]
            T0 = OFF_D + 2 * WS
            sg = [V("sg%d" % i, T0 + i * 2048, 2048, dt=F32) for i in range(2)]
            mt = [V("mt%d" % i, T0 + 4096 + i * 2048, 2048, dt=F32) for i in range(2)]
            tt_ = [V("tt%d" % i, T0 + 8192 + i * 2048, 2048, dt=F32) for i in range(2)]
            mb = [V("mb%d" % i, T0 + 12288 + i * 1024, 1024) for i in range(2)]
            wo = V("wo", T0 + 14336, 16384, pat="p (k n) -> p k n", n=D)
            sd2 = [V("sd2%d" % i, T0 + 30720 + i * 2048, 2048, dt=F32) for i in range(2)]
            rs2 = [V("rs2%d" % i, T0 + 34816 + i * 2048, 2048, dt=F32) for i in range(2)]
            assert T0 + 38912 <= AR_BYTES
            P.recycle(["wg0", "wg1", "wao0", "wao1", "wco0", "wco1", "wpo0", "wpo1", "sg0", "sg1", "mt0", "mt1",
                       "tt0", "tt1", "mb0", "mb1", "wo", "sd20", "sd21", "rs20", "rs21"])

            def load_wset(j):
                i = j % 2
                for a in range(3):
                    wdma(wg[i][:, :, a, :],
                         w_in[l, :, OG + a * D + j * 128:OG + a * D + (j + 1) * 128].rearrange("(k p) n -> p k n", p=128),
                         ("wg%d" % i, a), "wg%d_%d" % (i, a))
                wdma(wao[i], w_ao[l, :, j * 128:(j + 1) * 128].rearrange("(k p) n -> p k n", p=128), ("wao%d" % i,), "wao%d" % i)
                wdma(wco[i], w_co[l, :, j * 128:(j + 1) * 128].rearrange("(k p) n -> p k n", p=128), ("wco%d" % i,), "wco%d" % i)
                g = j // 2
                r0 = (g % 2) * 64
                wdma(wpo[i][r0:r0 + 64, :], w_po[l, g, :, (j % 2) * 128:(j % 2 + 1) * 128], ("wpo%d" % i,), "wpo%d" % i)

            load_wset(0)
            ucnt2 = [0]
            for j in range(8):
                i = j % 2
                if j + 1 < 8:
                    load_wset(j + 1)
                if j == 1:
                    wdma(wo, w_o[l].rearrange("(k p) n -> p k n", p=128), ("wo",), "wo")
                g = j // 2
                r0 = (g % 2) * 64
                for c in range(NCH):
                    u = ucnt2[0] % 2
                    ucnt2[0] += 1
                    cs = slice(c * CH, (c + 1) * CH)
                    bg, by = bank(), bank()
                    gemm(PS[bg][:, :], bg, [(wg[i][:, k, 0, :], hT[:, k, cs]) for k in range(KC)], [("wg%d" % i, 0)] + hk(c))
                    gemm(PS[by][:, :], by, [(wao[i][:, k, :], aT[:, k, cs]) for k in range(4)],
                         [("wao%d" % i,)] + [("aT", k, c, q) for k in range(4) for q in range(2)])
                    P.add("act", (lambda bg, u: lambda e: e.activation(out=sg[u], in_=PS[bg][:, :], func=AF.Sigmoid))(bg, u),
                          reads=[("ps", bg)], writes=[("sg%d" % u,)])
                    P.add("dve", (lambda by, u: lambda e: e.tensor_tensor(out=mt[u], in0=sg[u], in1=PS[by][:, :], op=ALU.mult))(by, u),
                          reads=[("ps", by), ("sg%d" % u,)], writes=[("mt%d" % u,)])
                    bg, by = bank(), bank()
                    gemm(PS[bg][:, :], bg, [(wg[i][:, k, 1, :], hT[:, k, cs]) for k in range(KC)], [("wg%d" % i, 1)] + hk(c))
                    gemm(PS[by][:, :], by, [(wco[i][:, k, :], cbz[:, k, cs]) for k in range(2)],
                         [("wco%d" % i,), ("cbz", 0, c), ("cbz", 1, c)])
                    P.add("act", (lambda bg, u: lambda e: e.activation(out=sg[u], in_=PS[bg][:, :], func=AF.Sigmoid))(bg, u),
                          reads=[("ps", bg)], writes=[("sg%d" % u,)])
                    P.add("dve", (lambda by, u: lambda e: e.tensor_tensor(out=tt_[u], in0=sg[u], in1=PS[by][:, :], op=ALU.mult))(by, u),
                          reads=[("ps", by), ("sg%d" % u,)], writes=[("tt%d" % u,)])
                    P.add("pool", (lambda u: lambda e: e.tensor_tensor(out=mt[u], in0=mt[u], in1=tt_[u], op=ALU.add))(u),
                          reads=[("mt%d" % u,), ("tt%d" % u,)], writes=[("mt%d" % u,)])
                    bg, by = bank(), bank()
                    gemm(PS[bg][:, :], bg, [(wg[i][:, k, 2, :], hT[:, k, cs]) for k in range(KC)], [("wg%d" % i, 2)] + hk(c))
                    mm(PS[by][:, :], ("ps", by), wpo[i][r0:r0 + 64, :], dT[r0:r0 + 64, g // 2, cs], True, True,
                       [("wpo%d" % i,), ("dT", g // 2, r0)])
                    P.add("act", (lambda bg, u: lambda e: e.activation(out=sg[u], in_=PS[bg][:, :], func=AF.Sigmoid))(bg, u),
                          reads=[("ps", bg)], writes=[("sg%d" % u,)])
                    P.add("dve", (lambda by, u, j: lambda e: e.scalar_tensor_tensor(out=tt_[u], in0=PS[by][:, :], scalar=psc[:, l, j:j + 1],
                                                                                    in1=sg[u], op0=ALU.mult, op1=ALU.mult))(by, u, j),
                          reads=[("ps", by), ("sg%d" % u,), ("vec",)], writes=[("tt%d" % u,)])
                    P.add("pool", (lambda u: lambda e: e.tensor_tensor(out=mb[u], in0=mt[u], in1=tt_[u], op=ALU.add))(u),
                          reads=[("mt%d" % u,), ("tt%d" % u,)], writes=[("mb%d" % u,)])
                    P.add("sp", (lambda u, j, c: lambda e: e.dma_start(out=mT[j * 128:(j + 1) * 128, c * CH:(c + 1) * CH], in_=mb[u]))(u, j, c),
                          reads=[("mb%d" % u,)], writes=[("mT", j, c)], dma="mb%d" % u)

            chk('P3b')
            mc = [V("mc%d" % i, OFF_AT + 49152 + i * 8192, 8192, pat="p (k n) -> p k n", n=CH) for i in range(2)]
            assert OFF_AT + 49152 + 16384 <= OFF_D
            P.recycle(["xc0", "xc1", "sqx0", "sqx1", "mc0", "mc1"])
            def p3b_tail(c):
                i = c % 2
                xk = [("xc%d" % i, o) for o in range(KC)]
                norm_b(l, c, xc[i], xk, sqx[i], ("sqx%d" % i,), g2, sd2[i], ("sd2%d" % i,))

            for c in range(NCH):
                i = c % 2
                xk = [("xc%d" % i, o) for o in range(KC)]
                P.add("sp", (lambda c, i: lambda e: e.dma_start(
                    out=xc[i], in_=src[:, c * CH:(c + 1) * CH].rearrange("(k p) n -> p k n", p=128)))(c, i),
                    reads=[(srckey, c)], writes=xk, dma="xc%d" % i)
                P.add("sp", (lambda c, i: lambda e: e.dma_start(
                    out=mc[i], in_=mT[:, c * CH:(c + 1) * CH].rearrange("(k p) n -> p k n", p=128)))(c, i),
                    reads=[("mT", j, c) for j in range(8)], writes=[("mc%d" % i,)], dma="mc%d" % i)
                for o in range(8):
                    b = bank()
                    gemm(PS[b][:, :], b, [(wo[:, k, o * 128:(o + 1) * 128], mc[i][:, k, :]) for k in range(KC)],
                         [("wo",), ("mc%d" % i,)])
                    P.add("dve", (lambda b, i, o: lambda e: e.tensor_tensor(out=xc[i][:, o, :], in0=xc[i][:, o, :], in1=PS[b][:, :],
                                                                            op=ALU.add))(b, i, o),
                          reads=[("ps", b), ("xc%d" % i, o)], writes=[("xc%d" % i, o)])
                    if o == 1 and c >= 1:
                        p3b_tail(c - 1)
                P.add("sp", (lambda c, i: lambda e: e.dma_start(
                    out=xs[:, c * CH:(c + 1) * CH].rearrange("(k p) n -> p k n", p=128), in_=xc[i]))(c, i),
                    reads=xk, writes=[("xs", c)], dma="xst%d" % i)
                norm_a(xc[i], xk, sqx[i], ("sqx%d" % i,))
            p3b_tail(NCH - 1)

            chk('P4')
            HS = S // 2
            uT = V("uT", OFF_AT, NHB * HS * 2, pat="p (k n) -> p k n", n=HS)
            F0 = OFF_AT + NHB * HS * 2
            wgu = [V("wgu%d" % i, F0 + i * 4096, 4096, pat="p (k a n) -> p k a n", a=2, n=128) for i in range(2)]
            wfo = [V("wfo%d" % i, F0 + 8192 + i * 5632, 5632, pat="p (k n) -> p k n", n=128) for i in range(2)]
            xo = [V("xo%d" % i, F0 + 19456 + i * 8192, 8192, dt=F32) for i in range(2)]
            sgl = [V("sgl%d" % i, F0 + 35840 + i * 2048, 2048, dt=F32) for i in range(2)]
            assert F0 + 39936 <= AR_BYTES
            P.recycle(["uT", "wgu0", "wgu1", "wfo0", "wfo1", "xo0", "xo1", "sgl0", "sgl1"])
            wcnt = [0]
            ocn = [0]
            for half in range(2):
                for hbk in range(NHB):
                    i = wcnt[0] % 2
                    wcnt[0] += 1
                    wdma(wgu[i][:, :, 0, :], w_fi[l, :, hbk * 128:(hbk + 1) * 128].rearrange("(k p) n -> p k n", p=128),
                         ("wgu%d" % i, 0), "wgu%d_0" % i)
                    wdma(wgu[i][:, :, 1, :], w_fi[l, :, D_FF + hbk * 128:D_FF + (hbk + 1) * 128].rearrange("(k p) n -> p k n", p=128),
                         ("wgu%d" % i, 1), "wgu%d_1" % i)
                    for cc in range(4):
                        c = half * 4 + cc
                        cs = slice(c * CH, (c + 1) * CH)
                        u = (hbk * 4 + cc) % 2
                        bg, bu = bank(), bank()
                        gemm(PS[bg][:, :], bg, [(wgu[i][:, k, 0, :], hT[:, k, cs]) for k in range(KC)], [("wgu%d" % i, 0)] + hk(c))
                        gemm(PS[bu][:, :], bu, [(wgu[i][:, k, 1, :], hT[:, k, cs]) for k in range(KC)], [("wgu%d" % i, 1)] + hk(c))
                        P.add("act", (lambda bg, u: lambda e: e.activation(out=sgl[u], in_=PS[bg][:, :], func=AF.Sigmoid))(bg, u),
                              reads=[("ps", bg)], writes=[("sgl%d" % u,)])
                        P.add("dve", (lambda bg, u: lambda e: e.tensor_tensor(out=sgl[u], in0=sgl[u], in1=PS[bg][:, :], op=ALU.mult))(bg, u),
                              reads=[("ps", bg), ("sgl%d" % u,)], writes=[("sgl%d" % u,)])
                        P.add("dve", (lambda bu, u, hbk, cc: lambda e: e.tensor_tensor(out=uT[:, hbk, cc * CH:(cc + 1) * CH], in0=sgl[u],
                                                                                       in1=PS[bu][:, :], op=ALU.mult))(bu, u, hbk, cc),
                              reads=[("ps", bu), ("sgl%d" % u,)], writes=[("uT", hbk, cc)])
                for o in range(8):
                    i = ocn[0] % 2
                    ocn[0] += 1
                    wdma(wfo[i], w_fo[l, :, o * 128:(o + 1) * 128].rearrange("(k p) n -> p k n", p=128), ("wfo%d" % i,), "wfo%d" % i)
                    P.add("sp", (lambda o, i, half: lambda e: e.dma_start(
                        out=xo[i], in_=xs[o * 128:(o + 1) * 128, half * HS:(half + 1) * HS]))(o, i, half),
                        reads=[("xs", half * 4 + cc) for cc in range(4)], writes=[("xo%d" % i, cc) for cc in range(4)], dma="xo%d" % i)
                    for cc in range(4):
                        b = bank()
                        gemm(PS[b][:, :], b, [(wfo[i][:, k, :], uT[:, k, cc * CH:(cc + 1) * CH]) for k in range(NHB)],
                             [("wfo%d" % i,)] + [("uT", k, cc) for k in range(NHB)])
                        P.add("dve", (lambda b, i, cc: lambda e: e.tensor_tensor(out=xo[i][:, cc * CH:(cc + 1) * CH],
                                                                                 in0=xo[i][:, cc * CH:(cc + 1) * CH], in1=PS[b][:, :],
                                                                                 op=ALU.add))(b, i, cc),
                              reads=[("ps", b), ("xo%d" % i, cc)], writes=[("xo%d" % i, cc)])
                    P.add("sp", (lambda o, i, half: lambda e: e.dma_start(
                        out=dst[o * 128:(o + 1) * 128, half * HS:(half + 1) * HS], in_=xo[i]))(o, i, half),
                        reads=[("xo%d" % i, cc) for cc in range(4)], writes=[(dstkey + "_o", o, half)], dma="xot%d" % i)
                P.marker(reads=[(dstkey + "_o", o, half) for o in range(8)],
                         writes=[(dstkey, half * 4 + cc) for cc in range(4)])
        try:
            for l_ in range(depth):
                layer(l_)
        except _Stop:
            pass
        P.add("sp", None, reads=[("yT_o", o, half) for o in range(8) for half in range(2)] + [(k,) for k in ("d_hT", "d_aT", "d_cbz", "d_dT", "d_c", "d_v")])
        P.emit(nc, st)
    return nc, P


def _consts():
    c = np.zeros((128, NCST), np.float32)
    c[:, 0:128] = np.eye(128, dtype=np.float32)
    s = np.arange(128)[:, None]
    t = np.arange(128)[None, :]
    c[:, 128:256] = np.where(s <= t, 0.0, -30000.0)
    tt = np.arange(16, dtype=np.float32)
    for blk, (wa, wb) in enumerate(((2, 4), (8, 16))):
        c[0:64, 256 + blk * 16:272 + blk * 16] = 1.0 / np.minimum(tt + 1.0, wa)
        c[64:128, 256 + blk * 16:272 + blk * 16] = 1.0 / np.minimum(tt + 1.0, wb)
        c[0:64, 288 + blk] = 1.0 / wa
        c[64:128, 288 + blk] = 1.0 / wb
    return c


def _vecs(norm_mix_g, norm_ffn_g, pool_scale, conv_w, q_norm_g, k_norm_g, forget_b):
    v = np.zeros((128, NVEC), np.float32)
    f = lambda a: np.asarray(a, np.float32).reshape(4, 8, 128).transpose(2, 0, 1).reshape(128, 32)
    v[:, 0:32] = f(norm_mix_g)
    v[:, 32:64] = f(norm_ffn_g)
    v[:, 64:96] = f(pool_scale)
    v[:, 96:120] = np.asarray(conv_w, np.float32).reshape(4, 3, 2, 128).transpose(3, 0, 1, 2).reshape(128, 24)
    v[0:64, 120:124] = np.asarray(q_norm_g, np.float32).T
    v[64:128, 120:124] = np.asarray(q_norm_g, np.float32).T
    v[0:64, 124:128] = np.asarray(k_norm_g, np.float32).T
    v[64:128, 124:128] = np.asarray(k_norm_g, np.float32).T
    v[0:8, 128:132] = np.asarray(forget_b, np.float32).T
    return v


_NC_CACHE = {}


def make_in_maps(inputs, cores):
    x = np.asarray(inputs["x"], np.float32)
    shared = {
        "w_in": np.ascontiguousarray(inputs["w_in"], np.float32),
        "w_attn_out": np.ascontiguousarray(inputs["w_attn_out"], np.float32),
        "w_conv_out": np.ascontiguousarray(inputs["w_conv_out"], np.float32),
        "pool_w": np.ascontiguousarray(inputs["pool_w"], np.float32),
        "w_o": np.ascontiguousarray(inputs["w_o"], np.float32),
        "w_ffn_in": np.ascontiguousarray(inputs["w_ffn_in"], np.float32),
        "w_ffn_out": np.ascontiguousarray(inputs["w_ffn_out"], np.float32),
        "vecs": _vecs(inputs["norm_mix_g"], inputs["norm_ffn_g"], inputs["pool_scale"], inputs["conv_w"],
                      inputs["q_norm_g"], inputs["k_norm_g"], inputs["forget_b"]),
        "cst": _consts(),
    }
    maps = []
    for b in cores:
        m = dict(shared)
        m["xT"] = np.ascontiguousarray(x[b].T)
        maps.append(m)
    return maps


def kernel(**inputs):
    inputs = {k: np.asarray(v) for k, v in inputs.items()}
    if "nc" not in _NC_CACHE:
        _NC_CACHE["nc"] = build(4)[0]
    nc = _NC_CACHE["nc"]
    cores = list(range(8))
    res = run_bass_kernel_spmd(nc, make_in_maps(inputs, cores), core_ids=cores)
    out = np.stack([np.ascontiguousarray(r["yT"].T) for r in res.results], axis=0)
    return out.astype(np.float32)
```

```python
import contextlib
import numpy as np
import concourse.bass as bass
import concourse.mybir as mybir
from concourse.bass_utils import run_bass_kernel_spmd

F32 = mybir.dt.float32
BF16 = mybir.dt.bfloat16
AF = mybir.ActivationFunctionType
ALU = mybir.AluOpType

D = 1024
S = 4096
NCH = 8
CH = 512
KC = 8
D_IN = 5640
D_FF = 2816
NHB = 22
EPS = 1e-6
ENGS = ("pe", "act", "dve", "pool", "sp")

OQ, OK_, OV, OF, OCX, OCB, OCC, OPX, OG = 0, 512, 1024, 1536, 1544, 1800, 2056, 2312, 2568


class Op:
    __slots__ = ("eng", "fn", "deps", "is_dma", "slot", "epoch", "signal",
                 "ordinal", "cum", "ndma", "pos", "marker")


class Prog:
    def __init__(self):
        self.streams = {e: [] for e in ENGS}
        self.lastw = {}
        self.readers = {}
        self.epoch = 0
        self.slot_cum = {}
        self.regions = {}
        self.bufkeys = {}
        self.buf_fence = {}

    @staticmethod
    def _compress(deps):
        best_c = {}
        best_d = {}
        for w in deps:
            if w.is_dma:
                k = (w.slot, w.epoch)
                o = best_d.get(k)
                if o is None or w.cum > o.cum:
                    best_d[k] = w
            else:
                o = best_c.get(w.eng)
                if o is None or w.pos > o.pos:
                    best_c[w.eng] = w
        return list(best_c.values()) + list(best_d.values())

    def region(self, name, lo, hi):
        self.regions[name] = (lo, hi)

    def recycle(self, new_names):
        olds = set()
        for n in new_names:
            lo, hi = self.regions[n]
            for m, (l2, h2) in self.regions.items():
                if l2 < hi and lo < h2:
                    olds.add(m)
        deps = {}
        for n in olds:
            for k in self.bufkeys.get(n, ()):
                w = self.lastw.pop(k, None)
                if w is not None:
                    deps[id(w)] = w
                for r in self.readers.pop(k, ()):
                    deps[id(r)] = r
            self.bufkeys[n] = set()
            m = self.buf_fence.pop(n, None)
            if m is not None:
                for w in m.deps:
                    deps[id(w)] = w
        M = Op()
        M.marker = True
        M.deps = self._compress(deps.values())
        for n in new_names:
            self.buf_fence[n] = M

    def marker(self, reads=(), writes=()):
        deps = {}

        def put(w):
            if w.marker:
                for x in w.deps:
                    deps[id(x)] = x
            else:
                deps[id(w)] = w
        for k in reads:
            w = self.lastw.get(k)
            if w is not None:
                put(w)
        for k in writes:
            w = self.lastw.get(k)
            if w is not None:
                put(w)
            for r in self.readers.get(k, ()):
                put(r)
        M = Op()
        M.marker = True
        M.deps = self._compress(deps.values())
        for k in writes:
            self.lastw[k] = M
            self.readers[k] = []
        return M

    def add(self, eng, fn, reads=(), writes=(), dma=None, ndma=1):
        op = Op()
        op.marker = False
        op.eng = eng
        op.fn = fn
        op.is_dma = dma is not None
        op.slot = dma
        op.epoch = 0 if dma is not None else self.epoch
        op.signal = op.is_dma
        op.ordinal = 0
        op.ndma = ndma
        op.pos = len(self.streams[eng])
        deps = {}

        def put(w, raw):
            if w.marker:
                for x in w.deps:
                    if id(x) not in deps:
                        deps[id(x)] = (x, True)
            else:
                o = deps.get(id(w))
                if o is None or (raw and not o[1]):
                    deps[id(w)] = (w, raw)

        for k in reads:
            w = self.lastw.get(k)
            if w is None:
                w = self.buf_fence.get(k[0])
            if w is not None:
                put(w, True)
        for k in writes:
            w = self.lastw.get(k)
            if w is None:
                w = self.buf_fence.get(k[0])
            if w is not None:
                put(w, False)
            for r in self.readers.get(k, ()):
                put(r, False)
        final = []
        for w, raw in deps.values():
            if w is op:
                continue
            if (not w.is_dma) and (not op.is_dma) and w.eng == eng == "pe":
                continue
            final.append(w)
        final = self._compress(final)
        for w in final:
            w.signal = True
        op.deps = final
        if op.is_dma:
            key = (dma, 0)
            c = self.slot_cum.get(key, 0) + ndma
            self.slot_cum[key] = c
            op.cum = c
        for k in reads:
            self.readers.setdefault(k, []).append(op)
            self.bufkeys.setdefault(k[0], set()).add(k)
        for k in writes:
            self.lastw[k] = op
            self.readers[k] = []
            self.bufkeys.setdefault(k[0], set()).add(k)
        self.streams[eng].append(op)
        return op

    def emit(self, nc, stack):
        for e in ENGS:
            cnt = {}
            for op in self.streams[e]:
                if op.is_dma or not op.signal:
                    continue
                c = cnt.get(op.epoch, 0) + 1
                cnt[op.epoch] = c
                op.ordinal = c
        sems = {}

        def sem_for(key):
            if key not in sems:
                sems[key] = stack.enter_context(nc.semaphore("s%d" % len(sems)))
            return sems[key]

        for e in ENGS:
            for op in self.streams[e]:
                if op.is_dma:
                    sem_for(("d", op.slot, op.epoch))
                elif op.signal:
                    sem_for(("e", op.eng, op.epoch))
        self.nsems = len(sems)
        block = stack.enter_context(nc.Block())
        streams = self.streams

        def run(eng_name):
            def body(engine):
                seen = {}
                for op in streams[eng_name]:
                    need = {}
                    for d in op.deps:
                        if d.is_dma:
                            k = ("d", d.slot, d.epoch)
                            v = 16 * d.cum
                        else:
                            k = ("e", d.eng, d.epoch)
                            v = d.ordinal
                        if v > need.get(k, 0):
                            need[k] = v
                    for k, v in need.items():
                        if v > seen.get(k, 0):
                            engine.wait_ge(sems[k], v)
                            seen[k] = v
                    if op.fn is None:
                        continue
                    r = op.fn(engine)
                    if op.is_dma:
                        s = sems[("d", op.slot, op.epoch)]
                        if not isinstance(r, (list, tuple)):
                            r = [r]
                        assert len(r) == op.ndma
                        for inst in r:
                            inst.then_inc(s, 16)
                    elif op.signal:
                        if isinstance(r, (list, tuple)):
                            r = r[-1]
                        r.then_inc(sems[("e", op.eng, op.epoch)], 1)
            return body

        block.tensor(run("pe"))
        block.scalar(run("act"))
        block.vector(run("dve"))
        block.gpsimd(run("pool"))
        block.sync(run("sp"))


OFF_HT = 0
OFF_AT = 65536
OFF_C = 98304
OFF_D = 131072
OFF_E = 164352
AR_BYTES = 201728
NCST = 290
USE_PAIR = True
NVEC = 132


class _Stop(Exception):
    pass


def build(depth=4, dbg=(), stop=None):
    nc = bass.Bass("TRN2", target_bir_lowering=False)
    dram = lambda name, shape, dt=F32, kind="ExternalInput": nc.dram_tensor(name, shape, dt, kind=kind).ap()
    xT = dram("xT", [NCH, 128, KC, CH])
    w_in = dram("w_in", [4, D, D_IN])
    w_ao = dram("w_attn_out", [4, 512, D])
    w_co = dram("w_conv_out", [4, 256, D])
    w_po = dram("pool_w", [4, 4, 64, 256])
    w_o = dram("w_o", [4, D, D])
    w_fi = dram("w_ffn_in", [4, D, 2 * D_FF])
    w_fo = dram("w_ffn_out", [4, D_FF, D])
    vecs_d = dram("vecs", [128, NVEC])
    cst_d = dram("cst", [128, NCST])
    yT = dram("yT", [NCH, 128, KC, CH], kind="ExternalOutput")
    xs = dram("xs_scr", [NCH, 128, KC, CH], kind="Internal")
    mT = dram("mT_scr", [D, S], BF16, kind="Internal")
    cpd = dram("cp_scr", [8, 4, S], BF16, kind="Internal")
    dbg_out = {}
    for nm, shape, dt in (("d_hT", [D, S], BF16), ("d_aT", [512, S], BF16), ("d_cbz", [256, S], BF16),
                          ("d_dT", [256, S], BF16), ("d_c", [8, S], F32), ("d_qa", [68, S], BF16),
                          ("d_ka", [68, S], BF16), ("d_v", [128, 32 * 8 * 65], BF16)):
        if nm in dbg:
            dbg_out[nm] = dram(nm, shape, dt, kind="ExternalOutput")

    P = Prog()
    st = contextlib.ExitStack()
    with st:
        sbt = lambda name, shape, dt: st.enter_context(nc.sbuf_tensor(name, shape, dt))
        arena = sbt("arena", [128, AR_BYTES // 2], BF16)
        cst = sbt("cstf", [128, NCST], F32)
        vec = sbt("vecf", [128, NVEC], F32)
        ident = sbt("ident", [128, 128], BF16)
        maskb = sbt("maskb", [128, 128], BF16)
        ones = sbt("onesb", [128, 128], BF16)
        sel = sbt("sel", [65, 64], BF16)
        gk8 = sbt("gk8", [128, 4], F32)
        nfb = sbt("nfb", [8, 4], F32)
        onef = sbt("onef", [8, 1], F32)
        bd = sbt("bd", [128, 128], BF16)
        PS = [st.enter_context(nc.psum_tensor("ps%d" % i, [128, 512], F32)) for i in range(8)]

        def V(name, off, nbytes, p0=0, p1=128, dt=BF16, pat=None, **kw):
            P.region(name, off, off + nbytes)
            ap = arena[p0:p1, off // 2:(off + nbytes) // 2]
            if dt == F32:
                ap = ap.bitcast(F32)
            if pat:
                ap = ap.rearrange(pat, **kw)
            return ap

        g1 = vec[:, 0:32].rearrange("p (l k) -> p l k", k=8)
        g2 = vec[:, 32:64].rearrange("p (l k) -> p l k", k=8)
        psc = vec[:, 64:96].rearrange("p (l k) -> p l k", k=8)
        cw = vec[:, 96:120].rearrange("p (l j b) -> p l j b", j=3, b=2)
        gq2 = vec[:, 120:124]
        gk = vec[:, 124:128]
        fb = vec[0:8, 128:132]
        invtab = [cst[:, 256:272], cst[:, 272:288]]
        invw = cst[:, 288:290]

        hT = V("hT", OFF_HT, 65536, pat="p (k n) -> p k n", n=S)
        aT = V("aT", OFF_AT, 32768, pat="p (k n) -> p k n", n=S)
        QA = [V("QA%d" % i, OFF_C + i * 8192, 8192) for i in range(2)]
        KA = [V("KA%d" % i, OFF_C + 16384 + i * 8192, 8192) for i in range(2)]
        VA = V("VA", OFF_D, 33280, pat="p (t h d) -> p t h d", h=8, d=65)

        pscnt = [0]

        def bank():
            b = pscnt[0] % 8
            pscnt[0] += 1
            return b

        def mm(ps_ap, pskey, lhsT, rhs, start, stop, reads):
            P.add("pe", lambda e: e.matmul(ps_ap, lhsT=lhsT, rhs=rhs, start=start, stop=stop),
                  reads=reads, writes=[pskey])

        def gemm(ps_ap, b, pairs, reads):
            n = len(pairs)
            for i, (l, r) in enumerate(pairs):
                mm(ps_ap, ("ps", b), l, r, i == 0, i == n - 1, reads)

        def wdma(out_ap, in_ap, key, slot, n=1):
            P.add("pool", lambda e: e.dma_start(out=out_ap, in_=in_ap), writes=[key], dma=slot)

        P.add("sp", lambda e: e.dma_start(out=cst[:], in_=cst_d), writes=[("cst",)], dma="cst")
        P.add("sp", lambda e: e.dma_start(out=vec[:], in_=vecs_d), writes=[("vec",)], dma="vec")
        P.add("dve", lambda e: e.tensor_copy(out=ident[:], in_=cst[:, 0:128]), reads=[("cst",)], writes=[("ident",)])
        P.add("dve", lambda e: e.tensor_copy(out=maskb[:], in_=cst[:, 128:256]), reads=[("cst",)], writes=[("maskb",)])
        P.add("pool", lambda e: e.memset(ones[:], 1.0), writes=[("ones",)])
        P.add("pool", lambda e: e.memset(sel[:], 0.0), writes=[("sel",)])
        P.add("pool", lambda e: e.memset(sel[64:65, :], 1.0), writes=[("sel",)])
        P.add("pool", lambda e: e.memset(onef[:], 1.0), writes=[("onef",)])
        P.add("pool", lambda e: e.memset(bd[:], 0.0), writes=[("bd",)])
        P.add("pool", lambda e: e.memset(bd[0:64, 0:64], 1.0), writes=[("bd",)])
        P.add("pool", lambda e: e.memset(bd[64:128, 64:128], 1.0), writes=[("bd",)])
        P.add("dve", lambda e: e.tensor_scalar(out=gk8[:], in0=gk, scalar1=0.125, scalar2=None, op0=ALU.mult),
              reads=[("vec",)], writes=[("gk8",)])
        P.add("dve", lambda e: e.tensor_scalar(out=nfb[:], in0=fb, scalar1=-1.0, scalar2=None, op0=ALU.mult),
              reads=[("vec",)], writes=[("nfb",)])

        def norm_a(xc_ap, xckeys, sq_ap, sqkey):
            P.add("act", lambda e: e.activation(out=sq_ap, in_=xc_ap, func=AF.Square),
                  reads=xckeys, writes=[sqkey])

        def norm_b(l, c, xc_ap, xckeys, sq_ap, sqkey, gvec, sd_ap, sdkey):
            b = bank()
            gemm(PS[b][:, :], b, [(ones[:, :], sq_ap[:, k, :]) for k in range(KC)], [sqkey, ("ones",)])
            P.add("act", lambda e: e.activation(out=sd_ap, in_=PS[b][:, :], func=AF.Ln, scale=1.0 / D, bias=EPS),
                  reads=[("ps", b)], writes=[sdkey])
            P.add("act", lambda e: e.activation(out=sd_ap, in_=sd_ap, func=AF.Exp, scale=-0.5),
                  reads=[sdkey], writes=[sdkey])
            for k in range(KC):
                P.add("dve", (lambda k: lambda e: e.scalar_tensor_tensor(
                    out=hT[:, k, c * CH:(c + 1) * CH], in0=xc_ap[:, k, :], scalar=gvec[:, l, k:k + 1],
                    in1=sd_ap, op0=ALU.mult, op1=ALU.mult))(k),
                    reads=[xckeys[k], sdkey, ("vec",)], writes=[("hT", c, k)])

        def hk(c):
            return [("hT", c, k) for k in range(KC)]

        def chk(name):
            if stop == name:
                raise _Stop()

        def layer(l):
            P.epoch = l
            src = xT if l == 0 else xs
            srckey = "xT" if l == 0 else "xs"
            last = (l == depth - 1)
            dst = yT if last else xs
            dstkey = "yT" if last else "xs"

            xc = [V("xc%d" % i, OFF_AT + i * 16384, 16384, dt=F32, pat="p (k n) -> p k n", n=CH) for i in range(2)]
            sqx = [V("sqx%d" % i, OFF_AT + 32768 + i * 8192, 8192, pat="p (k n) -> p k n", n=CH) for i in range(2)]
            sdt = [V("sdt%d" % i, OFF_AT + 49152 + i * 2048, 2048, dt=F32) for i in range(2)]
            rst = [V("rst%d" % i, OFF_AT + 53248 + i * 2048, 2048, dt=F32) for i in range(2)]
            P.recycle(["xc0", "xc1", "sqx0", "sqx1", "sdt0", "sdt1", "rst0", "rst1"])
            for c in range(NCH):
                i = c % 2
                P.add("sp", (lambda c, i: lambda e: e.dma_start(out=xc[i], in_=src[c]))(c, i),
                    reads=[(srckey, c)], writes=[("xc%d" % i, o) for o in range(KC)], dma="xc%d" % i)
                xk = [("xc%d" % i, o) for o in range(KC)]
                norm_a(xc[i], xk, sqx[i], ("sqx%d" % i,))
                norm_b(l, c, xc[i], xk, sqx[i], ("sqx%d" % i,), g1, sdt[i], ("sdt%d" % i,))
            if l == 0 and "d_hT" in dbg_out:
                P.add("sp", lambda e: e.dma_start(out=dbg_out["d_hT"].rearrange("(k p) n -> p k n", p=128), in_=hT),
                      reads=[k_ for c in range(NCH) for k_ in hk(c)], writes=[("d_hT",)], dma="dbg")

            chk('P2a')
            fE = V("fE", OFF_C, 16384, p0=0, p1=8, dt=F32)
            fR = V("fR", OFF_C + 16384, 16384, p0=0, p1=8, dt=F32)
            cp4 = V("cp4", OFF_AT, 32768, p0=0, p1=8, pat="p (j n) -> p j n", n=S)
            wf = V("wf", OFF_E, 128, pat="p (k n) -> p k n", n=8)
            wv = V("wv", OFF_E + 128, 8192, pat="p (k n) -> p k n", n=512)
            P.recycle(["fE", "fR", "cp4", "wf", "wv"])
            wdma(wf, w_in[l, :, OF:OF + 8].rearrange("(k p) n -> p k n", p=128), ("wf",), "wf")
            wdma(wv, w_in[l, :, OV:OV + 512].rearrange("(k p) n -> p k n", p=128), ("wv",), "wv")
            for c in range(NCH):
                b = bank()
                gemm(PS[b][0:8, :], b, [(wf[:, k, :], hT[:, k, c * CH:(c + 1) * CH]) for k in range(KC)],
                     [("wf",)] + hk(c))
                P.add("act", (lambda c, b: lambda e: e.activation(
                    out=fE[:, c * CH:(c + 1) * CH], in_=PS[b][0:8, :], func=AF.Exp, scale=-1.0, bias=nfb[:, l:l + 1]))(c, b),
                    reads=[("ps", b), ("nfb",)], writes=[("fE", c)])
            P.add("act", lambda e: e.activation(out=fE, in_=fE, func=AF.Ln, scale=1.0, bias=1.0),
                  reads=[("fE", c) for c in range(NCH)], writes=[("fE", "ln")])
            P.add("dve", lambda e: e.tensor_tensor_scan(out=fR, data0=onef[:, 0:1].to_broadcast([8, S]), data1=fE,
                                                        initial=0.0, op0=ALU.mult, op1=ALU.subtract),
                  reads=[("fE", "ln"), ("onef",)], writes=[("fR",)])
            if l == 0 and "d_c" in dbg_out:
                P.add("sp", lambda e: e.dma_start(out=dbg_out["d_c"], in_=fR), reads=[("fR",)], writes=[("d_c",)], dma="dbg")
            P.add("dve", lambda e: e.tensor_copy(out=cp4[:, 0, :], in_=fR), reads=[("fR",)], writes=[("cp4", 0)])
            P.add("dve", lambda e: e.tensor_tensor(out=fE, in0=fR, in1=cp4[:, 0, :], op=ALU.subtract),
                  reads=[("fR",), ("cp4", 0)], writes=[("fE", "res")])
            P.add("dve", lambda e: e.tensor_copy(out=cp4[:, 1, :], in_=fE), reads=[("fE", "res")], writes=[("cp4", 1)])
            P.add("dve", lambda e: e.tensor_scalar(out=cp4[:, 2:4, :], in0=cp4[:, 0:2, :], scalar1=-1.0, scalar2=None,
                                                   op0=ALU.mult),
                  reads=[("cp4", 0), ("cp4", 1)], writes=[("cp4", 2)])
            P.add("sp", lambda e: e.dma_start(out=cpd, in_=cp4), reads=[("cp4", 0), ("cp4", 1), ("cp4", 2)],
                  writes=[("cpd",)], dma="cpd")

            chk('P2b')
            P.recycle(["VA"])
            P.add("pool", lambda e: e.memset(VA[:, :, :, 64:65], 1.0), writes=[("VA", "ones")])
            for tt in range(32):
                b = bank()
                gemm(PS[b][:, :], b, [(hT[:, k, tt * 128:(tt + 1) * 128], wv[:, k, :]) for k in range(KC)],
                     [("wv",)] + hk(tt // 4))
                P.add("dve", (lambda tt, b: lambda e: e.tensor_copy(
                    out=VA[:, tt, :, 0:64], in_=PS[b][:, :].rearrange("p (h d) -> p h d", d=64)))(tt, b),
                    reads=[("ps", b)], writes=[("VA", tt)])
            if l == 0 and "d_v" in dbg_out:
                P.add("sp", lambda e: e.dma_start(out=dbg_out["d_v"], in_=VA.rearrange("p t h d -> p (t h d)")),
                      reads=[("VA", tt) for tt in range(32)] + [("VA", "ones")], writes=[("d_v",)], dma="dbg")

            chk('P2c')
            E0 = OFF_E + 8320
            PT = [V("PT%d" % i, E0 + i * 1024, 1024) for i in range(4)]
            wqk = [V("wqk%d" % i, E0 + 4096 + i * 4096, 4096, pat="p (a k n) -> p a k n", a=2, n=128) for i in range(2)]
            sqh = [V("sqh%d" % i, E0 + 12288 + i * 1024, 1024) for i in range(2)]
            lnh = [V("lnh%d" % i, E0 + 14336 + i * 2048, 2048, dt=F32) for i in range(2)]
            recf = V("recf", E0 + 18432, 2048, p0=0, p1=65, dt=F32)
            rdh = V("rdh", E0 + 20480, 1024, p0=0, p1=65)
            rdl = V("rdl", E0 + 21504, 1024, p0=0, p1=65)
            bcs = [V("bcs%d" % i, E0 + 22528 + i * 2048, 2048, p0=0, p1=64, dt=F32) for i in range(2)]
            atm = [V("atm%d" % i, E0 + 26624 + i * 1024, 1024, p0=0, p1=64) for i in range(2)]
            qtm = [V("qtm%d" % i, E0 + 26624 + i * 1024, 1024, p0=64, p1=128) for i in range(2)]
            assert E0 + 28672 <= AR_BYTES
            P.recycle(["PT%d" % i for i in range(4)] + ["wqk0", "wqk1", "sqh0", "sqh1", "lnh0", "lnh1",
                                                       "recf", "rdh", "rdl", "bcs0", "bcs1", "atm0", "atm1", "qtm0", "qtm1"])
            P.recycle(["QA0", "QA1", "KA0", "KA1", "aT"])
            P.add("pool", lambda e: e.memset(rdh[:, :], 0.0), writes=[("rdh",)])
            P.add("pool", lambda e: e.memset(rdl[:, :], 0.0), writes=[("rdl",)])
            if USE_PAIR:
                P.add("pool", lambda e: e.memset(QA[0][64:68, :], 1.0), writes=[("QA0", "aug")])
                P.add("pool", lambda e: e.memset(KA[0][64:68, :], 1.0), writes=[("KA0", "aug")])
                P.add("pool", lambda e: e.memset(QA[1][0:64, :], 0.0), writes=[("QA1", "aug")])
                P.add("pool", lambda e: e.memset(KA[1][0:64, :], 0.0), writes=[("KA1", "aug")])
                P.add("pool", lambda e: e.memset(QA[1][32:34, :], 1.0), writes=[("QA1", "aug")])
                P.add("pool", lambda e: e.memset(KA[1][0:2, :], 1.0), writes=[("KA1", "aug")])
            else:
                for i in range(2):
                    P.add("pool", (lambda i: lambda e: e.memset(QA[i][64:68, :], 1.0))(i), writes=[("QA%d" % i, "aug")])
                    P.add("pool", (lambda i: lambda e: e.memset(KA[i][64:68, :], 1.0))(i), writes=[("KA%d" % i, "aug")])

            ucnt = [0]

            def load_wqk(p):
                pb_ = p % 2
                wdma(wqk[pb_][:, 0], w_in[l, :, OQ + p * 128:OQ + (p + 1) * 128].rearrange("(k p) n -> p k n", p=128),
                     ("wqk%d" % pb_, 0), "wq%d" % pb_)
                wdma(wqk[pb_][:, 1], w_in[l, :, OK_ + p * 128:OK_ + (p + 1) * 128].rearrange("(k p) n -> p k n", p=128),
                     ("wqk%d" % pb_, 1), "wk%d" % pb_)

            def proj_pair(p):
                pb_ = p % 2
                for hh, (qr, kr) in enumerate(((64, 66), (0, 32))):
                    h = 2 * p + hh
                    P.add("sp", (lambda hh, h, qr: lambda e: e.dma_start(out=QA[hh][qr:qr + 2, :], in_=cpd[h, 0:2, :]))(hh, h, qr),
                          reads=[("cpd",)], writes=[("QA%d" % hh, "aug")], dma="qaug%d" % hh)
                    P.add("sp", (lambda hh, h, kr: lambda e: e.dma_start(out=KA[hh][kr:kr + 2, :], in_=cpd[h, 2:4, :]))(hh, h, kr),
                          reads=[("cpd",)], writes=[("KA%d" % hh, "aug")], dma="kaug%d" % hh)
                units = [(c, a) for c in range(NCH) for a in range(2)]
                st_ = {}

                def stage_a(ui):
                    c, a = units[ui]
                    u = ucnt[0] % 2
                    ucnt[0] += 1
                    b = bank()
                    st_[ui] = (u, b)
                    gemm(PS[b][:, :], b, [(wqk[pb_][:, a, k, :], hT[:, k, c * CH:(c + 1) * CH]) for k in range(KC)],
                         [("wqk%d" % pb_, a)] + hk(c))
                    P.add("act", lambda e: e.activation(out=sqh[u], in_=PS[b][:, :], func=AF.Square),
                          reads=[("ps", b)], writes=[("sqh%d" % u,)])

                def stage_b(ui):
                    c, a = units[ui]
                    u, b = st_[ui]
                    b2 = bank()
                    mm(PS[b2][:, :], ("ps", b2), bd[:, :], sqh[u], True, True, [("sqh%d" % u,), ("bd",)])
                    P.add("act", lambda e: e.activation(out=lnh[u], in_=PS[b2][:, :], func=AF.Ln, scale=1.0 / 64, bias=EPS),
                          reads=[("ps", b2)], writes=[("lnh%d" % u,)])
                    P.add("act", lambda e: e.activation(out=lnh[u], in_=lnh[u], func=AF.Exp, scale=-0.5),
                          reads=[("lnh%d" % u,)], writes=[("lnh%d" % u,)])
                    dt_ = (QA, KA)[a]
                    dn = ("QA", "KA")[a]
                    gv = (gq2, gk8)[a]
                    P.add("dve", lambda e: e.scalar_tensor_tensor(
                        out=dt_[0][0:64, c * CH:(c + 1) * CH], in0=PS[b][0:64, :], scalar=gv[0:64, l:l + 1],
                        in1=lnh[u][0:64, :], op0=ALU.mult, op1=ALU.mult),
                        reads=[("ps", b), ("lnh%d" % u,), ("vec",), ("gk8",)], writes=[(dn + "0", c)])
                    P.add("dve", lambda e: e.scalar_tensor_tensor(
                        out=dt_[1][64:128, c * CH:(c + 1) * CH], in0=PS[b][64:128, :], scalar=gv[64:128, l:l + 1],
                        in1=lnh[u][64:128, :], op0=ALU.mult, op1=ALU.mult),
                        reads=[("ps", b), ("lnh%d" % u,), ("vec",), ("gk8",)], writes=[(dn + "1", c)])

                nU = len(units)
                stage_a(0)
                for ui in range(nU):
                    if ui + 1 < nU:
                        stage_a(ui + 1)
                    stage_b(ui)

            def proj_head(h, pb_):
                hh = h % 2
                units = [(c, a) for c in range(NCH) for a in range(2)]
                st_ = {}

                def stage_a(ui):
                    c, a = units[ui]
                    u = ucnt[0] % 2
                    ucnt[0] += 1
                    b = bank()
                    st_[ui] = (u, b)
                    gemm(PS[b][0:64, :], b, [(wqk[pb_][:, a, k, hh * 64:(hh + 1) * 64], hT[:, k, c * CH:(c + 1) * CH]) for k in range(KC)],
                         [("wqk%d" % pb_, a)] + hk(c))
                    P.add("act", lambda e: e.activation(out=sqh[u][0:64, :], in_=PS[b][0:64, :], func=AF.Square),
                          reads=[("ps", b)], writes=[("sqh%d" % u,)])

                def stage_b(ui):
                    c, a = units[ui]
                    u, b = st_[ui]
                    b2 = bank()
                    mm(PS[b2][0:64, :], ("ps", b2), ones[0:64, 0:64], sqh[u][0:64, :], True, True, [("sqh%d" % u,), ("ones",)])
                    P.add("act", lambda e: e.activation(out=lnh[u][0:64, :], in_=PS[b2][0:64, :], func=AF.Ln, scale=1.0 / 64, bias=EPS),
                          reads=[("ps", b2)], writes=[("lnh%d" % u,)])
                    P.add("act", lambda e: e.activation(out=lnh[u][0:64, :], in_=lnh[u][0:64, :], func=AF.Exp, scale=-0.5),
                          reads=[("lnh%d" % u,)], writes=[("lnh%d" % u,)])
                    dt_ = (QA, KA)[a]
                    dn = ("QA", "KA")[a]
                    gv = (gq2, gk8)[a]
                    P.add("dve", lambda e: e.scalar_tensor_tensor(
                        out=dt_[hh][0:64, c * CH:(c + 1) * CH], in0=PS[b][0:64, :], scalar=gv[0:64, l:l + 1],
                        in1=lnh[u][0:64, :], op0=ALU.mult, op1=ALU.mult),
                        reads=[("ps", b), ("lnh%d" % u,), ("vec",), ("gk8",)], writes=[(dn + "%d" % hh, c)])

                nU = len(units)
                stage_a(0)
                for ui in range(nU):
                    if ui + 1 < nU:
                        stage_a(ui + 1)
                    stage_b(ui)

            def aug_rows(p):
                for hh in range(2):
                    h = 2 * p + hh
                    P.add("sp", (lambda hh, h: lambda e: e.dma_start(out=QA[hh][64:66, :], in_=cpd[h, 0:2, :]))(hh, h),
                          reads=[("cpd",)], writes=[("QA%d" % hh, "aug")], dma="qaug%d" % hh)
                    P.add("sp", (lambda hh, h: lambda e: e.dma_start(out=KA[hh][66:68, :], in_=cpd[h, 2:4, :]))(hh, h),
                          reads=[("cpd",)], writes=[("KA%d" % hh, "aug")], dma="kaug%d" % hh)

            ocnt = [0]

            def attn_head(h):
                hb = h % 2
                qa, ka = QA[hb], KA[hb]
                qn, kn = "QA%d" % hb, "KA%d" % hb
                tiles = []
                for c in range(NCH):
                    for j in range(4 * c + 4):
                        tiles.append((c, j))
                n = len(tiles)
                state = {}
                pending = []

                def emit_S(i):
                    c, j = tiles[i]
                    n0 = max(0, j - 4 * c) * 128
                    diag = j >= 4 * c
                    sbk = i % 4
                    rd = [(qn, c), (kn, j // 4), (qn, "aug"), (kn, "aug")]
                    kk = 128 if (USE_PAIR and hb == 1) else 68
                    mm(PS[sbk][:, n0:CH], ("ps", sbk), ka[0:kk, j * 128:(j + 1) * 128],
                       qa[0:kk, c * CH + n0:(c + 1) * CH], True, not diag, rd)
                    if diag:
                        mm(PS[sbk][:, n0:n0 + 128], ("ps", sbk), ident[:, :], maskb[:, :], False, True,
                           [("ident",), ("maskb",)])
                    P.add("act", lambda e: e.activation(out=PT[sbk][:, n0:CH], in_=PS[sbk][:, n0:CH], func=AF.Exp),
                          reads=[("ps", sbk)], writes=[("PT%d" % sbk,)])

                def emit_PV(i):
                    c, j = tiles[i]
                    n0 = max(0, j - 4 * c) * 128
                    sbk = i % 4
                    lastj = 4 * c + 3
                    if j == 0:
                        state["ob"] = 4 + (ocnt[0] % 2)
                        ocnt[0] += 1
                    ob = state["ob"]
                    mm(PS[ob][0:65, n0:CH], ("ps", ob), VA[:, j, h, 0:65], PT[sbk][:, n0:CH], j == 0, j == lastj,
                       [("VA", j), ("VA", "ones"), ("PT%d" % sbk,)])
                    if j == lastj:
                        u = c % 2
                        P.add("dve", lambda e: e.reciprocal(out=recf[64:65, :], in_=PS[ob][64:65, :]),
                              reads=[("ps", ob)], writes=[("recf",)])
                        P.add("dve", lambda e: e.tensor_copy(out=rdh[64:65, :], in_=recf[64:65, :]),
                              reads=[("recf",)], writes=[("rdh",)])
                        P.add("dve", lambda e: e.tensor_tensor(out=rdl[64:65, :], in0=recf[64:65, :], in1=rdh[64:65, :],
                                                               op=ALU.subtract),
                              reads=[("recf",), ("rdh",)], writes=[("rdl",)])

                        def part2():
                            mm(PS[6][0:64, :], ("ps", 6), sel[:, :], rdh[:, :], True, False, [("sel",), ("rdh",)])
                            mm(PS[6][0:64, :], ("ps", 6), sel[:, :], rdl[:, :], False, True, [("sel",), ("rdl",)])
                            P.add("act", lambda e: e.activation(out=bcs[u], in_=PS[6][0:64, :], func=AF.Copy),
                                  reads=[("ps", 6)], writes=[("bcs%d" % u,)])
                            if h % 2 == 0:
                                P.add("dve", lambda e: e.tensor_tensor(out=aT[0:64, h // 2, c * CH:(c + 1) * CH],
                                                                       in0=PS[ob][0:64, :], in1=bcs[u], op=ALU.mult),
                                      reads=[("ps", ob), ("bcs%d" % u,)], writes=[("aT", h // 2, c, 0)])
                            else:
                                P.add("dve", lambda e: e.tensor_tensor(out=atm[u], in0=PS[ob][0:64, :], in1=bcs[u], op=ALU.mult),
                                      reads=[("ps", ob), ("bcs%d" % u,)], writes=[("atm%d" % u,)])
                                P.add("sp", lambda e: e.dma_start(out=aT[64:128, h // 2, c * CH:(c + 1) * CH], in_=atm[u]),
                                      reads=[("atm%d" % u,)], writes=[("aT", h // 2, c, 1)], dma="atm%d" % u)
                        pending.append((i + 3, part2))

                LA = 3
                for i in range(min(LA, n)):
                    emit_S(i)
                for i in range(n):
                    if i + LA < n:
                        emit_S(i + LA)
                    emit_PV(i)
                    while pending and pending[0][0] <= i:
                        pending.pop(0)[1]()
                while pending:
                    pending.pop(0)[1]()

            pscnt[0] = 0
            load_wqk(0)
            for p in range(4):
                if p + 1 < 4:
                    load_wqk(p + 1)
                if USE_PAIR:
                    proj_pair(p)
                else:
                    aug_rows(p)
                    proj_head(2 * p, p % 2)
                    proj_head(2 * p + 1, p % 2)
                attn_head(2 * p)
                attn_head(2 * p + 1)
            if l == 0 and "d_aT" in dbg_out:
                P.add("sp", lambda e: e.dma_start(out=dbg_out["d_aT"].rearrange("(k p) n -> p k n", p=128), in_=aT),
                      reads=[("aT", k, c, q) for k in range(4) for c in range(NCH) for q in range(2)],
                      writes=[("d_aT",)], dma="dbg")

            chk('P2d')
            NPB = 16448
            ub = V("ub", OFF_D, NPB, dt=F32)
            pa = V("pa", OFF_D + NPB, NPB, dt=F32)
            pb = V("pb", OFF_D + 2 * NPB, NPB, dt=F32)
            wpx = [V("wpx%d" % i, OFF_D + 3 * NPB + i * 2048, 2048, pat="p (k n) -> p k n", n=128) for i in range(2)]
            dT = V("dT", OFF_C + 16384, 16384, pat="p (k n) -> p k n", n=S)
            cbz = V("cbz", OFF_C, 16384, pat="p (k n) -> p k n", n=S)
            assert OFF_D + 3 * NPB + 4096 <= AR_BYTES
            P.recycle(["ub", "pa", "pb", "wpx0", "wpx1", "dT", "cbz"])
            for t_, nm in ((ub, "ub"), (pa, "pa"), (pb, "pb")):
                P.add("pool", (lambda t_: lambda e: e.memset(t_[:, 0:16], 0.0))(t_), writes=[(nm, "halo")])
            H = 16
            for blk in range(2):
                wdma(wpx[blk], w_in[l, :, OPX + blk * 128:OPX + (blk + 1) * 128].rearrange("(k p) n -> p k n", p=128),
                     ("wpx%d" % blk,), "wpx%d" % blk)
            for blk in range(2):
                for c in range(NCH):
                    b = bank()
                    gemm(PS[b][:, :], b, [(wpx[blk][:, k, :], hT[:, k, c * CH:(c + 1) * CH]) for k in range(KC)],
                         [("wpx%d" % blk,)] + hk(c))
                    P.add("act", (lambda b, c: lambda e: e.activation(out=ub[:, H + c * CH:H + (c + 1) * CH], in_=PS[b][:, :],
                                                                       func=AF.Copy))(b, c),
                          reads=[("ps", b)], writes=[("ub", c)])
                allu = [("ub", c) for c in range(NCH)] + [("ub", "halo")]
                P.add("dve", lambda e: e.tensor_tensor(out=pa[:, H:H + S], in0=ub[:, H:H + S], in1=ub[:, H - 1:H - 1 + S], op=ALU.add),
                      reads=allu + [("pa", "halo")], writes=[("pa", "w")])
                if blk == 0:
                    P.add("dve", lambda e: e.tensor_tensor(out=pb[64:128, H:H + S], in0=pa[64:128, H:H + S],
                                                           in1=pa[64:128, H - 2:H - 2 + S], op=ALU.add),
                          reads=[("pa", "w"), ("pa", "halo"), ("pb", "halo")], writes=[("pb", "w")])
                    resA, resB = pa, pb
                    rk = [("pa", "w"), ("pb", "w")]
                else:
                    P.add("dve", lambda e: e.tensor_tensor(out=pb[:, H:H + S], in0=pa[:, H:H + S], in1=pa[:, H - 2:H - 2 + S], op=ALU.add),
                          reads=[("pa", "w"), ("pa", "halo"), ("pb", "halo")], writes=[("pb", "w")])
                    P.add("dve", lambda e: e.tensor_tensor(out=pa[:, H:H + S], in0=pb[:, H:H + S], in1=pb[:, H - 4:H - 4 + S], op=ALU.add),
                          reads=[("pb", "w"), ("pb", "halo"), ("pa", "halo")], writes=[("pa", "w")])
                    P.add("dve", lambda e: e.tensor_tensor(out=pb[64:128, H:H + S], in0=pa[64:128, H:H + S],
                                                           in1=pa[64:128, H - 8:H - 8 + S], op=ALU.add),
                          reads=[("pa", "w"), ("pa", "halo"), ("pb", "w")], writes=[("pb", "w2")])
                    resA, resB = pa, pb
                    rk = [("pa", "w"), ("pb", "w2"), ("pb", "w")]
                for (r_, p0, p1) in ((resA, 0, 64), (resB, 64, 128)):
                    P.add("dve", (lambda r_, p0, p1, blk: lambda e: e.scalar_tensor_tensor(
                        out=dT[p0:p1, blk, :], in0=r_[p0:p1, H:H + S], scalar=invw[p0:p1, blk:blk + 1],
                        in1=ub[p0:p1, H:H + S], op0=ALU.mult, op1=ALU.subtract))(r_, p0, p1, blk),
                        reads=rk + allu + [("cst",)], writes=[("dT", blk, p0)])
                    P.add("dve", (lambda r_, p0, p1, blk: lambda e: e.tensor_tensor(
                        out=r_[p0:p1, H:H + 16], in0=r_[p0:p1, H:H + 16], in1=invtab[blk][p0:p1, :], op=ALU.mult))(r_, p0, p1, blk),
                        reads=rk + [("dT", blk, p0), ("cst",)], writes=[("ptmp", blk, p0)])
                    P.add("dve", (lambda r_, p0, p1, blk: lambda e: e.tensor_tensor(
                        out=dT[p0:p1, blk, 0:16], in0=r_[p0:p1, H:H + 16], in1=ub[p0:p1, H:H + 16], op=ALU.subtract))(r_, p0, p1, blk),
                        reads=[("ptmp", blk, p0)] + allu, writes=[("dT", blk, p0)])
                if blk == 0:
                    for nm in ("pa", "pb"):
                        pass
            if l == 0 and "d_dT" in dbg_out:
                P.add("sp", lambda e: e.dma_start(out=dbg_out["d_dT"].rearrange("(k p) n -> p k n", p=128), in_=dT),
                      reads=[("dT", b_, p_) for b_ in range(2) for p_ in (0, 64)], writes=[("d_dT",)], dma="dbg")

            chk('P2e')
            NZ = 16400
            zb = V("zb", OFF_D, NZ, dt=F32)
            cxs = [V("cxs%d" % i, OFF_D + NZ + i * 2048, 2048, dt=F32) for i in range(2)]
            acc = [V("acc%d" % i, OFF_D + NZ + 4096 + i * 2048, 2048, dt=F32) for i in range(2)]
            wc3 = [V("wc3%d" % i, OFF_D + NZ + 8192 + i * 6144, 6144, pat="p (k a n) -> p k a n", a=3, n=128) for i in range(2)]
            P.recycle(["zb", "cxs0", "cxs1", "acc0", "acc1", "wc30", "wc31"])
            P.add("pool", lambda e: e.memset(zb[:, 0:2], 0.0), writes=[("zb", "halo")])
            for blk in range(2):
                for a, off in enumerate((OCX, OCB, OCC)):
                    wdma(wc3[blk][:, :, a, :], w_in[l, :, off + blk * 128:off + (blk + 1) * 128].rearrange("(k p) n -> p k n", p=128),
                         ("wc3%d" % blk, a), "wc3%d_%d" % (blk, a))
            for blk in range(2):
                for c in range(NCH):
                    u = c % 2
                    bx, bb, bc_ = bank(), bank(), bank()
                    for a, b in ((0, bx), (1, bb), (2, bc_)):
                        gemm(PS[b][:, :], b, [(wc3[blk][:, k, a, :], hT[:, k, c * CH:(c + 1) * CH]) for k in range(KC)],
                             [("wc3%d" % blk, a)] + hk(c))
                    P.add("act", (lambda bx, u: lambda e: e.activation(out=cxs[u], in_=PS[bx][:, :], func=AF.Copy))(bx, u),
                          reads=[("ps", bx)], writes=[("cxs%d" % u,)])
                    P.add("dve", (lambda bc_, u, c: lambda e: e.tensor_tensor(out=zb[:, 2 + c * CH:2 + (c + 1) * CH], in0=PS[bc_][:, :],
                                                                              in1=cxs[u], op=ALU.mult))(bc_, u, c),
                          reads=[("ps", bc_), ("cxs%d" % u,)], writes=[("zb", c)])
                    zr = [("zb", c), ("zb", "halo")] + ([("zb", c - 1)] if c > 0 else [])
                    P.add("dve", (lambda u, c, blk: lambda e: e.tensor_scalar(out=acc[u], in0=zb[:, c * CH:(c + 1) * CH],
                                                                              scalar1=cw[:, l, 0, blk:blk + 1], scalar2=None, op0=ALU.mult))(u, c, blk),
                          reads=zr + [("vec",)], writes=[("acc%d" % u,)])
                    for jj in (1, 2):
                        P.add("dve", (lambda u, c, blk, jj: lambda e: e.scalar_tensor_tensor(
                            out=acc[u], in0=zb[:, jj + c * CH:jj + (c + 1) * CH], scalar=cw[:, l, jj, blk:blk + 1],
                            in1=acc[u], op0=ALU.mult, op1=ALU.add))(u, c, blk, jj),
                            reads=zr + [("acc%d" % u,), ("vec",)], writes=[("acc%d" % u,)])
                    P.add("dve", (lambda u, c, blk, bb: lambda e: e.tensor_tensor(out=cbz[:, blk, c * CH:(c + 1) * CH], in0=acc[u],
                                                                                  in1=PS[bb][:, :], op=ALU.mult))(u, c, blk, bb),
                          reads=[("acc%d" % u,), ("ps", bb)], writes=[("cbz", blk, c)])
            if l == 0 and "d_cbz" in dbg_out:
                P.add("sp", lambda e: e.dma_start(out=dbg_out["d_cbz"].rearrange("(k p) n -> p k n", p=128), in_=cbz),
                      reads=[("cbz", b_, c_) for b_ in range(2) for c_ in range(NCH)], writes=[("d_cbz",)], dma="dbg")

            chk('P3a')
            WS = 7936
            wg = [V("wg%d" % i, OFF_D + i * WS, 6144, pat="p (k a n) -> p k a n", a=3, n=128) for i in range(2)]
            wao = [V("wao%d" % i, OFF_D + i * WS + 6144, 1024, pat="p (k n) -> p k n", n=128) for i in range(2)]
            wco = [V("wco%d" % i, OFF_D + i * WS + 7168, 512, pat="p (k n) -> p k n", n=128) for i in range(2)]
            wpo = [V("wpo%d" % i, OFF_D + i * WS + 7680, 256) for i in range(2)]
            T0 = OFF_D + 2 * WS
            sg = [V("sg%d" % i, T0 + i * 2048, 2048, dt=F32) for i in range(2)]
            mt = [V("mt%d" % i, T0 + 4096 + i * 2048, 2048, dt=F32) for i in range(2)]
            tt_ = [V("tt%d" % i, T0 + 8192 + i * 2048, 2048, dt=F32) for i in range(2)]
            mb = [V("mb%d" % i, T0 + 12288 + i * 1024, 1024) for i in range(2)]
            wo = V("wo", T0 + 14336, 16384, pat="p (k n) -> p k n", n=D)
            sd2 = [V("sd2%d" % i, T0 + 30720 + i * 2048, 2048, dt=F32) for i in range(2)]
            rs2 = [V("rs2%d" % i, T0 + 34816 + i * 2048, 2048, dt=F32) for i in range(2)]
            assert T0 + 38912 <= AR_BYTES
            P.recycle(["wg0", "wg1", "wao0", "wao1", "wco0", "wco1", "wpo0", "wpo1", "sg0", "sg1", "mt0", "mt1",
                       "tt0", "tt1", "mb0", "mb1", "wo", "sd20", "sd21", "rs20", "rs21"])

            def load_wset(j):
                i = j % 2
                for a in range(3):
                    wdma(wg[i][:, :, a, :],
                         w_in[l, :, OG + a * D + j * 128:OG + a * D + (j + 1) * 128].rearrange("(k p) n -> p k n", p=128),
                         ("wg%d" % i, a), "wg%d_%d" % (i, a))
                wdma(wao[i], w_ao[l, :, j * 128:(j + 1) * 128].rearrange("(k p) n -> p k n", p=128), ("wao%d" % i,), "wao%d" % i)
                wdma(wco[i], w_co[l, :, j * 128:(j + 1) * 128].rearrange("(k p) n -> p k n", p=128), ("wco%d" % i,), "wco%d" % i)
                g = j // 2
                r0 = (g % 2) * 64
                wdma(wpo[i][r0:r0 + 64, :], w_po[l, g, :, (j % 2) * 128:(j % 2 + 1) * 128], ("wpo%d" % i,), "wpo%d" % i)

            load_wset(0)
            ucnt2 = [0]
            for j in range(8):
                i = j % 2
                if j + 1 < 8:
                    load_wset(j + 1)
                if j == 1:
                    wdma(wo, w_o[l].rearrange("(k p) n -> p k n", p=128), ("wo",), "wo")
                g = j // 2
                r0 = (g % 2) * 64
                for c in range(NCH):
                    u = ucnt2[0] % 2
                    ucnt2[0] += 1
                    cs = slice(c * CH, (c + 1) * CH)
                    bg, by = bank(), bank()
                    gemm(PS[bg][:, :], bg, [(wg[i][:, k, 0, :], hT[:, k, cs]) for k in range(KC)], [("wg%d" % i, 0)] + hk(c))
                    gemm(PS[by][:, :], by, [(wao[i][:, k, :], aT[:, k, cs]) for k in range(4)],
                         [("wao%d" % i,)] + [("aT", k, c, q) for k in range(4) for q in range(2)])
                    P.add("act", (lambda bg, u: lambda e: e.activation(out=sg[u], in_=PS[bg][:, :], func=AF.Sigmoid))(bg, u),
                          reads=[("ps", bg)], writes=[("sg%d" % u,)])
                    P.add("dve", (lambda by, u: lambda e: e.tensor_tensor(out=mt[u], in0=sg[u], in1=PS[by][:, :], op=ALU.mult))(by, u),
                          reads=[("ps", by), ("sg%d" % u,)], writes=[("mt%d" % u,)])
                    bg, by = bank(), bank()
                    gemm(PS[bg][:, :], bg, [(wg[i][:, k, 1, :], hT[:, k, cs]) for k in range(KC)], [("wg%d" % i, 1)] + hk(c))
                    gemm(PS[by][:, :], by, [(wco[i][:, k, :], cbz[:, k, cs]) for k in range(2)],
                         [("wco%d" % i,), ("cbz", 0, c), ("cbz", 1, c)])
                    P.add("act", (lambda bg, u: lambda e: e.activation(out=sg[u], in_=PS[bg][:, :], func=AF.Sigmoid))(bg, u),
                          reads=[("ps", bg)], writes=[("sg%d" % u,)])
                    P.add("dve", (lambda by, u: lambda e: e.tensor_tensor(out=tt_[u], in0=sg[u], in1=PS[by][:, :], op=ALU.mult))(by, u),
                          reads=[("ps", by), ("sg%d" % u,)], writes=[("tt%d" % u,)])
                    P.add("pool", (lambda u: lambda e: e.tensor_tensor(out=mt[u], in0=mt[u], in1=tt_[u], op=ALU.add))(u),
                          reads=[("mt%d" % u,), ("tt%d" % u,)], writes=[("mt%d" % u,)])
                    bg, by = bank(), bank()
                    gemm(PS[bg][:, :], bg, [(wg[i][:, k, 2, :], hT[:, k, cs]) for k in range(KC)], [("wg%d" % i, 2)] + hk(c))
                    mm(PS[by][:, :], ("ps", by), wpo[i][r0:r0 + 64, :], dT[r0:r0 + 64, g // 2, cs], True, True,
                       [("wpo%d" % i,), ("dT", g // 2, r0)])
                    P.add("act", (lambda bg, u: lambda e: e.activation(out=sg[u], in_=PS[bg][:, :], func=AF.Sigmoid))(bg, u),
                          reads=[("ps", bg)], writes=[("sg%d" % u,)])
                    P.add("dve", (lambda by, u, j: lambda e: e.scalar_tensor_tensor(out=tt_[u], in0=PS[by][:, :], scalar=psc[:, l, j:j + 1],
                                                                                    in1=sg[u], op0=ALU.mult, op1=ALU.mult))(by, u, j),
                          reads=[("ps", by), ("sg%d" % u,), ("vec",)], writes=[("tt%d" % u,)])
                    P.add("pool", (lambda u: lambda e: e.tensor_tensor(out=mb[u], in0=mt[u], in1=tt_[u], op=ALU.add))(u),
                          reads=[("mt%d" % u,), ("tt%d" % u,)], writes=[("mb%d" % u,)])
                    P.add("sp", (lambda u, j, c: lambda e: e.dma_start(out=mT[j * 128:(j + 1) * 128, c * CH:(c + 1) * CH], in_=mb[u]))(u, j, c),
                          reads=[("mb%d" % u,)], writes=[("mT", j, c)], dma="mb%d" % u)

            chk('P3b')
            mc = [V("mc%d" % i, OFF_AT + 49152 + i * 8192, 8192, pat="p (k n) -> p k n", n=CH) for i in range(2)]
            assert OFF_AT + 49152 + 16384 <= OFF_D
            P.recycle(["xc0", "xc1", "sqx0", "sqx1", "mc0", "mc1"])
            def p3b_tail(c):
                i = c % 2
                xk = [("xc%d" % i, o) for o in range(KC)]
                norm_b(l, c, xc[i], xk, sqx[i], ("sqx%d" % i,), g2, sd2[i], ("sd2%d" % i,))

            def p3b_load(c):
                i = c % 2
                P.add("sp", (lambda c, i: lambda e: e.dma_start(out=xc[i], in_=src[c]))(c, i),
                      reads=[(srckey, c)], writes=[("xc%d" % i, o) for o in range(KC)], dma="xc%d" % i)
                P.add("sp", (lambda c, i: lambda e: e.dma_start(
                    out=mc[i], in_=mT[:, c * CH:(c + 1) * CH].rearrange("(k p) n -> p k n", p=128)))(c, i),
                    reads=[("mT", j, c) for j in range(8)], writes=[("mc%d" % i,)], dma="mc%d" % i)

            p3b_load(0)
            for c in range(NCH):
                i = c % 2
                xk = [("xc%d" % i, o) for o in range(KC)]
                for o in range(8):
                    b = bank()
                    gemm(PS[b][:, :], b, [(wo[:, k, o * 128:(o + 1) * 128], mc[i][:, k, :]) for k in range(KC)],
                         [("wo",), ("mc%d" % i,)])
                    P.add("dve", (lambda b, i, o: lambda e: e.tensor_tensor(out=xc[i][:, o, :], in0=xc[i][:, o, :], in1=PS[b][:, :],
                                                                            op=ALU.add))(b, i, o),
                          reads=[("ps", b), ("xc%d" % i, o)], writes=[("xc%d" % i, o)])
                    if o == 1:
                        if c >= 1:
                            p3b_tail(c - 1)
                        if c + 1 < NCH:
                            p3b_load(c + 1)
                P.add("sp", (lambda c, i: lambda e: e.dma_start(out=xs[c], in_=xc[i]))(c, i),
                      reads=xk, writes=[("xs", c)], dma="xst%d" % i)
                norm_a(xc[i], xk, sqx[i], ("sqx%d" % i,))
            p3b_tail(NCH - 1)

            chk('P4')
            HS = S // 2
            uT = V("uT", OFF_AT, NHB * HS * 2, pat="p (k n) -> p k n", n=HS)
            F0 = OFF_AT + NHB * HS * 2
            wgu = [V("wgu%d" % i, F0 + i * 4096, 4096, pat="p (k a n) -> p k a n", a=2, n=128) for i in range(2)]
            wfo = [V("wfo%d" % i, F0 + 8192 + i * 5632, 5632, pat="p (k n) -> p k n", n=128) for i in range(2)]
            xo = [V("xo%d" % i, F0 + 19456 + i * 8192, 8192, dt=F32) for i in range(2)]
            sgl = [V("sgl%d" % i, F0 + 35840 + i * 2048, 2048, dt=F32) for i in range(2)]
            assert F0 + 39936 <= AR_BYTES
            P.recycle(["uT", "wgu0", "wgu1", "wfo0", "wfo1", "xo0", "xo1", "sgl0", "sgl1"])
            wcnt = [0]
            ocn = [0]
            for half in range(2):
                for hbk in range(NHB):
                    i = wcnt[0] % 2
                    wcnt[0] += 1
                    wdma(wgu[i][:, :, 0, :], w_fi[l, :, hbk * 128:(hbk + 1) * 128].rearrange("(k p) n -> p k n", p=128),
                         ("wgu%d" % i, 0), "wgu%d_0" % i)
                    wdma(wgu[i][:, :, 1, :], w_fi[l, :, D_FF + hbk * 128:D_FF + (hbk + 1) * 128].rearrange("(k p) n -> p k n", p=128),
                         ("wgu%d" % i, 1), "wgu%d_1" % i)
                    for cc in range(4):
                        c = half * 4 + cc
                        cs = slice(c * CH, (c + 1) * CH)
                        u = (hbk * 4 + cc) % 2
                        bg, bu = bank(), bank()
                        gemm(PS[bg][:, :], bg, [(wgu[i][:, k, 0, :], hT[:, k, cs]) for k in range(KC)], [("wgu%d" % i, 0)] + hk(c))
                        gemm(PS[bu][:, :], bu, [(wgu[i][:, k, 1, :], hT[:, k, cs]) for k in range(KC)], [("wgu%d" % i, 1)] + hk(c))
                        P.add("act", (lambda bg, u: lambda e: e.activation(out=sgl[u], in_=PS[bg][:, :], func=AF.Sigmoid))(bg, u),
                              reads=[("ps", bg)], writes=[("sgl%d" % u,)])
                        P.add("dve", (lambda bg, u: lambda e: e.tensor_tensor(out=sgl[u], in0=sgl[u], in1=PS[bg][:, :], op=ALU.mult))(bg, u),
                              reads=[("ps", bg), ("sgl%d" % u,)], writes=[("sgl%d" % u,)])
                        P.add("dve", (lambda bu, u, hbk, cc: lambda e: e.tensor_tensor(out=uT[:, hbk, cc * CH:(cc + 1) * CH], in0=sgl[u],
                                                                                       in1=PS[bu][:, :], op=ALU.mult))(bu, u, hbk, cc),
                              reads=[("ps", bu), ("sgl%d" % u,)], writes=[("uT", hbk, cc)])
                for o in range(8):
                    i = ocn[0] % 2
                    ocn[0] += 1
                    wdma(wfo[i], w_fo[l, :, o * 128:(o + 1) * 128].rearrange("(k p) n -> p k n", p=128), ("wfo%d" % i,), "wfo%d" % i)
                    P.add("sp", (lambda o, i, half: lambda e: e.dma_start(
                        out=xo[i].rearrange("p (c n) -> p c n", n=CH),
                        in_=xs[half * 4:(half + 1) * 4, :, o, :].rearrange("c p n -> p c n")))(o, i, half),
                        reads=[("xs", half * 4 + cc) for cc in range(4)], writes=[("xo%d" % i, cc) for cc in range(4)], dma="xo%d" % i)
                    for cc in range(4):
                        b = bank()
                        gemm(PS[b][:, :], b, [(wfo[i][:, k, :], uT[:, k, cc * CH:(cc + 1) * CH]) for k in range(NHB)],
                             [("wfo%d" % i,)] + [("uT", k, cc) for k in range(NHB)])
                        P.add("dve", (lambda b, i, cc: lambda e: e.tensor_tensor(out=xo[i][:, cc * CH:(cc + 1) * CH],
                                                                                 in0=xo[i][:, cc * CH:(cc + 1) * CH], in1=PS[b][:, :],
                                                                                 op=ALU.add))(b, i, cc),
                              reads=[("ps", b), ("xo%d" % i, cc)], writes=[("xo%d" % i, cc)])
                    P.add("sp", (lambda o, i, half: lambda e: e.dma_start(
                        out=dst[half * 4:(half + 1) * 4, :, o, :].rearrange("c p n -> p c n"),
                        in_=xo[i].rearrange("p (c n) -> p c n", n=CH)))(o, i, half),
                        reads=[("xo%d" % i, cc) for cc in range(4)], writes=[(dstkey + "_o", o, half)], dma="xot%d" % i)
                P.marker(reads=[(dstkey + "_o", o, half) for o in range(8)],
                         writes=[(dstkey, half * 4 + cc) for cc in range(4)])
        try:
            for l_ in range(depth):
                layer(l_)
        except _Stop:
            pass
        P.add("sp", None, reads=[("yT_o", o, half) for o in range(8) for half in range(2)] + [(k,) for k in ("d_hT", "d_aT", "d_cbz", "d_dT", "d_c", "d_v")])
        P.emit(nc, st)
    return nc, P


def _consts():
    c = np.zeros((128, NCST), np.float32)
    c[:, 0:128] = np.eye(128, dtype=np.float32)
    s = np.arange(128)[:, None]
    t = np.arange(128)[None, :]
    c[:, 128:256] = np.where(s <= t, 0.0, -30000.0)
    tt = np.arange(16, dtype=np.float32)
    for blk, (wa, wb) in enumerate(((2, 4), (8, 16))):
        c[0:64, 256 + blk * 16:272 + blk * 16] = 1.0 / np.minimum(tt + 1.0, wa)
        c[64:128, 256 + blk * 16:272 + blk * 16] = 1.0 / np.minimum(tt + 1.0, wb)
        c[0:64, 288 + blk] = 1.0 / wa
        c[64:128, 288 + blk] = 1.0 / wb
    return c


def _vecs(norm_mix_g, norm_ffn_g, pool_scale, conv_w, q_norm_g, k_norm_g, forget_b):
    v = np.zeros((128, NVEC), np.float32)
    f = lambda a: np.asarray(a, np.float32).reshape(4, 8, 128).transpose(2, 0, 1).reshape(128, 32)
    v[:, 0:32] = f(norm_mix_g)
    v[:, 32:64] = f(norm_ffn_g)
    v[:, 64:96] = f(pool_scale)
    v[:, 96:120] = np.asarray(conv_w, np.float32).reshape(4, 3, 2, 128).transpose(3, 0, 1, 2).reshape(128, 24)
    v[0:64, 120:124] = np.asarray(q_norm_g, np.float32).T
    v[64:128, 120:124] = np.asarray(q_norm_g, np.float32).T
    v[0:64, 124:128] = np.asarray(k_norm_g, np.float32).T
    v[64:128, 124:128] = np.asarray(k_norm_g, np.float32).T
    v[0:8, 128:132] = np.asarray(forget_b, np.float32).T
    return v


_NC_CACHE = {}


def make_in_maps(inputs, cores):
    x = np.asarray(inputs["x"], np.float32)
    shared = {
        "w_in": np.ascontiguousarray(inputs["w_in"], np.float32),
        "w_attn_out": np.ascontiguousarray(inputs["w_attn_out"], np.float32),
        "w_conv_out": np.ascontiguousarray(inputs["w_conv_out"], np.float32),
        "pool_w": np.ascontiguousarray(inputs["pool_w"], np.float32),
        "w_o": np.ascontiguousarray(inputs["w_o"], np.float32),
        "w_ffn_in": np.ascontiguousarray(inputs["w_ffn_in"], np.float32),
        "w_ffn_out": np.ascontiguousarray(inputs["w_ffn_out"], np.float32),
        "vecs": _vecs(inputs["norm_mix_g"], inputs["norm_ffn_g"], inputs["pool_scale"], inputs["conv_w"],
                      inputs["q_norm_g"], inputs["k_norm_g"], inputs["forget_b"]),
        "cst": _consts(),
    }
    maps = []
    for b in cores:
        m = dict(shared)
        m["xT"] = np.ascontiguousarray(x[b].reshape(NCH, CH, KC, 128).transpose(0, 3, 2, 1))
        maps.append(m)
    return maps


def kernel(**inputs):
    inputs = {k: np.asarray(v) for k, v in inputs.items()}
    if "nc" not in _NC_CACHE:
        _NC_CACHE["nc"] = build(4)[0]
    nc = _NC_CACHE["nc"]
    cores = list(range(8))
    res = run_bass_kernel_spmd(nc, make_in_maps(inputs, cores), core_ids=cores)
    out = np.stack([np.ascontiguousarray(r["yT"].transpose(0, 3, 2, 1)).reshape(S, D) for r in res.results], axis=0)
    return out.astype(np.float32)
```

```python
import contextlib
import numpy as np
import concourse.bass as bass
import concourse.mybir as mybir
from concourse.bass_utils import run_bass_kernel_spmd

F32 = mybir.dt.float32
BF16 = mybir.dt.bfloat16
AF = mybir.ActivationFunctionType
ALU = mybir.AluOpType

D = 1024
S = 4096
NCH = 8
CH = 512
KC = 8
D_IN = 5640
D_FF = 2816
NHB = 22
EPS = 1e-6
ENGS = ("pe", "act", "dve", "pool", "sp")

OQ, OK_, OV, OF, OCX, OCB, OCC, OPX, OG = 0, 512, 1024, 1536, 1544, 1800, 2056, 2312, 2568


class Op:
    __slots__ = ("eng", "fn", "deps", "is_dma", "slot", "epoch", "signal",
                 "ordinal", "cum", "ndma", "pos", "marker")


class Prog:
    def __init__(self):
        self.streams = {e: [] for e in ENGS}
        self.lastw = {}
        self.readers = {}
        self.epoch = 0
        self.slot_cum = {}
        self.regions = {}
        self.bufkeys = {}
        self.buf_fence = {}

    @staticmethod
    def _compress(deps):
        best_c = {}
        best_d = {}
        for w in deps:
            if w.is_dma:
                k = (w.slot, w.epoch)
                o = best_d.get(k)
                if o is None or w.cum > o.cum:
                    best_d[k] = w
            else:
                o = best_c.get(w.eng)
                if o is None or w.pos > o.pos:
                    best_c[w.eng] = w
        return list(best_c.values()) + list(best_d.values())

    def region(self, name, lo, hi):
        self.regions[name] = (lo, hi)

    def recycle(self, new_names):
        olds = set()
        for n in new_names:
            lo, hi = self.regions[n]
            for m, (l2, h2) in self.regions.items():
                if l2 < hi and lo < h2:
                    olds.add(m)
        deps = {}
        for n in olds:
            for k in self.bufkeys.get(n, ()):
                w = self.lastw.pop(k, None)
                if w is not None:
                    deps[id(w)] = w
                for r in self.readers.pop(k, ()):
                    deps[id(r)] = r
            self.bufkeys[n] = set()
            m = self.buf_fence.pop(n, None)
            if m is not None:
                for w in m.deps:
                    deps[id(w)] = w
        M = Op()
        M.marker = True
        M.deps = self._compress(deps.values())
        for n in new_names:
            self.buf_fence[n] = M

    def marker(self, reads=(), writes=()):
        deps = {}

        def put(w):
            if w.marker:
                for x in w.deps:
                    deps[id(x)] = x
            else:
                deps[id(w)] = w
        for k in reads:
            w = self.lastw.get(k)
            if w is not None:
                put(w)
        for k in writes:
            w = self.lastw.get(k)
            if w is not None:
                put(w)
            for r in self.readers.get(k, ()):
                put(r)
        M = Op()
        M.marker = True
        M.deps = self._compress(deps.values())
        for k in writes:
            self.lastw[k] = M
            self.readers[k] = []
        return M

    def add(self, eng, fn, reads=(), writes=(), dma=None, ndma=1):
        op = Op()
        op.marker = False
        op.eng = eng
        op.fn = fn
        op.is_dma = dma is not None
        op.slot = dma
        op.epoch = 0 if dma is not None else self.epoch
        op.signal = op.is_dma
        op.ordinal = 0
        op.ndma = ndma
        op.pos = len(self.streams[eng])
        deps = {}

        def put(w, raw):
            if w.marker:
                for x in w.deps:
                    if id(x) not in deps:
                        deps[id(x)] = (x, True)
            else:
                o = deps.get(id(w))
                if o is None or (raw and not o[1]):
                    deps[id(w)] = (w, raw)

        for k in reads:
            w = self.lastw.get(k)
            if w is None:
                w = self.buf_fence.get(k[0])
            if w is not None:
                put(w, True)
        for k in writes:
            w = self.lastw.get(k)
            if w is None:
                w = self.buf_fence.get(k[0])
            if w is not None:
                put(w, False)
            for r in self.readers.get(k, ()):
                put(r, False)
        final = []
        for w, raw in deps.values():
            if w is op:
                continue
            if (not w.is_dma) and (not op.is_dma) and w.eng == eng == "pe":
                continue
            final.append(w)
        final = self._compress(final)
        for w in final:
            w.signal = True
        op.deps = final
        if op.is_dma:
            key = (dma, 0)
            c = self.slot_cum.get(key, 0) + ndma
            self.slot_cum[key] = c
            op.cum = c
        for k in reads:
            self.readers.setdefault(k, []).append(op)
            self.bufkeys.setdefault(k[0], set()).add(k)
        for k in writes:
            self.lastw[k] = op
            self.readers[k] = []
            self.bufkeys.setdefault(k[0], set()).add(k)
        self.streams[eng].append(op)
        return op

    def emit(self, nc, stack):
        for e in ENGS:
            cnt = {}
            for op in self.streams[e]:
                if op.is_dma or not op.signal:
                    continue
                c = cnt.get(op.epoch, 0) + 1
                cnt[op.epoch] = c
                op.ordinal = c
        sems = {}

        def sem_for(key):
            if key not in sems:
                sems[key] = stack.enter_context(nc.semaphore("s%d" % len(sems)))
            return sems[key]

        for e in ENGS:
            for op in self.streams[e]:
                if op.is_dma:
                    sem_for(("d", op.slot, op.epoch))
                elif op.signal:
                    sem_for(("e", op.eng, op.epoch))
        self.nsems = len(sems)
        block = stack.enter_context(nc.Block())
        streams = self.streams

        def run(eng_name):
            def body(engine):
                seen = {}
                for op in streams[eng_name]:
                    need = {}
                    for d in op.deps:
                        if d.is_dma:
                            k = ("d", d.slot, d.epoch)
                            v = 16 * d.cum
                        else:
                            k = ("e", d.eng, d.epoch)
                            v = d.ordinal
                        if v > need.get(k, 0):
                            need[k] = v
                    for k, v in need.items():
                        if v > seen.get(k, 0):
                            engine.wait_ge(sems[k], v)
                            seen[k] = v
                    if op.fn is None:
                        continue
                    r = op.fn(engine)
                    if op.is_dma:
                        s = sems[("d", op.slot, op.epoch)]
                        if not isinstance(r, (list, tuple)):
                            r = [r]
                        assert len(r) == op.ndma
                        for inst in r:
                            inst.then_inc(s, 16)
                    elif op.signal:
                        if isinstance(r, (list, tuple)):
                            r = r[-1]
                        r.then_inc(sems[("e", op.eng, op.epoch)], 1)
            return body

        block.tensor(run("pe"))
        block.scalar(run("act"))
        block.vector(run("dve"))
        block.gpsimd(run("pool"))
        block.sync(run("sp"))


OFF_HT = 0
OFF_AT = 65536
OFF_C = 98304
OFF_D = 131072
OFF_E = 164352
AR_BYTES = 201728
NCST = 290
USE_PAIR = True
NVEC = 132


class _Stop(Exception):
    pass


def build(depth=4, dbg=(), stop=None):
    nc = bass.Bass("TRN2", target_bir_lowering=False)
    dram = lambda name, shape, dt=F32, kind="ExternalInput": nc.dram_tensor(name, shape, dt, kind=kind).ap()
    xT = dram("xT", [NCH, 128, KC, CH])
    w_in = dram("w_in", [4, D, D_IN])
    w_ao = dram("w_attn_out", [4, 512, D])
    w_co = dram("w_conv_out", [4, 256, D])
    w_po = dram("pool_w", [4, 4, 64, 256])
    w_o = dram("w_o", [4, D, D])
    w_fi = dram("w_ffn_in", [4, D, 2 * D_FF])
    w_fo = dram("w_ffn_out", [4, D_FF, D])
    vecs_d = dram("vecs", [128, NVEC])
    cst_d = dram("cst", [128, NCST])
    yT = dram("yT", [NCH, 128, KC, CH], kind="ExternalOutput")
    xs = dram("xs_scr", [NCH, 128, KC, CH], kind="Internal")
    mT = dram("mT_scr", [D, S], BF16, kind="Internal")
    cpd = dram("cp_scr", [8, 4, S], BF16, kind="Internal")
    dbg_out = {}
    for nm, shape, dt in (("d_hT", [D, S], BF16), ("d_aT", [512, S], BF16), ("d_cbz", [256, S], BF16),
                          ("d_dT", [256, S], BF16), ("d_c", [8, S], F32), ("d_qa", [68, S], BF16),
                          ("d_ka", [68, S], BF16), ("d_v", [128, 32 * 8 * 65], BF16)):
        if nm in dbg:
            dbg_out[nm] = dram(nm, shape, dt, kind="ExternalOutput")

    P = Prog()
    st = contextlib.ExitStack()
    with st:
        sbt = lambda name, shape, dt: st.enter_context(nc.sbuf_tensor(name, shape, dt))
        arena = sbt("arena", [128, AR_BYTES // 2], BF16)
        cst = sbt("cstf", [128, NCST], F32)
        vec = sbt("vecf", [128, NVEC], F32)
        ident = sbt("ident", [128, 128], BF16)
        maskb = sbt("maskb", [128, 128], BF16)
        ones = sbt("onesb", [128, 128], BF16)
        sel = sbt("sel", [65, 64], BF16)
        gk8 = sbt("gk8", [128, 4], F32)
        nfb = sbt("nfb", [8, 4], F32)
        onef = sbt("onef", [8, 1], F32)
        bd = sbt("bd", [128, 128], BF16)
        PS = [st.enter_context(nc.psum_tensor("ps%d" % i, [128, 512], F32)) for i in range(8)]

        def V(name, off, nbytes, p0=0, p1=128, dt=BF16, pat=None, **kw):
            P.region(name, off, off + nbytes)
            ap = arena[p0:p1, off // 2:(off + nbytes) // 2]
            if dt == F32:
                ap = ap.bitcast(F32)
            if pat:
                ap = ap.rearrange(pat, **kw)
            return ap

        g1 = vec[:, 0:32].rearrange("p (l k) -> p l k", k=8)
        g2 = vec[:, 32:64].rearrange("p (l k) -> p l k", k=8)
        psc = vec[:, 64:96].rearrange("p (l k) -> p l k", k=8)
        cw = vec[:, 96:120].rearrange("p (l j b) -> p l j b", j=3, b=2)
        gq2 = vec[:, 120:124]
        gk = vec[:, 124:128]
        fb = vec[0:8, 128:132]
        invtab = [cst[:, 256:272], cst[:, 272:288]]
        invw = cst[:, 288:290]

        hT = V("hT", OFF_HT, 65536, pat="p (k n) -> p k n", n=S)
        aT = V("aT", OFF_AT, 32768, pat="p (k n) -> p k n", n=S)
        QA = [V("QA%d" % i, OFF_C + i * 8192, 8192) for i in range(2)]
        KA = [V("KA%d" % i, OFF_C + 16384 + i * 8192, 8192) for i in range(2)]
        VA = V("VA", OFF_D, 33280, pat="p (t h d) -> p t h d", h=8, d=65)

        pscnt = [0]

        def bank():
            b = pscnt[0] % 8
            pscnt[0] += 1
            return b

        def mm(ps_ap, pskey, lhsT, rhs, start, stop, reads):
            P.add("pe", lambda e: e.matmul(ps_ap, lhsT=lhsT, rhs=rhs, start=start, stop=stop),
                  reads=reads, writes=[pskey])

        def gemm(ps_ap, b, pairs, reads):
            n = len(pairs)
            for i, (l, r) in enumerate(pairs):
                mm(ps_ap, ("ps", b), l, r, i == 0, i == n - 1, reads)

        def wdma(out_ap, in_ap, key, slot, n=1):
            P.add("pool", lambda e: e.dma_start(out=out_ap, in_=in_ap), writes=[key], dma=slot)

        P.add("sp", lambda e: e.dma_start(out=cst[:], in_=cst_d), writes=[("cst",)], dma="cst")
        P.add("sp", lambda e: e.dma_start(out=vec[:], in_=vecs_d), writes=[("vec",)], dma="vec")
        P.add("dve", lambda e: e.tensor_copy(out=ident[:], in_=cst[:, 0:128]), reads=[("cst",)], writes=[("ident",)])
        P.add("dve", lambda e: e.tensor_copy(out=maskb[:], in_=cst[:, 128:256]), reads=[("cst",)], writes=[("maskb",)])
        P.add("pool", lambda e: e.memset(ones[:], 1.0), writes=[("ones",)])
        P.add("pool", lambda e: e.memset(sel[:], 0.0), writes=[("sel",)])
        P.add("pool", lambda e: e.memset(sel[64:65, :], 1.0), writes=[("sel",)])
        P.add("pool", lambda e: e.memset(onef[:], 1.0), writes=[("onef",)])
        P.add("pool", lambda e: e.memset(bd[:], 0.0), writes=[("bd",)])
        P.add("pool", lambda e: e.memset(bd[0:64, 0:64], 1.0), writes=[("bd",)])
        P.add("pool", lambda e: e.memset(bd[64:128, 64:128], 1.0), writes=[("bd",)])
        P.add("dve", lambda e: e.tensor_scalar(out=gk8[:], in0=gk, scalar1=0.125, scalar2=None, op0=ALU.mult),
              reads=[("vec",)], writes=[("gk8",)])
        P.add("dve", lambda e: e.tensor_scalar(out=nfb[:], in0=fb, scalar1=-1.0, scalar2=None, op0=ALU.mult),
              reads=[("vec",)], writes=[("nfb",)])

        def norm_a(xc_ap, xckeys, sq_ap, sqkey):
            P.add("act", lambda e: e.activation(out=sq_ap, in_=xc_ap, func=AF.Square),
                  reads=xckeys, writes=[sqkey])

        def norm_b(l, c, xc_ap, xckeys, sq_ap, sqkey, gvec, sd_ap, sdkey):
            b = bank()
            gemm(PS[b][:, :], b, [(ones[:, :], sq_ap[:, k, :]) for k in range(KC)], [sqkey, ("ones",)])
            P.add("act", lambda e: e.activation(out=sd_ap, in_=PS[b][:, :], func=AF.Ln, scale=1.0 / D, bias=EPS),
                  reads=[("ps", b)], writes=[sdkey])
            P.add("act", lambda e: e.activation(out=sd_ap, in_=sd_ap, func=AF.Exp, scale=-0.5),
                  reads=[sdkey], writes=[sdkey])
            for k in range(KC):
                P.add("dve", (lambda k: lambda e: e.scalar_tensor_tensor(
                    out=hT[:, k, c * CH:(c + 1) * CH], in0=xc_ap[:, k, :], scalar=gvec[:, l, k:k + 1],
                    in1=sd_ap, op0=ALU.mult, op1=ALU.mult))(k),
                    reads=[xckeys[k], sdkey, ("vec",)], writes=[("hT", c, k)])

        def hk(c):
            return [("hT", c, k) for k in range(KC)]

        def chk(name):
            if stop == name:
                raise _Stop()

        def layer(l):
            P.epoch = l
            src = xT if l == 0 else xs
            srckey = "xT" if l == 0 else "xs"
            last = (l == depth - 1)
            dst = yT if last else xs
            dstkey = "yT" if last else "xs"

            xc = [V("xc%d" % i, OFF_AT + i * 16384, 16384, dt=F32, pat="p (k n) -> p k n", n=CH) for i in range(2)]
            sqx = [V("sqx%d" % i, OFF_AT + 32768 + i * 8192, 8192, pat="p (k n) -> p k n", n=CH) for i in range(2)]
            sdt = [V("sdt%d" % i, OFF_AT + 49152 + i * 2048, 2048, dt=F32) for i in range(2)]
            rst = [V("rst%d" % i, OFF_AT + 53248 + i * 2048, 2048, dt=F32) for i in range(2)]
            P.recycle(["xc0", "xc1", "sqx0", "sqx1", "sdt0", "sdt1", "rst0", "rst1"])
            for c in range(NCH):
                i = c % 2
                P.add("sp", (lambda c, i: lambda e: e.dma_start(out=xc[i], in_=src[c]))(c, i),
                    reads=[(srckey, c)], writes=[("xc%d" % i, o) for o in range(KC)], dma="xc%d" % i)
                xk = [("xc%d" % i, o) for o in range(KC)]
                norm_a(xc[i], xk, sqx[i], ("sqx%d" % i,))
                norm_b(l, c, xc[i], xk, sqx[i], ("sqx%d" % i,), g1, sdt[i], ("sdt%d" % i,))
            if l == 0 and "d_hT" in dbg_out:
                P.add("sp", lambda e: e.dma_start(out=dbg_out["d_hT"].rearrange("(k p) n -> p k n", p=128), in_=hT),
                      reads=[k_ for c in range(NCH) for k_ in hk(c)], writes=[("d_hT",)], dma="dbg")

            chk('P2a')
            fE = V("fE", OFF_C, 16384, p0=0, p1=8, dt=F32)
            fR = V("fR", OFF_C + 16384, 16384, p0=0, p1=8, dt=F32)
            cp4 = V("cp4", OFF_AT, 32768, p0=0, p1=8, pat="p (j n) -> p j n", n=S)
            wf = V("wf", OFF_E, 128, pat="p (k n) -> p k n", n=8)
            wv = V("wv", OFF_E + 128, 8192, pat="p (k n) -> p k n", n=512)
            P.recycle(["fE", "fR", "cp4", "wf", "wv"])
            wdma(wf, w_in[l, :, OF:OF + 8].rearrange("(k p) n -> p k n", p=128), ("wf",), "wf")
            wdma(wv, w_in[l, :, OV:OV + 512].rearrange("(k p) n -> p k n", p=128), ("wv",), "wv")
            for c in range(NCH):
                b = bank()
                gemm(PS[b][0:8, :], b, [(wf[:, k, :], hT[:, k, c * CH:(c + 1) * CH]) for k in range(KC)],
                     [("wf",)] + hk(c))
                P.add("act", (lambda c, b: lambda e: e.activation(
                    out=fE[:, c * CH:(c + 1) * CH], in_=PS[b][0:8, :], func=AF.Exp, scale=-1.0, bias=nfb[:, l:l + 1]))(c, b),
                    reads=[("ps", b), ("nfb",)], writes=[("fE", c)])
            P.add("act", lambda e: e.activation(out=fE, in_=fE, func=AF.Ln, scale=1.0, bias=1.0),
                  reads=[("fE", c) for c in range(NCH)], writes=[("fE", "ln")])
            P.add("dve", lambda e: e.tensor_tensor_scan(out=fR, data0=onef[:, 0:1].to_broadcast([8, S]), data1=fE,
                                                        initial=0.0, op0=ALU.mult, op1=ALU.subtract),
                  reads=[("fE", "ln"), ("onef",)], writes=[("fR",)])
            if l == 0 and "d_c" in dbg_out:
                P.add("sp", lambda e: e.dma_start(out=dbg_out["d_c"], in_=fR), reads=[("fR",)], writes=[("d_c",)], dma="dbg")
            P.add("dve", lambda e: e.tensor_copy(out=cp4[:, 0, :], in_=fR), reads=[("fR",)], writes=[("cp4", 0)])
            P.add("dve", lambda e: e.tensor_tensor(out=fE, in0=fR, in1=cp4[:, 0, :], op=ALU.subtract),
                  reads=[("fR",), ("cp4", 0)], writes=[("fE", "res")])
            P.add("dve", lambda e: e.tensor_copy(out=cp4[:, 1, :], in_=fE), reads=[("fE", "res")], writes=[("cp4", 1)])
            P.add("dve", lambda e: e.tensor_scalar(out=cp4[:, 2:4, :], in0=cp4[:, 0:2, :], scalar1=-1.0, scalar2=None,
                                                   op0=ALU.mult),
                  reads=[("cp4", 0), ("cp4", 1)], writes=[("cp4", 2)])
            P.add("sp", lambda e: e.dma_start(out=cpd, in_=cp4), reads=[("cp4", 0), ("cp4", 1), ("cp4", 2)],
                  writes=[("cpd",)], dma="cpd")

            chk('P2b')
            P.recycle(["VA"])
            P.add("pool", lambda e: e.memset(VA[:, :, :, 64:65], 1.0), writes=[("VA", "ones")])
            for tt in range(32):
                b = bank()
                gemm(PS[b][:, :], b, [(hT[:, k, tt * 128:(tt + 1) * 128], wv[:, k, :]) for k in range(KC)],
                     [("wv",)] + hk(tt // 4))
                P.add("dve", (lambda tt, b: lambda e: e.tensor_copy(
                    out=VA[:, tt, :, 0:64], in_=PS[b][:, :].rearrange("p (h d) -> p h d", d=64)))(tt, b),
                    reads=[("ps", b)], writes=[("VA", tt)])
            if l == 0 and "d_v" in dbg_out:
                P.add("sp", lambda e: e.dma_start(out=dbg_out["d_v"], in_=VA.rearrange("p t h d -> p (t h d)")),
                      reads=[("VA", tt) for tt in range(32)] + [("VA", "ones")], writes=[("d_v",)], dma="dbg")

            chk('P2c')
            E0 = OFF_E + 8320
            PT = [V("PT%d" % i, E0 + i * 1024, 1024) for i in range(4)]
            wqk = [V("wqk%d" % i, E0 + 4096 + i * 4096, 4096, pat="p (a k n) -> p a k n", a=2, n=128) for i in range(2)]
            sqh = [V("sqh%d" % i, E0 + 12288 + i * 1024, 1024) for i in range(2)]
            lnh = [V("lnh%d" % i, E0 + 14336 + i * 2048, 2048, dt=F32) for i in range(2)]
            recf = V("recf", E0 + 18432, 2048, p0=0, p1=65, dt=F32)
            rdh = V("rdh", E0 + 20480, 1024, p0=0, p1=65)
            rdl = V("rdl", E0 + 21504, 1024, p0=0, p1=65)
            bcs = [V("bcs%d" % i, E0 + 22528 + i * 2048, 2048, p0=0, p1=64, dt=F32) for i in range(2)]
            atm = [V("atm%d" % i, E0 + 26624 + i * 1024, 1024, p0=0, p1=64) for i in range(2)]
            qtm = [V("qtm%d" % i, E0 + 26624 + i * 1024, 1024, p0=64, p1=128) for i in range(2)]
            assert E0 + 28672 <= AR_BYTES
            P.recycle(["PT%d" % i for i in range(4)] + ["wqk0", "wqk1", "sqh0", "sqh1", "lnh0", "lnh1",
                                                       "recf", "rdh", "rdl", "bcs0", "bcs1", "atm0", "atm1", "qtm0", "qtm1"])
            P.recycle(["QA0", "QA1", "KA0", "KA1", "aT"])
            P.add("pool", lambda e: e.memset(rdh[:, :], 0.0), writes=[("rdh",)])
            P.add("pool", lambda e: e.memset(rdl[:, :], 0.0), writes=[("rdl",)])
            if USE_PAIR:
                P.add("pool", lambda e: e.memset(QA[0][64:68, :], 1.0), writes=[("QA0", "aug")])
                P.add("pool", lambda e: e.memset(KA[0][64:68, :], 1.0), writes=[("KA0", "aug")])
                P.add("pool", lambda e: e.memset(QA[1][0:64, :], 0.0), writes=[("QA1", "aug")])
                P.add("pool", lambda e: e.memset(KA[1][0:64, :], 0.0), writes=[("KA1", "aug")])
                P.add("pool", lambda e: e.memset(QA[1][32:34, :], 1.0), writes=[("QA1", "aug")])
                P.add("pool", lambda e: e.memset(KA[1][0:2, :], 1.0), writes=[("KA1", "aug")])
            else:
                for i in range(2):
                    P.add("pool", (lambda i: lambda e: e.memset(QA[i][64:68, :], 1.0))(i), writes=[("QA%d" % i, "aug")])
                    P.add("pool", (lambda i: lambda e: e.memset(KA[i][64:68, :], 1.0))(i), writes=[("KA%d" % i, "aug")])

            ucnt = [0]

            def load_wqk(p):
                pb_ = p % 2
                wdma(wqk[pb_][:, 0], w_in[l, :, OQ + p * 128:OQ + (p + 1) * 128].rearrange("(k p) n -> p k n", p=128),
                     ("wqk%d" % pb_, 0), "wq%d" % pb_)
                wdma(wqk[pb_][:, 1], w_in[l, :, OK_ + p * 128:OK_ + (p + 1) * 128].rearrange("(k p) n -> p k n", p=128),
                     ("wqk%d" % pb_, 1), "wk%d" % pb_)

            def proj_pair(p):
                pb_ = p % 2
                for hh, (qr, kr) in enumerate(((64, 66), (0, 32))):
                    h = 2 * p + hh
                    P.add("sp", (lambda hh, h, qr: lambda e: e.dma_start(out=QA[hh][qr:qr + 2, :], in_=cpd[h, 0:2, :]))(hh, h, qr),
                          reads=[("cpd",)], writes=[("QA%d" % hh, "aug")], dma="qaug%d" % hh)
                    P.add("sp", (lambda hh, h, kr: lambda e: e.dma_start(out=KA[hh][kr:kr + 2, :], in_=cpd[h, 2:4, :]))(hh, h, kr),
                          reads=[("cpd",)], writes=[("KA%d" % hh, "aug")], dma="kaug%d" % hh)
                units = [(c, a) for c in range(NCH) for a in range(2)]
                st_ = {}

                def stage_a(ui):
                    c, a = units[ui]
                    u = ucnt[0] % 2
                    ucnt[0] += 1
                    b = bank()
                    st_[ui] = (u, b)
                    gemm(PS[b][:, :], b, [(wqk[pb_][:, a, k, :], hT[:, k, c * CH:(c + 1) * CH]) for k in range(KC)],
                         [("wqk%d" % pb_, a)] + hk(c))
                    P.add("act", lambda e: e.activation(out=sqh[u], in_=PS[b][:, :], func=AF.Square),
                          reads=[("ps", b)], writes=[("sqh%d" % u,)])

                def stage_b(ui):
                    c, a = units[ui]
                    u, b = st_[ui]
                    b2 = bank()
                    mm(PS[b2][:, :], ("ps", b2), bd[:, :], sqh[u], True, True, [("sqh%d" % u,), ("bd",)])
                    P.add("act", lambda e: e.activation(out=lnh[u], in_=PS[b2][:, :], func=AF.Ln, scale=1.0 / 64, bias=EPS),
                          reads=[("ps", b2)], writes=[("lnh%d" % u,)])
                    P.add("act", lambda e: e.activation(out=lnh[u], in_=lnh[u], func=AF.Exp, scale=-0.5),
                          reads=[("lnh%d" % u,)], writes=[("lnh%d" % u,)])
                    dt_ = (QA, KA)[a]
                    dn = ("QA", "KA")[a]
                    gv = (gq2, gk8)[a]
                    P.add("dve", lambda e: e.scalar_tensor_tensor(
                        out=dt_[0][0:64, c * CH:(c + 1) * CH], in0=PS[b][0:64, :], scalar=gv[0:64, l:l + 1],
                        in1=lnh[u][0:64, :], op0=ALU.mult, op1=ALU.mult),
                        reads=[("ps", b), ("lnh%d" % u,), ("vec",), ("gk8",)], writes=[(dn + "0", c)])
                    P.add("dve", lambda e: e.scalar_tensor_tensor(
                        out=dt_[1][64:128, c * CH:(c + 1) * CH], in0=PS[b][64:128, :], scalar=gv[64:128, l:l + 1],
                        in1=lnh[u][64:128, :], op0=ALU.mult, op1=ALU.mult),
                        reads=[("ps", b), ("lnh%d" % u,), ("vec",), ("gk8",)], writes=[(dn + "1", c)])

                nU = len(units)
                stage_a(0)
                for ui in range(nU):
                    if ui + 1 < nU:
                        stage_a(ui + 1)
                    stage_b(ui)

            def proj_head(h, pb_):
                hh = h % 2
                units = [(c, a) for c in range(NCH) for a in range(2)]
                st_ = {}

                def stage_a(ui):
                    c, a = units[ui]
                    u = ucnt[0] % 2
                    ucnt[0] += 1
                    b = bank()
                    st_[ui] = (u, b)
                    gemm(PS[b][0:64, :], b, [(wqk[pb_][:, a, k, hh * 64:(hh + 1) * 64], hT[:, k, c * CH:(c + 1) * CH]) for k in range(KC)],
                         [("wqk%d" % pb_, a)] + hk(c))
                    P.add("act", lambda e: e.activation(out=sqh[u][0:64, :], in_=PS[b][0:64, :], func=AF.Square),
                          reads=[("ps", b)], writes=[("sqh%d" % u,)])

                def stage_b(ui):
                    c, a = units[ui]
                    u, b = st_[ui]
                    b2 = bank()
                    mm(PS[b2][0:64, :], ("ps", b2), ones[0:64, 0:64], sqh[u][0:64, :], True, True, [("sqh%d" % u,), ("ones",)])
                    P.add("act", lambda e: e.activation(out=lnh[u][0:64, :], in_=PS[b2][0:64, :], func=AF.Ln, scale=1.0 / 64, bias=EPS),
                          reads=[("ps", b2)], writes=[("lnh%d" % u,)])
                    P.add("act", lambda e: e.activation(out=lnh[u][0:64, :], in_=lnh[u][0:64, :], func=AF.Exp, scale=-0.5),
                          reads=[("lnh%d" % u,)], writes=[("lnh%d" % u,)])
                    dt_ = (QA, KA)[a]
                    dn = ("QA", "KA")[a]
                    gv = (gq2, gk8)[a]
                    P.add("dve", lambda e: e.scalar_tensor_tensor(
                        out=dt_[hh][0:64, c * CH:(c + 1) * CH], in0=PS[b][0:64, :], scalar=gv[0:64, l:l + 1],
                        in1=lnh[u][0:64, :], op0=ALU.mult, op1=ALU.mult),
                        reads=[("ps", b), ("lnh%d" % u,), ("vec",), ("gk8",)], writes=[(dn + "%d" % hh, c)])

                nU = len(units)
                stage_a(0)
                for ui in range(nU):
                    if ui + 1 < nU:
                        stage_a(ui + 1)
                    stage_b(ui)

            def aug_rows(p):
                for hh in range(2):
                    h = 2 * p + hh
                    P.add("sp", (lambda hh, h: lambda e: e.dma_start(out=QA[hh][64:66, :], in_=cpd[h, 0:2, :]))(hh, h),
                          reads=[("cpd",)], writes=[("QA%d" % hh, "aug")], dma="qaug%d" % hh)
                    P.add("sp", (lambda hh, h: lambda e: e.dma_start(out=KA[hh][66:68, :], in_=cpd[h, 2:4, :]))(hh, h),
                          reads=[("cpd",)], writes=[("KA%d" % hh, "aug")], dma="kaug%d" % hh)

            ocnt = [0]

            def attn_head(h):
                hb = h % 2
                qa, ka = QA[hb], KA[hb]
                qn, kn = "QA%d" % hb, "KA%d" % hb
                tiles = []
                for c in range(NCH):
                    for j in range(4 * c + 4):
                        tiles.append((c, j))
                n = len(tiles)
                state = {}
                pending = []

                def emit_S(i):
                    c, j = tiles[i]
                    n0 = max(0, j - 4 * c) * 128
                    diag = j >= 4 * c
                    sbk = i % 4
                    rd = [(qn, c), (kn, j // 4), (qn, "aug"), (kn, "aug")]
                    kk = 128 if (USE_PAIR and hb == 1) else 68
                    mm(PS[sbk][:, n0:CH], ("ps", sbk), ka[0:kk, j * 128:(j + 1) * 128],
                       qa[0:kk, c * CH + n0:(c + 1) * CH], True, not diag, rd)
                    if diag:
                        mm(PS[sbk][:, n0:n0 + 128], ("ps", sbk), ident[:, :], maskb[:, :], False, True,
                           [("ident",), ("maskb",)])
                    P.add("act", lambda e: e.activation(out=PT[sbk][:, n0:CH], in_=PS[sbk][:, n0:CH], func=AF.Exp),
                          reads=[("ps", sbk)], writes=[("PT%d" % sbk,)])

                def emit_PV(i):
                    c, j = tiles[i]
                    n0 = max(0, j - 4 * c) * 128
                    sbk = i % 4
                    lastj = 4 * c + 3
                    if j == 0:
                        state["ob"] = (4, 5, 7)[ocnt[0] % 3]
                        ocnt[0] += 1
                    ob = state["ob"]
                    mm(PS[ob][0:65, n0:CH], ("ps", ob), VA[:, j, h, 0:65], PT[sbk][:, n0:CH], j == 0, j == lastj,
                       [("VA", j), ("VA", "ones"), ("PT%d" % sbk,)])
                    if j == lastj:
                        u = c % 2
                        P.add("dve", lambda e: e.tensor_copy(out=rdh[64:65, :], in_=PS[ob][64:65, :]),
                              reads=[("ps", ob)], writes=[("rdh",)])
                        P.add("dve", lambda e: e.tensor_tensor(out=rdl[64:65, :], in0=PS[ob][64:65, :], in1=rdh[64:65, :],
                                                               op=ALU.subtract),
                              reads=[("ps", ob), ("rdh",)], writes=[("rdl",)])

                        def part2():
                            mm(PS[6][0:64, :], ("ps", 6), sel[:, :], rdh[:, :], True, False, [("sel",), ("rdh",)])
                            mm(PS[6][0:64, :], ("ps", 6), sel[:, :], rdl[:, :], False, True, [("sel",), ("rdl",)])
                            P.add("act", lambda e: e.activation(out=bcs[u], in_=PS[6][0:64, :], func=AF.Copy),
                                  reads=[("ps", 6)], writes=[("bcs%d" % u,)])
                            P.add("dve", lambda e: e.reciprocal(out=bcs[u], in_=bcs[u]),
                                  reads=[("bcs%d" % u,)], writes=[("bcs%d" % u,)])
                            if h % 2 == 0:
                                P.add("dve", lambda e: e.tensor_tensor(out=aT[0:64, h // 2, c * CH:(c + 1) * CH],
                                                                       in0=PS[ob][0:64, :], in1=bcs[u], op=ALU.mult),
                                      reads=[("ps", ob), ("bcs%d" % u,)], writes=[("aT", h // 2, c, 0)])
                            else:
                                P.add("dve", lambda e: e.tensor_tensor(out=atm[u], in0=PS[ob][0:64, :], in1=bcs[u], op=ALU.mult),
                                      reads=[("ps", ob), ("bcs%d" % u,)], writes=[("atm%d" % u,)])
                                P.add("sp", lambda e: e.dma_start(out=aT[64:128, h // 2, c * CH:(c + 1) * CH], in_=atm[u]),
                                      reads=[("atm%d" % u,)], writes=[("aT", h // 2, c, 1)], dma="atm%d" % u)
                        pending.append((i + 4, part2))

                LA = 3
                for i in range(min(LA, n)):
                    emit_S(i)
                for i in range(n):
                    if i + LA < n:
                        emit_S(i + LA)
                    emit_PV(i)
                    while pending and pending[0][0] <= i:
                        pending.pop(0)[1]()
                while pending:
                    pending.pop(0)[1]()

            pscnt[0] = 0
            load_wqk(0)
            for p in range(4):
                if p + 1 < 4:
                    load_wqk(p + 1)
                if USE_PAIR:
                    proj_pair(p)
                else:
                    aug_rows(p)
                    proj_head(2 * p, p % 2)
                    proj_head(2 * p + 1, p % 2)
                attn_head(2 * p)
                attn_head(2 * p + 1)
            if l == 0 and "d_aT" in dbg_out:
                P.add("sp", lambda e: e.dma_start(out=dbg_out["d_aT"].rearrange("(k p) n -> p k n", p=128), in_=aT),
                      reads=[("aT", k, c, q) for k in range(4) for c in range(NCH) for q in range(2)],
                      writes=[("d_aT",)], dma="dbg")

            chk('P2d')
            NPB = 16448
            ub = V("ub", OFF_D, NPB, dt=F32)
            pa = V("pa", OFF_D + NPB, NPB, dt=F32)
            pb = V("pb", OFF_D + 2 * NPB, NPB, dt=F32)
            wpx = [V("wpx%d" % i, OFF_D + 3 * NPB + i * 2048, 2048, pat="p (k n) -> p k n", n=128) for i in range(2)]
            dT = V("dT", OFF_C + 16384, 16384, pat="p (k n) -> p k n", n=S)
            cbz = V("cbz", OFF_C, 16384, pat="p (k n) -> p k n", n=S)
            assert OFF_D + 3 * NPB + 4096 <= AR_BYTES
            P.recycle(["ub", "pa", "pb", "wpx0", "wpx1", "dT", "cbz"])
            for t_, nm in ((ub, "ub"), (pa, "pa"), (pb, "pb")):
                P.add("pool", (lambda t_: lambda e: e.memset(t_[:, 0:16], 0.0))(t_), writes=[(nm, "halo")])
            H = 16
            for blk in range(2):
                wdma(wpx[blk], w_in[l, :, OPX + blk * 128:OPX + (blk + 1) * 128].rearrange("(k p) n -> p k n", p=128),
                     ("wpx%d" % blk,), "wpx%d" % blk)
            for blk in range(2):
                for c in range(NCH):
                    b = bank()
                    gemm(PS[b][:, :], b, [(wpx[blk][:, k, :], hT[:, k, c * CH:(c + 1) * CH]) for k in range(KC)],
                         [("wpx%d" % blk,)] + hk(c))
                    P.add("act", (lambda b, c: lambda e: e.activation(out=ub[:, H + c * CH:H + (c + 1) * CH], in_=PS[b][:, :],
                                                                       func=AF.Copy))(b, c),
                          reads=[("ps", b)], writes=[("ub", c)])
                allu = [("ub", c) for c in range(NCH)] + [("ub", "halo")]
                P.add("dve", lambda e: e.tensor_tensor(out=pa[:, H:H + S], in0=ub[:, H:H + S], in1=ub[:, H - 1:H - 1 + S], op=ALU.add),
                      reads=allu + [("pa", "halo")], writes=[("pa", "w")])
                if blk == 0:
                    P.add("dve", lambda e: e.tensor_tensor(out=pb[64:128, H:H + S], in0=pa[64:128, H:H + S],
                                                           in1=pa[64:128, H - 2:H - 2 + S], op=ALU.add),
                          reads=[("pa", "w"), ("pa", "halo"), ("pb", "halo")], writes=[("pb", "w")])
                    resA, resB = pa, pb
                    rk = [("pa", "w"), ("pb", "w")]
                else:
                    P.add("dve", lambda e: e.tensor_tensor(out=pb[:, H:H + S], in0=pa[:, H:H + S], in1=pa[:, H - 2:H - 2 + S], op=ALU.add),
                          reads=[("pa", "w"), ("pa", "halo"), ("pb", "halo")], writes=[("pb", "w")])
                    P.add("dve", lambda e: e.tensor_tensor(out=pa[:, H:H + S], in0=pb[:, H:H + S], in1=pb[:, H - 4:H - 4 + S], op=ALU.add),
                          reads=[("pb", "w"), ("pb", "halo"), ("pa", "halo")], writes=[("pa", "w")])
                    P.add("dve", lambda e: e.tensor_tensor(out=pb[64:128, H:H + S], in0=pa[64:128, H:H + S],
                                                           in1=pa[64:128, H - 8:H - 8 + S], op=ALU.add),
                          reads=[("pa", "w"), ("pa", "halo"), ("pb", "w")], writes=[("pb", "w2")])
                    resA, resB = pa, pb
                    rk = [("pa", "w"), ("pb", "w2"), ("pb", "w")]
                for (r_, p0, p1) in ((resA, 0, 64), (resB, 64, 128)):
                    P.add("dve", (lambda r_, p0, p1, blk: lambda e: e.scalar_tensor_tensor(
                        out=dT[p0:p1, blk, :], in0=r_[p0:p1, H:H + S], scalar=invw[p0:p1, blk:blk + 1],
                        in1=ub[p0:p1, H:H + S], op0=ALU.mult, op1=ALU.subtract))(r_, p0, p1, blk),
                        reads=rk + allu + [("cst",)], writes=[("dT", blk, p0)])
                    P.add("dve", (lambda r_, p0, p1, blk: lambda e: e.tensor_tensor(
                        out=r_[p0:p1, H:H + 16], in0=r_[p0:p1, H:H + 16], in1=invtab[blk][p0:p1, :], op=ALU.mult))(r_, p0, p1, blk),
                        reads=rk + [("dT", blk, p0), ("cst",)], writes=[("ptmp", blk, p0)])
                    P.add("dve", (lambda r_, p0, p1, blk: lambda e: e.tensor_tensor(
                        out=dT[p0:p1, blk, 0:16], in0=r_[p0:p1, H:H + 16], in1=ub[p0:p1, H:H + 16], op=ALU.subtract))(r_, p0, p1, blk),
                        reads=[("ptmp", blk, p0)] + allu, writes=[("dT", blk, p0)])
                if blk == 0:
                    for nm in ("pa", "pb"):
                        pass
            if l == 0 and "d_dT" in dbg_out:
                P.add("sp", lambda e: e.dma_start(out=dbg_out["d_dT"].rearrange("(k p) n -> p k n", p=128), in_=dT),
                      reads=[("dT", b_, p_) for b_ in range(2) for p_ in (0, 64)], writes=[("d_dT",)], dma="dbg")

            chk('P2e')
            NZ = 16400
            zb = V("zb", OFF_D, NZ, dt=F32)
            cxs = [V("cxs%d" % i, OFF_D + NZ + i * 2048, 2048, dt=F32) for i in range(2)]
            acc = [V("acc%d" % i, OFF_D + NZ + 4096 + i * 2048, 2048, dt=F32) for i in range(2)]
            wc3 = [V("wc3%d" % i, OFF_D + NZ + 8192 + i * 6144, 6144, pat="p (k a n) -> p k a n", a=3, n=128) for i in range(2)]
            P.recycle(["zb", "cxs0", "cxs1", "acc0", "acc1", "wc30", "wc31"])
            P.add("pool", lambda e: e.memset(zb[:, 0:2], 0.0), writes=[("zb", "halo")])
            for blk in range(2):
                for a, off in enumerate((OCX, OCB, OCC)):
                    wdma(wc3[blk][:, :, a, :], w_in[l, :, off + blk * 128:off + (blk + 1) * 128].rearrange("(k p) n -> p k n", p=128),
                         ("wc3%d" % blk, a), "wc3%d_%d" % (blk, a))
            for blk in range(2):
                for c in range(NCH):
                    u = c % 2
                    bx, bb, bc_ = bank(), bank(), bank()
                    for a, b in ((0, bx), (1, bb), (2, bc_)):
                        gemm(PS[b][:, :], b, [(wc3[blk][:, k, a, :], hT[:, k, c * CH:(c + 1) * CH]) for k in range(KC)],
                             [("wc3%d" % blk, a)] + hk(c))
                    P.add("act", (lambda bx, u: lambda e: e.activation(out=cxs[u], in_=PS[bx][:, :], func=AF.Copy))(bx, u),
                          reads=[("ps", bx)], writes=[("cxs%d" % u,)])
                    P.add("dve", (lambda bc_, u, c: lambda e: e.tensor_tensor(out=zb[:, 2 + c * CH:2 + (c + 1) * CH], in0=PS[bc_][:, :],
                                                                              in1=cxs[u], op=ALU.mult))(bc_, u, c),
                          reads=[("ps", bc_), ("cxs%d" % u,)], writes=[("zb", c)])
                    zr = [("zb", c), ("zb", "halo")] + ([("zb", c - 1)] if c > 0 else [])
                    P.add("dve", (lambda u, c, blk: lambda e: e.tensor_scalar(out=acc[u], in0=zb[:, c * CH:(c + 1) * CH],
                                                                              scalar1=cw[:, l, 0, blk:blk + 1], scalar2=None, op0=ALU.mult))(u, c, blk),
                          reads=zr + [("vec",)], writes=[("acc%d" % u,)])
                    for jj in (1, 2):
                        P.add("dve", (lambda u, c, blk, jj: lambda e: e.scalar_tensor_tensor(
                            out=acc[u], in0=zb[:, jj + c * CH:jj + (c + 1) * CH], scalar=cw[:, l, jj, blk:blk + 1],
                            in1=acc[u], op0=ALU.mult, op1=ALU.add))(u, c, blk, jj),
                            reads=zr + [("acc%d" % u,), ("vec",)], writes=[("acc%d" % u,)])
                    P.add("dve", (lambda u, c, blk, bb: lambda e: e.tensor_tensor(out=cbz[:, blk, c * CH:(c + 1) * CH], in0=acc[u],
                                                                                  in1=PS[bb][:, :], op=ALU.mult))(u, c, blk, bb),
                          reads=[("acc%d" % u,), ("ps", bb)], writes=[("cbz", blk, c)])
            if l == 0 and "d_cbz" in dbg_out:
                P.add("sp", lambda e: e.dma_start(out=dbg_out["d_cbz"].rearrange("(k p) n -> p k n", p=128), in_=cbz),
                      reads=[("cbz", b_, c_) for b_ in range(2) for c_ in range(NCH)], writes=[("d_cbz",)], dma="dbg")

            chk('P3a')
            WS = 7936
            wg = [V("wg%d" % i, OFF_D + i * WS, 6144, pat="p (k a n) -> p k a n", a=3, n=128) for i in range(2)]
            wao = [V("wao%d" % i, OFF_D + i * WS + 6144, 1024, pat="p (k n) -> p k n", n=128) for i in range(2)]
            wco = [V("wco%d" % i, OFF_D + i * WS + 7168, 512, pat="p (k n) -> p k n", n=128) for i in range(2)]
            wpo = [V("wpo%d" % i, OFF_D + i * WS + 7680, 256) for i in range(2)]
            T0 = OFF_D + 2 * WS
            sg = [V("sg%d" % i, T0 + i * 2048, 2048, dt=F32) for i in range(2)]
            mt = [V("mt%d" % i, T0 + 4096 + i * 2048, 2048, dt=F32) for i in range(2)]
            tt_ = [V("tt%d" % i, T0 + 8192 + i * 2048, 2048, dt=F32) for i in range(2)]
            mb = [V("mb%d" % i, T0 + 12288 + i * 1024, 1024) for i in range(2)]
            wo = V("wo", T0 + 14336, 16384, pat="p (k n) -> p k n", n=D)
            sd2 = [V("sd2%d" % i, T0 + 30720 + i * 2048, 2048, dt=F32) for i in range(2)]
            rs2 = [V("rs2%d" % i, T0 + 34816 + i * 2048, 2048, dt=F32) for i in range(2)]
            assert T0 + 38912 <= AR_BYTES
            P.recycle(["wg0", "wg1", "wao0", "wao1", "wco0", "wco1", "wpo0", "wpo1", "sg0", "sg1", "mt0", "mt1",
                       "tt0", "tt1", "mb0", "mb1", "wo", "sd20", "sd21", "rs20", "rs21"])

            def load_wset(j):
                i = j % 2
                for a in range(3):
                    wdma(wg[i][:, :, a, :],
                         w_in[l, :, OG + a * D + j * 128:OG + a * D + (j + 1) * 128].rearrange("(k p) n -> p k n", p=128),
                         ("wg%d" % i, a), "wg%d_%d" % (i, a))
                wdma(wao[i], w_ao[l, :, j * 128:(j + 1) * 128].rearrange("(k p) n -> p k n", p=128), ("wao%d" % i,), "wao%d" % i)
                wdma(wco[i], w_co[l, :, j * 128:(j + 1) * 128].rearrange("(k p) n -> p k n", p=128), ("wco%d" % i,), "wco%d" % i)
                g = j // 2
                r0 = (g % 2) * 64
                wdma(wpo[i][r0:r0 + 64, :], w_po[l, g, :, (j % 2) * 128:(j % 2 + 1) * 128], ("wpo%d" % i,), "wpo%d" % i)

            load_wset(0)
            ucnt2 = [0]
            for j in range(8):
                i = j % 2
                if j + 1 < 8:
                    load_wset(j + 1)
                if j == 1:
                    wdma(wo, w_o[l].rearrange("(k p) n -> p k n", p=128), ("wo",), "wo")
                g = j // 2
                r0 = (g % 2) * 64
                for c in range(NCH):
                    u = ucnt2[0] % 2
                    ucnt2[0] += 1
                    cs = slice(c * CH, (c + 1) * CH)
                    bg, by = bank(), bank()
                    gemm(PS[bg][:, :], bg, [(wg[i][:, k, 0, :], hT[:, k, cs]) for k in range(KC)], [("wg%d" % i, 0)] + hk(c))
                    gemm(PS[by][:, :], by, [(wao[i][:, k, :], aT[:, k, cs]) for k in range(4)],
                         [("wao%d" % i,)] + [("aT", k, c, q) for k in range(4) for q in range(2)])
                    P.add("act", (lambda bg, u: lambda e: e.activation(out=sg[u], in_=PS[bg][:, :], func=AF.Sigmoid))(bg, u),
                          reads=[("ps", bg)], writes=[("sg%d" % u,)])
                    P.add("dve", (lambda by, u: lambda e: e.tensor_tensor(out=mt[u], in0=sg[u], in1=PS[by][:, :], op=ALU.mult))(by, u),
                          reads=[("ps", by), ("sg%d" % u,)], writes=[("mt%d" % u,)])
                    bg, by = bank(), bank()
                    gemm(PS[bg][:, :], bg, [(wg[i][:, k, 1, :], hT[:, k, cs]) for k in range(KC)], [("wg%d" % i, 1)] + hk(c))
                    gemm(PS[by][:, :], by, [(wco[i][:, k, :], cbz[:, k, cs]) for k in range(2)],
                         [("wco%d" % i,), ("cbz", 0, c), ("cbz", 1, c)])
                    P.add("act", (lambda bg, u: lambda e: e.activation(out=sg[u], in_=PS[bg][:, :], func=AF.Sigmoid))(bg, u),
                          reads=[("ps", bg)], writes=[("sg%d" % u,)])
                    P.add("dve", (lambda by, u: lambda e: e.tensor_tensor(out=tt_[u], in0=sg[u], in1=PS[by][:, :], op=ALU.mult))(by, u),
                          reads=[("ps", by), ("sg%d" % u,)], writes=[("tt%d" % u,)])
                    P.add("pool", (lambda u: lambda e: e.tensor_tensor(out=mt[u], in0=mt[u], in1=tt_[u], op=ALU.add))(u),
                          reads=[("mt%d" % u,), ("tt%d" % u,)], writes=[("mt%d" % u,)])
                    bg, by = bank(), bank()
                    gemm(PS[bg][:, :], bg, [(wg[i][:, k, 2, :], hT[:, k, cs]) for k in range(KC)], [("wg%d" % i, 2)] + hk(c))
                    mm(PS[by][:, :], ("ps", by), wpo[i][r0:r0 + 64, :], dT[r0:r0 + 64, g // 2, cs], True, True,
                       [("wpo%d" % i,), ("dT", g // 2, r0)])
                    P.add("act", (lambda bg, u: lambda e: e.activation(out=sg[u], in_=PS[bg][:, :], func=AF.Sigmoid))(bg, u),
                          reads=[("ps", bg)], writes=[("sg%d" % u,)])
                    P.add("dve", (lambda by, u, j: lambda e: e.scalar_tensor_tensor(out=tt_[u], in0=PS[by][:, :], scalar=psc[:, l, j:j + 1],
                                                                                    in1=sg[u], op0=ALU.mult, op1=ALU.mult))(by, u, j),
                          reads=[("ps", by), ("sg%d" % u,), ("vec",)], writes=[("tt%d" % u,)])
                    P.add("pool", (lambda u: lambda e: e.tensor_tensor(out=mb[u], in0=mt[u], in1=tt_[u], op=ALU.add))(u),
                          reads=[("mt%d" % u,), ("tt%d" % u,)], writes=[("mb%d" % u,)])
                    P.add("sp", (lambda u, j, c: lambda e: e.dma_start(out=mT[j * 128:(j + 1) * 128, c * CH:(c + 1) * CH], in_=mb[u]))(u, j, c),
                          reads=[("mb%d" % u,)], writes=[("mT", j, c)], dma="mb%d" % u)

            chk('P3b')
            mc = [V("mc%d" % i, OFF_AT + 49152 + i * 8192, 8192, pat="p (k n) -> p k n", n=CH) for i in range(2)]
            assert OFF_AT + 49152 + 16384 <= OFF_D
            P.recycle(["xc0", "xc1", "sqx0", "sqx1", "mc0", "mc1"])
            def p3b_tail(c):
                i = c % 2
                xk = [("xc%d" % i, o) for o in range(KC)]
                norm_b(l, c, xc[i], xk, sqx[i], ("sqx%d" % i,), g2, sd2[i], ("sd2%d" % i,))

            def p3b_load(c):
                i = c % 2
                P.add("sp", (lambda c, i: lambda e: e.dma_start(out=xc[i], in_=src[c]))(c, i),
                      reads=[(srckey, c)], writes=[("xc%d" % i, o) for o in range(KC)], dma="xc%d" % i)
                P.add("sp", (lambda c, i: lambda e: e.dma_start(
                    out=mc[i], in_=mT[:, c * CH:(c + 1) * CH].rearrange("(k p) n -> p k n", p=128)))(c, i),
                    reads=[("mT", j, c) for j in range(8)], writes=[("mc%d" % i,)], dma="mc%d" % i)

            p3b_load(0)
            for c in range(NCH):
                i = c % 2
                xk = [("xc%d" % i, o) for o in range(KC)]
                for o in range(8):
                    b = bank()
                    gemm(PS[b][:, :], b, [(wo[:, k, o * 128:(o + 1) * 128], mc[i][:, k, :]) for k in range(KC)],
                         [("wo",), ("mc%d" % i,)])
                    P.add("dve", (lambda b, i, o: lambda e: e.tensor_tensor(out=xc[i][:, o, :], in0=xc[i][:, o, :], in1=PS[b][:, :],
                                                                            op=ALU.add))(b, i, o),
                          reads=[("ps", b), ("xc%d" % i, o)], writes=[("xc%d" % i, o)])
                    if o == 1:
                        if c >= 1:
                            p3b_tail(c - 1)
                        if c + 1 < NCH:
                            p3b_load(c + 1)
                P.add("sp", (lambda c, i: lambda e: e.dma_start(out=xs[c], in_=xc[i]))(c, i),
                      reads=xk, writes=[("xs", c)], dma="xst%d" % i)
                norm_a(xc[i], xk, sqx[i], ("sqx%d" % i,))
            p3b_tail(NCH - 1)

            chk('P4')
            HS = S // 2
            uT = V("uT", OFF_AT, NHB * HS * 2, pat="p (k n) -> p k n", n=HS)
            F0 = OFF_AT + NHB * HS * 2
            wgu = [V("wgu%d" % i, F0 + i * 4096, 4096, pat="p (k a n) -> p k a n", a=2, n=128) for i in range(2)]
            wfo = [V("wfo%d" % i, F0 + 8192 + i * 5632, 5632, pat="p (k n) -> p k n", n=128) for i in range(2)]
            xo = [V("xo%d" % i, F0 + 19456 + i * 8192, 8192, dt=F32) for i in range(2)]
            sgl = [V("sgl%d" % i, F0 + 35840 + i * 2048, 2048, dt=F32) for i in range(2)]
            assert F0 + 39936 <= AR_BYTES
            P.recycle(["uT", "wgu0", "wgu1", "wfo0", "wfo1", "xo0", "xo1", "sgl0", "sgl1"])
            wcnt = [0]
            ocn = [0]
            for half in range(2):
                for hbk in range(NHB):
                    i = wcnt[0] % 2
                    wcnt[0] += 1
                    wdma(wgu[i][:, :, 0, :], w_fi[l, :, hbk * 128:(hbk + 1) * 128].rearrange("(k p) n -> p k n", p=128),
                         ("wgu%d" % i, 0), "wgu%d_0" % i)
                    wdma(wgu[i][:, :, 1, :], w_fi[l, :, D_FF + hbk * 128:D_FF + (hbk + 1) * 128].rearrange("(k p) n -> p k n", p=128),
                         ("wgu%d" % i, 1), "wgu%d_1" % i)
                    for cc in range(4):
                        c = half * 4 + cc
                        cs = slice(c * CH, (c + 1) * CH)
                        u = (hbk * 4 + cc) % 2
                        bg, bu = bank(), bank()
                        gemm(PS[bg][:, :], bg, [(wgu[i][:, k, 0, :], hT[:, k, cs]) for k in range(KC)], [("wgu%d" % i, 0)] + hk(c))
                        gemm(PS[bu][:, :], bu, [(wgu[i][:, k, 1, :], hT[:, k, cs]) for k in range(KC)], [("wgu%d" % i, 1)] + hk(c))
                        P.add("act", (lambda bg, u: lambda e: e.activation(out=sgl[u], in_=PS[bg][:, :], func=AF.Sigmoid))(bg, u),
                              reads=[("ps", bg)], writes=[("sgl%d" % u,)])
                        P.add("dve", (lambda bg, u: lambda e: e.tensor_tensor(out=sgl[u], in0=sgl[u], in1=PS[bg][:, :], op=ALU.mult))(bg, u),
                              reads=[("ps", bg), ("sgl%d" % u,)], writes=[("sgl%d" % u,)])
                        P.add("dve", (lambda bu, u, hbk, cc: lambda e: e.tensor_tensor(out=uT[:, hbk, cc * CH:(cc + 1) * CH], in0=sgl[u],
                                                                                       in1=PS[bu][:, :], op=ALU.mult))(bu, u, hbk, cc),
                              reads=[("ps", bu), ("sgl%d" % u,)], writes=[("uT", hbk, cc)])
                for o in range(8):
                    i = ocn[0] % 2
                    ocn[0] += 1
                    wdma(wfo[i], w_fo[l, :, o * 128:(o + 1) * 128].rearrange("(k p) n -> p k n", p=128), ("wfo%d" % i,), "wfo%d" % i)
                    P.add("sp", (lambda o, i, half: lambda e: e.dma_start(
                        out=xo[i].rearrange("p (c n) -> p c n", n=CH),
                        in_=xs[half * 4:(half + 1) * 4, :, o, :].rearrange("c p n -> p c n")))(o, i, half),
                        reads=[("xs", half * 4 + cc) for cc in range(4)], writes=[("xo%d" % i, cc) for cc in range(4)], dma="xo%d" % i)
                    for cc in range(4):
                        b = bank()
                        gemm(PS[b][:, :], b, [(wfo[i][:, k, :], uT[:, k, cc * CH:(cc + 1) * CH]) for k in range(NHB)],
                             [("wfo%d" % i,)] + [("uT", k, cc) for k in range(NHB)])
                        P.add("dve", (lambda b, i, cc: lambda e: e.tensor_tensor(out=xo[i][:, cc * CH:(cc + 1) * CH],
                                                                                 in0=xo[i][:, cc * CH:(cc + 1) * CH], in1=PS[b][:, :],
                                                                                 op=ALU.add))(b, i, cc),
                              reads=[("ps", b), ("xo%d" % i, cc)], writes=[("xo%d" % i, cc)])
                    P.add("sp", (lambda o, i, half: lambda e: e.dma_start(
                        out=dst[half * 4:(half + 1) * 4, :, o, :].rearrange("c p n -> p c n"),
                        in_=xo[i].rearrange("p (c n) -> p c n", n=CH)))(o, i, half),
                        reads=[("xo%d" % i, cc) for cc in range(4)], writes=[(dstkey + "_o", o, half)], dma="xot%d" % i)
                P.marker(reads=[(dstkey + "_o", o, half) for o in range(8)],
                         writes=[(dstkey, half * 4 + cc) for cc in range(4)])
        try:
            for l_ in range(depth):
                layer(l_)
        except _Stop:
            pass
        P.add("sp", None, reads=[("yT_o", o, half) for o in range(8) for half in range(2)] + [(k,) for k in ("d_hT", "d_aT", "d_cbz", "d_dT", "d_c", "d_v")])
        P.emit(nc, st)
    return nc, P


def _consts():
    c = np.zeros((128, NCST), np.float32)
    c[:, 0:128] = np.eye(128, dtype=np.float32)
    s = np.arange(128)[:, None]
    t = np.arange(128)[None, :]
    c[:, 128:256] = np.where(s <= t, 0.0, -30000.0)
    tt = np.arange(16, dtype=np.float32)
    for blk, (wa, wb) in enumerate(((2, 4), (8, 16))):
        c[0:64, 256 + blk * 16:272 + blk * 16] = 1.0 / np.minimum(tt + 1.0, wa)
        c[64:128, 256 + blk * 16:272 + blk * 16] = 1.0 / np.minimum(tt + 1.0, wb)
        c[0:64, 288 + blk] = 1.0 / wa
        c[64:128, 288 + blk] = 1.0 / wb
    return c


def _vecs(norm_mix_g, norm_ffn_g, pool_scale, conv_w, q_norm_g, k_norm_g, forget_b):
    v = np.zeros((128, NVEC), np.float32)
    f = lambda a: np.asarray(a, np.float32).reshape(4, 8, 128).transpose(2, 0, 1).reshape(128, 32)
    v[:, 0:32] = f(norm_mix_g)
    v[:, 32:64] = f(norm_ffn_g)
    v[:, 64:96] = f(pool_scale)
    v[:, 96:120] = np.asarray(conv_w, np.float32).reshape(4, 3, 2, 128).transpose(3, 0, 1, 2).reshape(128, 24)
    v[0:64, 120:124] = np.asarray(q_norm_g, np.float32).T
    v[64:128, 120:124] = np.asarray(q_norm_g, np.float32).T
    v[0:64, 124:128] = np.asarray(k_norm_g, np.float32).T
    v[64:128, 124:128] = np.asarray(k_norm_g, np.float32).T
    v[0:8, 128:132] = np.asarray(forget_b, np.float32).T
    return v


_NC_CACHE = {}


def make_in_maps(inputs, cores):
    x = np.asarray(inputs["x"], np.float32)
    shared = {
        "w_in": np.ascontiguousarray(inputs["w_in"], np.float32),
        "w_attn_out": np.ascontiguousarray(inputs["w_attn_out"], np.float32),
        "w_conv_out": np.ascontiguousarray(inputs["w_conv_out"], np.float32),
        "pool_w": np.ascontiguousarray(inputs["pool_w"], np.float32),
        "w_o": np.ascontiguousarray(inputs["w_o"], np.float32),
        "w_ffn_in": np.ascontiguousarray(inputs["w_ffn_in"], np.float32),
        "w_ffn_out": np.ascontiguousarray(inputs["w_ffn_out"], np.float32),
        "vecs": _vecs(inputs["norm_mix_g"], inputs["norm_ffn_g"], inputs["pool_scale"], inputs["conv_w"],
                      inputs["q_norm_g"], inputs["k_norm_g"], inputs["forget_b"]),
        "cst": _consts(),
    }
    maps = []
    for b in cores:
        m = dict(shared)
        m["xT"] = np.ascontiguousarray(x[b].reshape(NCH, CH, KC, 128).transpose(0, 3, 2, 1))
        maps.append(m)
    return maps


def kernel(**inputs):
    inputs = {k: np.asarray(v) for k, v in inputs.items()}
    if "nc" not in _NC_CACHE:
        _NC_CACHE["nc"] = build(4)[0]
    nc = _NC_CACHE["nc"]
    cores = list(range(8))
    res = run_bass_kernel_spmd(nc, make_in_maps(inputs, cores), core_ids=cores)
    out = np.stack([np.ascontiguousarray(r["yT"].transpose(0, 3, 2, 1)).reshape(S, D) for r in res.results], axis=0)
    return out.astype(np.float32)
```

```python
import contextlib
import numpy as np
import concourse.bass as bass
import concourse.mybir as mybir
from concourse.bass_utils import run_bass_kernel_spmd

F32 = mybir.dt.float32
BF16 = mybir.dt.bfloat16
AF = mybir.ActivationFunctionType
ALU = mybir.AluOpType

D = 1024
S = 4096
NCH = 8
CH = 512
KC = 8
D_IN = 5640
D_FF = 2816
NHB = 22
EPS = 1e-6
ENGS = ("pe", "act", "dve", "pool", "sp")

OQ, OK_, OV, OF, OCX, OCB, OCC, OPX, OG = 0, 512, 1024, 1536, 1544, 1800, 2056, 2312, 2568


class Op:
    __slots__ = ("eng", "fn", "deps", "is_dma", "slot", "epoch", "signal",
                 "ordinal", "cum", "ndma", "pos", "marker")


class Prog:
    def __init__(self):
        self.streams = {e: [] for e in ENGS}
        self.lastw = {}
        self.readers = {}
        self.epoch = 0
        self.slot_cum = {}
        self.regions = {}
        self.bufkeys = {}
        self.buf_fence = {}

    @staticmethod
    def _compress(deps):
        best_c = {}
        best_d = {}
        for w in deps:
            if w.is_dma:
                k = (w.slot, w.epoch)
                o = best_d.get(k)
                if o is None or w.cum > o.cum:
                    best_d[k] = w
            else:
                o = best_c.get(w.eng)
                if o is None or w.pos > o.pos:
                    best_c[w.eng] = w
        return list(best_c.values()) + list(best_d.values())

    def region(self, name, lo, hi):
        self.regions[name] = (lo, hi)

    def recycle(self, new_names):
        olds = set()
        for n in new_names:
            lo, hi = self.regions[n]
            for m, (l2, h2) in self.regions.items():
                if l2 < hi and lo < h2:
                    olds.add(m)
        deps = {}
        for n in olds:
            for k in self.bufkeys.get(n, ()):
                w = self.lastw.pop(k, None)
                if w is not None:
                    deps[id(w)] = w
                for r in self.readers.pop(k, ()):
                    deps[id(r)] = r
            self.bufkeys[n] = set()
            m = self.buf_fence.pop(n, None)
            if m is not None:
                for w in m.deps:
                    deps[id(w)] = w
        M = Op()
        M.marker = True
        M.deps = self._compress(deps.values())
        for n in new_names:
            self.buf_fence[n] = M

    def marker(self, reads=(), writes=()):
        deps = {}

        def put(w):
            if w.marker:
                for x in w.deps:
                    deps[id(x)] = x
            else:
                deps[id(w)] = w
        for k in reads:
            w = self.lastw.get(k)
            if w is not None:
                put(w)
        for k in writes:
            w = self.lastw.get(k)
            if w is not None:
                put(w)
            for r in self.readers.get(k, ()):
                put(r)
        M = Op()
        M.marker = True
        M.deps = self._compress(deps.values())
        for k in writes:
            self.lastw[k] = M
            self.readers[k] = []
        return M

    def add(self, eng, fn, reads=(), writes=(), dma=None, ndma=1):
        op = Op()
        op.marker = False
        op.eng = eng
        op.fn = fn
        op.is_dma = dma is not None
        op.slot = dma
        op.epoch = 0 if dma is not None else self.epoch
        op.signal = op.is_dma
        op.ordinal = 0
        op.ndma = ndma
        op.pos = len(self.streams[eng])
        deps = {}

        def put(w, raw):
            if w.marker:
                for x in w.deps:
                    if id(x) not in deps:
                        deps[id(x)] = (x, True)
            else:
                o = deps.get(id(w))
                if o is None or (raw and not o[1]):
                    deps[id(w)] = (w, raw)

        for k in reads:
            w = self.lastw.get(k)
            if w is None:
                w = self.buf_fence.get(k[0])
            if w is not None:
                put(w, True)
        for k in writes:
            w = self.lastw.get(k)
            if w is None:
                w = self.buf_fence.get(k[0])
            if w is not None:
                put(w, False)
            for r in self.readers.get(k, ()):
                put(r, False)
        final = []
        for w, raw in deps.values():
            if w is op:
                continue
            if (not w.is_dma) and (not op.is_dma) and w.eng == eng == "pe":
                continue
            final.append(w)
        final = self._compress(final)
        for w in final:
            w.signal = True
        op.deps = final
        if op.is_dma:
            key = (dma, 0)
            c = self.slot_cum.get(key, 0) + ndma
            self.slot_cum[key] = c
            op.cum = c
        for k in reads:
            self.readers.setdefault(k, []).append(op)
            self.bufkeys.setdefault(k[0], set()).add(k)
        for k in writes:
            self.lastw[k] = op
            self.readers[k] = []
            self.bufkeys.setdefault(k[0], set()).add(k)
        self.streams[eng].append(op)
        return op

    def emit(self, nc, stack):
        for e in ENGS:
            cnt = {}
            for op in self.streams[e]:
                if op.is_dma or not op.signal:
                    continue
                c = cnt.get(op.epoch, 0) + 1
                cnt[op.epoch] = c
                op.ordinal = c
        sems = {}

        def sem_for(key):
            if key not in sems:
                sems[key] = stack.enter_context(nc.semaphore("s%d" % len(sems)))
            return sems[key]

        for e in ENGS:
            for op in self.streams[e]:
                if op.is_dma:
                    sem_for(("d", op.slot, op.epoch))
                elif op.signal:
                    sem_for(("e", op.eng, op.epoch))
        self.nsems = len(sems)
        block = stack.enter_context(nc.Block())
        streams = self.streams

        def run(eng_name):
            def body(engine):
                seen = {}
                for op in streams[eng_name]:
                    need = {}
                    for d in op.deps:
                        if d.is_dma:
                            k = ("d", d.slot, d.epoch)
                            v = 16 * d.cum
                        else:
                            k = ("e", d.eng, d.epoch)
                            v = d.ordinal
                        if v > need.get(k, 0):
                            need[k] = v
                    for k, v in need.items():
                        if v > seen.get(k, 0):
                            engine.wait_ge(sems[k], v)
                            seen[k] = v
                    if op.fn is None:
                        continue
                    r = op.fn(engine)
                    if op.is_dma:
                        s = sems[("d", op.slot, op.epoch)]
                        if not isinstance(r, (list, tuple)):
                            r = [r]
                        assert len(r) == op.ndma
                        for inst in r:
                            inst.then_inc(s, 16)
                    elif op.signal:
                        if isinstance(r, (list, tuple)):
                            r = r[-1]
                        r.then_inc(sems[("e", op.eng, op.epoch)], 1)
            return body

        block.tensor(run("pe"))
        block.scalar(run("act"))
        block.vector(run("dve"))
        block.gpsimd(run("pool"))
        block.sync(run("sp"))


OFF_HT = 0
OFF_AT = 65536
OFF_C = 98304
OFF_D = 131072
OFF_E = 164352
AR_BYTES = 201728
NCST = 290
USE_PAIR = True
NVEC = 132


class _Stop(Exception):
    pass


def build(depth=4, dbg=(), stop=None):
    nc = bass.Bass("TRN2", target_bir_lowering=False)
    dram = lambda name, shape, dt=F32, kind="ExternalInput": nc.dram_tensor(name, shape, dt, kind=kind).ap()
    xT = dram("xT", [NCH, 128, KC, CH])
    w_in = dram("w_in", [4, D, D_IN])
    w_ao = dram("w_attn_out", [4, 512, D])
    w_co = dram("w_conv_out", [4, 256, D])
    w_po = dram("pool_w", [4, 4, 64, 256])
    w_o = dram("w_o", [4, D, D])
    w_fi = dram("w_ffn_in", [4, D, 2 * D_FF])
    w_fo = dram("w_ffn_out", [4, D_FF, D])
    vecs_d = dram("vecs", [128, NVEC])
    cst_d = dram("cst", [128, NCST])
    yT = dram("yT", [NCH, 128, KC, CH], kind="ExternalOutput")
    xs = dram("xs_scr", [NCH, 128, KC, CH], kind="Internal")
    mT = dram("mT_scr", [NCH, 128, 8, CH], BF16, kind="Internal")
    cpd = dram("cp_scr", [8, 4, S], BF16, kind="Internal")
    dbg_out = {}
    for nm, shape, dt in (("d_hT", [D, S], BF16), ("d_aT", [512, S], BF16), ("d_cbz", [256, S], BF16),
                          ("d_dT", [256, S], BF16), ("d_c", [8, S], F32), ("d_qa", [68, S], BF16),
                          ("d_ka", [68, S], BF16), ("d_v", [128, 32 * 8 * 65], BF16)):
        if nm in dbg:
            dbg_out[nm] = dram(nm, shape, dt, kind="ExternalOutput")

    P = Prog()
    st = contextlib.ExitStack()
    with st:
        sbt = lambda name, shape, dt: st.enter_context(nc.sbuf_tensor(name, shape, dt))
        arena = sbt("arena", [128, AR_BYTES // 2], BF16)
        cst = sbt("cstf", [128, NCST], F32)
        vec = sbt("vecf", [128, NVEC], F32)
        ident = sbt("ident", [128, 128], BF16)
        maskb = sbt("maskb", [128, 128], BF16)
        ones = sbt("onesb", [128, 128], BF16)
        sel = sbt("sel", [65, 64], BF16)
        gk8 = sbt("gk8", [128, 4], F32)
        nfb = sbt("nfb", [8, 4], F32)
        onef = sbt("onef", [8, 1], F32)
        bd = sbt("bd", [128, 128], BF16)
        PS = [st.enter_context(nc.psum_tensor("ps%d" % i, [128, 512], F32)) for i in range(8)]

        def V(name, off, nbytes, p0=0, p1=128, dt=BF16, pat=None, **kw):
            P.region(name, off, off + nbytes)
            ap = arena[p0:p1, off // 2:(off + nbytes) // 2]
            if dt == F32:
                ap = ap.bitcast(F32)
            if pat:
                ap = ap.rearrange(pat, **kw)
            return ap

        g1 = vec[:, 0:32].rearrange("p (l k) -> p l k", k=8)
        g2 = vec[:, 32:64].rearrange("p (l k) -> p l k", k=8)
        psc = vec[:, 64:96].rearrange("p (l k) -> p l k", k=8)
        cw = vec[:, 96:120].rearrange("p (l j b) -> p l j b", j=3, b=2)
        gq2 = vec[:, 120:124]
        gk = vec[:, 124:128]
        fb = vec[0:8, 128:132]
        invtab = [cst[:, 256:272], cst[:, 272:288]]
        invw = cst[:, 288:290]

        hT = V("hT", OFF_HT, 65536, pat="p (k n) -> p k n", n=S)
        aT = V("aT", OFF_AT, 32768, pat="p (k n) -> p k n", n=S)
        QA = [V("QA%d" % i, OFF_C + i * 8192, 8192) for i in range(2)]
        KA = [V("KA%d" % i, OFF_C + 16384 + i * 8192, 8192) for i in range(2)]
        VA = V("VA", OFF_D, 33280, pat="p (t h d) -> p t h d", h=8, d=65)

        pscnt = [0]

        def bank():
            b = pscnt[0] % 8
            pscnt[0] += 1
            return b

        def mm(ps_ap, pskey, lhsT, rhs, start, stop, reads):
            P.add("pe", lambda e: e.matmul(ps_ap, lhsT=lhsT, rhs=rhs, start=start, stop=stop),
                  reads=reads, writes=[pskey])

        def gemm(ps_ap, b, pairs, reads):
            n = len(pairs)
            for i, (l, r) in enumerate(pairs):
                mm(ps_ap, ("ps", b), l, r, i == 0, i == n - 1, reads)

        def wdma(out_ap, in_ap, key, slot, n=1):
            P.add("pool", lambda e: e.dma_start(out=out_ap, in_=in_ap), writes=[key], dma=slot)

        P.add("sp", lambda e: e.dma_start(out=cst[:], in_=cst_d), writes=[("cst",)], dma="cst")
        P.add("sp", lambda e: e.dma_start(out=vec[:], in_=vecs_d), writes=[("vec",)], dma="vec")
        P.add("dve", lambda e: e.tensor_copy(out=ident[:], in_=cst[:, 0:128]), reads=[("cst",)], writes=[("ident",)])
        P.add("dve", lambda e: e.tensor_copy(out=maskb[:], in_=cst[:, 128:256]), reads=[("cst",)], writes=[("maskb",)])
        P.add("pool", lambda e: e.memset(ones[:], 1.0), writes=[("ones",)])
        P.add("pool", lambda e: e.memset(sel[:], 0.0), writes=[("sel",)])
        P.add("pool", lambda e: e.memset(sel[64:65, :], 1.0), writes=[("sel",)])
        P.add("pool", lambda e: e.memset(onef[:], 1.0), writes=[("onef",)])
        P.add("pool", lambda e: e.memset(bd[:], 0.0), writes=[("bd",)])
        P.add("pool", lambda e: e.memset(bd[0:64, 0:64], 1.0), writes=[("bd",)])
        P.add("pool", lambda e: e.memset(bd[64:128, 64:128], 1.0), writes=[("bd",)])
        P.add("dve", lambda e: e.tensor_scalar(out=gk8[:], in0=gk, scalar1=0.125, scalar2=None, op0=ALU.mult),
              reads=[("vec",)], writes=[("gk8",)])
        P.add("dve", lambda e: e.tensor_scalar(out=nfb[:], in0=fb, scalar1=-1.0, scalar2=None, op0=ALU.mult),
              reads=[("vec",)], writes=[("nfb",)])

        def norm_a(xc_ap, xckeys, sq_ap, sqkey):
            P.add("act", lambda e: e.activation(out=sq_ap, in_=xc_ap, func=AF.Square),
                  reads=xckeys, writes=[sqkey])

        def norm_b(l, c, xc_ap, xckeys, sq_ap, sqkey, gvec, sd_ap, sdkey):
            b = bank()
            gemm(PS[b][:, :], b, [(ones[:, :], sq_ap[:, k, :]) for k in range(KC)], [sqkey, ("ones",)])
            P.add("act", lambda e: e.activation(out=sd_ap, in_=PS[b][:, :], func=AF.Ln, scale=1.0 / D, bias=EPS),
                  reads=[("ps", b)], writes=[sdkey])
            P.add("act", lambda e: e.activation(out=sd_ap, in_=sd_ap, func=AF.Exp, scale=-0.5),
                  reads=[sdkey], writes=[sdkey])
            for k in range(KC):
                P.add("dve", (lambda k: lambda e: e.scalar_tensor_tensor(
                    out=hT[:, k, c * CH:(c + 1) * CH], in0=xc_ap[:, k, :], scalar=gvec[:, l, k:k + 1],
                    in1=sd_ap, op0=ALU.mult, op1=ALU.mult))(k),
                    reads=[xckeys[k], sdkey, ("vec",)], writes=[("hT", c, k)])

        def hk(c):
            return [("hT", c, k) for k in range(KC)]

        def chk(name):
            if stop == name:
                raise _Stop()

        def layer(l):
            P.epoch = l
            src = xT if l == 0 else xs
            srckey = "xT" if l == 0 else "xs"
            last = (l == depth - 1)
            dst = yT if last else xs
            dstkey = "yT" if last else "xs"

            xc = [V("xc%d" % i, OFF_AT + i * 16384, 16384, dt=F32, pat="p (k n) -> p k n", n=CH) for i in range(2)]
            sqx = [V("sqx%d" % i, OFF_AT + 32768 + i * 8192, 8192, pat="p (k n) -> p k n", n=CH) for i in range(2)]
            sdt = [V("sdt%d" % i, OFF_AT + 49152 + i * 2048, 2048, dt=F32) for i in range(2)]
            rst = [V("rst%d" % i, OFF_AT + 53248 + i * 2048, 2048, dt=F32) for i in range(2)]
            P.recycle(["xc0", "xc1", "sqx0", "sqx1", "sdt0", "sdt1", "rst0", "rst1"])
            for c in range(NCH):
                i = c % 2
                P.add("sp", (lambda c, i: lambda e: e.dma_start(out=xc[i], in_=src[c]))(c, i),
                    reads=[(srckey, c)], writes=[("xc%d" % i, o) for o in range(KC)], dma="xc%d" % i)
                xk = [("xc%d" % i, o) for o in range(KC)]
                norm_a(xc[i], xk, sqx[i], ("sqx%d" % i,))
                norm_b(l, c, xc[i], xk, sqx[i], ("sqx%d" % i,), g1, sdt[i], ("sdt%d" % i,))
            if l == 0 and "d_hT" in dbg_out:
                P.add("sp", lambda e: e.dma_start(out=dbg_out["d_hT"].rearrange("(k p) n -> p k n", p=128), in_=hT),
                      reads=[k_ for c in range(NCH) for k_ in hk(c)], writes=[("d_hT",)], dma="dbg")

            chk('P2a')
            fE = V("fE", OFF_C, 16384, p0=0, p1=8, dt=F32)
            fR = V("fR", OFF_C + 16384, 16384, p0=0, p1=8, dt=F32)
            cp4 = V("cp4", OFF_AT, 32768, p0=0, p1=8, pat="p (j n) -> p j n", n=S)
            wf = V("wf", OFF_E, 128, pat="p (k n) -> p k n", n=8)
            wv = V("wv", OFF_E + 128, 8192, pat="p (k n) -> p k n", n=512)
            P.recycle(["fE", "fR", "cp4", "wf", "wv"])
            wdma(wf, w_in[l, :, OF:OF + 8].rearrange("(k p) n -> p k n", p=128), ("wf",), "wf")
            wdma(wv, w_in[l, :, OV:OV + 512].rearrange("(k p) n -> p k n", p=128), ("wv",), "wv")
            for c in range(NCH):
                b = bank()
                gemm(PS[b][0:8, :], b, [(wf[:, k, :], hT[:, k, c * CH:(c + 1) * CH]) for k in range(KC)],
                     [("wf",)] + hk(c))
                P.add("act", (lambda c, b: lambda e: e.activation(
                    out=fE[:, c * CH:(c + 1) * CH], in_=PS[b][0:8, :], func=AF.Exp, scale=-1.0, bias=nfb[:, l:l + 1]))(c, b),
                    reads=[("ps", b), ("nfb",)], writes=[("fE", c)])
            P.add("act", lambda e: e.activation(out=fE, in_=fE, func=AF.Ln, scale=1.0, bias=1.0),
                  reads=[("fE", c) for c in range(NCH)], writes=[("fE", "ln")])
            P.add("dve", lambda e: e.tensor_tensor_scan(out=fR, data0=onef[:, 0:1].to_broadcast([8, S]), data1=fE,
                                                        initial=0.0, op0=ALU.mult, op1=ALU.subtract),
                  reads=[("fE", "ln"), ("onef",)], writes=[("fR",)])
            if l == 0 and "d_c" in dbg_out:
                P.add("sp", lambda e: e.dma_start(out=dbg_out["d_c"], in_=fR), reads=[("fR",)], writes=[("d_c",)], dma="dbg")
            P.add("dve", lambda e: e.tensor_copy(out=cp4[:, 0, :], in_=fR), reads=[("fR",)], writes=[("cp4", 0)])
            P.add("dve", lambda e: e.tensor_tensor(out=fE, in0=fR, in1=cp4[:, 0, :], op=ALU.subtract),
                  reads=[("fR",), ("cp4", 0)], writes=[("fE", "res")])
            P.add("dve", lambda e: e.tensor_copy(out=cp4[:, 1, :], in_=fE), reads=[("fE", "res")], writes=[("cp4", 1)])
            P.add("dve", lambda e: e.tensor_scalar(out=cp4[:, 2:4, :], in0=cp4[:, 0:2, :], scalar1=-1.0, scalar2=None,
                                                   op0=ALU.mult),
                  reads=[("cp4", 0), ("cp4", 1)], writes=[("cp4", 2)])
            P.add("sp", lambda e: e.dma_start(out=cpd, in_=cp4), reads=[("cp4", 0), ("cp4", 1), ("cp4", 2)],
                  writes=[("cpd",)], dma="cpd")

            chk('P2b')
            P.recycle(["VA"])
            P.add("pool", lambda e: e.memset(VA[:, :, :, 64:65], 1.0), writes=[("VA", "ones")])
            for tt in range(32):
                b = bank()
                gemm(PS[b][:, :], b, [(hT[:, k, tt * 128:(tt + 1) * 128], wv[:, k, :]) for k in range(KC)],
                     [("wv",)] + hk(tt // 4))
                P.add("dve", (lambda tt, b: lambda e: e.tensor_copy(
                    out=VA[:, tt, :, 0:64], in_=PS[b][:, :].rearrange("p (h d) -> p h d", d=64)))(tt, b),
                    reads=[("ps", b)], writes=[("VA", tt)])
            if l == 0 and "d_v" in dbg_out:
                P.add("sp", lambda e: e.dma_start(out=dbg_out["d_v"], in_=VA.rearrange("p t h d -> p (t h d)")),
                      reads=[("VA", tt) for tt in range(32)] + [("VA", "ones")], writes=[("d_v",)], dma="dbg")

            chk('P2c')
            E0 = OFF_E + 8320
            PT = [V("PT%d" % i, E0 + i * 1024, 1024) for i in range(4)]
            wqk = [V("wqk%d" % i, E0 + 4096 + i * 4096, 4096, pat="p (a k n) -> p a k n", a=2, n=128) for i in range(2)]
            sqh = [V("sqh%d" % i, E0 + 12288 + i * 1024, 1024) for i in range(2)]
            lnh = [V("lnh%d" % i, E0 + 14336 + i * 2048, 2048, dt=F32) for i in range(2)]
            recf = V("recf", E0 + 18432, 2048, p0=0, p1=65, dt=F32)
            rdh = V("rdh", E0 + 20480, 1024, p0=0, p1=65)
            rdl = V("rdl", E0 + 21504, 1024, p0=0, p1=65)
            bcs = [V("bcs%d" % i, E0 + 22528 + i * 2048, 2048, p0=0, p1=64, dt=F32) for i in range(2)]
            atm = [V("atm%d" % i, E0 + 26624 + i * 1024, 1024, p0=0, p1=64) for i in range(2)]
            qtm = [V("qtm%d" % i, E0 + 26624 + i * 1024, 1024, p0=64, p1=128) for i in range(2)]
            assert E0 + 28672 <= AR_BYTES
            P.recycle(["PT%d" % i for i in range(4)] + ["wqk0", "wqk1", "sqh0", "sqh1", "lnh0", "lnh1",
                                                       "recf", "rdh", "rdl", "bcs0", "bcs1", "atm0", "atm1", "qtm0", "qtm1"])
            P.recycle(["QA0", "QA1", "KA0", "KA1", "aT"])
            P.add("pool", lambda e: e.memset(rdh[:, :], 0.0), writes=[("rdh",)])
            P.add("pool", lambda e: e.memset(rdl[:, :], 0.0), writes=[("rdl",)])
            if USE_PAIR:
                P.add("pool", lambda e: e.memset(QA[0][64:68, :], 1.0), writes=[("QA0", "aug")])
                P.add("pool", lambda e: e.memset(KA[0][64:68, :], 1.0), writes=[("KA0", "aug")])
                P.add("pool", lambda e: e.memset(QA[1][0:64, :], 0.0), writes=[("QA1", "aug")])
                P.add("pool", lambda e: e.memset(KA[1][0:64, :], 0.0), writes=[("KA1", "aug")])
                P.add("pool", lambda e: e.memset(QA[1][32:34, :], 1.0), writes=[("QA1", "aug")])
                P.add("pool", lambda e: e.memset(KA[1][0:2, :], 1.0), writes=[("KA1", "aug")])
            else:
                for i in range(2):
                    P.add("pool", (lambda i: lambda e: e.memset(QA[i][64:68, :], 1.0))(i), writes=[("QA%d" % i, "aug")])
                    P.add("pool", (lambda i: lambda e: e.memset(KA[i][64:68, :], 1.0))(i), writes=[("KA%d" % i, "aug")])

            ucnt = [0]

            def load_wqk(p):
                pb_ = p % 2
                wdma(wqk[pb_][:, 0], w_in[l, :, OQ + p * 128:OQ + (p + 1) * 128].rearrange("(k p) n -> p k n", p=128),
                     ("wqk%d" % pb_, 0), "wq%d" % pb_)
                wdma(wqk[pb_][:, 1], w_in[l, :, OK_ + p * 128:OK_ + (p + 1) * 128].rearrange("(k p) n -> p k n", p=128),
                     ("wqk%d" % pb_, 1), "wk%d" % pb_)

            def proj_pair(p):
                pb_ = p % 2
                for hh, (qr, kr) in enumerate(((64, 66), (0, 32))):
                    h = 2 * p + hh
                    P.add("sp", (lambda hh, h, qr: lambda e: e.dma_start(out=QA[hh][qr:qr + 2, :], in_=cpd[h, 0:2, :]))(hh, h, qr),
                          reads=[("cpd",)], writes=[("QA%d" % hh, "aug")], dma="qaug%d" % hh)
                    P.add("sp", (lambda hh, h, kr: lambda e: e.dma_start(out=KA[hh][kr:kr + 2, :], in_=cpd[h, 2:4, :]))(hh, h, kr),
                          reads=[("cpd",)], writes=[("KA%d" % hh, "aug")], dma="kaug%d" % hh)
                units = [(c, a) for c in range(NCH) for a in range(2)]
                st_ = {}

                def stage_a(ui):
                    c, a = units[ui]
                    u = ucnt[0] % 2
                    ucnt[0] += 1
                    b = bank()
                    st_[ui] = (u, b)
                    gemm(PS[b][:, :], b, [(wqk[pb_][:, a, k, :], hT[:, k, c * CH:(c + 1) * CH]) for k in range(KC)],
                         [("wqk%d" % pb_, a)] + hk(c))
                    P.add("act", lambda e: e.activation(out=sqh[u], in_=PS[b][:, :], func=AF.Square),
                          reads=[("ps", b)], writes=[("sqh%d" % u,)])

                def stage_b(ui):
                    c, a = units[ui]
                    u, b = st_[ui]
                    b2 = bank()
                    mm(PS[b2][:, :], ("ps", b2), bd[:, :], sqh[u], True, True, [("sqh%d" % u,), ("bd",)])
                    P.add("act", lambda e: e.activation(out=lnh[u], in_=PS[b2][:, :], func=AF.Ln, scale=1.0 / 64, bias=EPS),
                          reads=[("ps", b2)], writes=[("lnh%d" % u,)])
                    P.add("act", lambda e: e.activation(out=lnh[u], in_=lnh[u], func=AF.Exp, scale=-0.5),
                          reads=[("lnh%d" % u,)], writes=[("lnh%d" % u,)])
                    dt_ = (QA, KA)[a]
                    dn = ("QA", "KA")[a]
                    gv = (gq2, gk8)[a]
                    P.add("dve", lambda e: e.scalar_tensor_tensor(
                        out=dt_[0][0:64, c * CH:(c + 1) * CH], in0=PS[b][0:64, :], scalar=gv[0:64, l:l + 1],
                        in1=lnh[u][0:64, :], op0=ALU.mult, op1=ALU.mult),
                        reads=[("ps", b), ("lnh%d" % u,), ("vec",), ("gk8",)], writes=[(dn + "0", c)])
                    P.add("dve", lambda e: e.scalar_tensor_tensor(
                        out=dt_[1][64:128, c * CH:(c + 1) * CH], in0=PS[b][64:128, :], scalar=gv[64:128, l:l + 1],
                        in1=lnh[u][64:128, :], op0=ALU.mult, op1=ALU.mult),
                        reads=[("ps", b), ("lnh%d" % u,), ("vec",), ("gk8",)], writes=[(dn + "1", c)])

                nU = len(units)
                stage_a(0)
                for ui in range(nU):
                    if ui + 1 < nU:
                        stage_a(ui + 1)
                    stage_b(ui)

            def proj_head(h, pb_):
                hh = h % 2
                units = [(c, a) for c in range(NCH) for a in range(2)]
                st_ = {}

                def stage_a(ui):
                    c, a = units[ui]
                    u = ucnt[0] % 2
                    ucnt[0] += 1
                    b = bank()
                    st_[ui] = (u, b)
                    gemm(PS[b][0:64, :], b, [(wqk[pb_][:, a, k, hh * 64:(hh + 1) * 64], hT[:, k, c * CH:(c + 1) * CH]) for k in range(KC)],
                         [("wqk%d" % pb_, a)] + hk(c))
                    P.add("act", lambda e: e.activation(out=sqh[u][0:64, :], in_=PS[b][0:64, :], func=AF.Square),
                          reads=[("ps", b)], writes=[("sqh%d" % u,)])

                def stage_b(ui):
                    c, a = units[ui]
                    u, b = st_[ui]
                    b2 = bank()
                    mm(PS[b2][0:64, :], ("ps", b2), ones[0:64, 0:64], sqh[u][0:64, :], True, True, [("sqh%d" % u,), ("ones",)])
                    P.add("act", lambda e: e.activation(out=lnh[u][0:64, :], in_=PS[b2][0:64, :], func=AF.Ln, scale=1.0 / 64, bias=EPS),
                          reads=[("ps", b2)], writes=[("lnh%d" % u,)])
                    P.add("act", lambda e: e.activation(out=lnh[u][0:64, :], in_=lnh[u][0:64, :], func=AF.Exp, scale=-0.5),
                          reads=[("lnh%d" % u,)], writes=[("lnh%d" % u,)])
                    dt_ = (QA, KA)[a]
                    dn = ("QA", "KA")[a]
                    gv = (gq2, gk8)[a]
                    P.add("dve", lambda e: e.scalar_tensor_tensor(
                        out=dt_[hh][0:64, c * CH:(c + 1) * CH], in0=PS[b][0:64, :], scalar=gv[0:64, l:l + 1],
                        in1=lnh[u][0:64, :], op0=ALU.mult, op1=ALU.mult),
                        reads=[("ps", b), ("lnh%d" % u,), ("vec",), ("gk8",)], writes=[(dn + "%d" % hh, c)])

                nU = len(units)
                stage_a(0)
                for ui in range(nU):
                    if ui + 1 < nU:
                        stage_a(ui + 1)
                    stage_b(ui)

            def aug_rows(p):
                for hh in range(2):
                    h = 2 * p + hh
                    P.add("sp", (lambda hh, h: lambda e: e.dma_start(out=QA[hh][64:66, :], in_=cpd[h, 0:2, :]))(hh, h),
                          reads=[("cpd",)], writes=[("QA%d" % hh, "aug")], dma="qaug%d" % hh)
                    P.add("sp", (lambda hh, h: lambda e: e.dma_start(out=KA[hh][66:68, :], in_=cpd[h, 2:4, :]))(hh, h),
                          reads=[("cpd",)], writes=[("KA%d" % hh, "aug")], dma="kaug%d" % hh)

            ocnt = [0]

            def attn_head(h):
                hb = h % 2
                qa, ka = QA[hb], KA[hb]
                qn, kn = "QA%d" % hb, "KA%d" % hb
                tiles = []
                for c in range(NCH):
                    for j in range(4 * c + 4):
                        tiles.append((c, j))
                n = len(tiles)
                state = {}
                pending = []

                def emit_S(i):
                    c, j = tiles[i]
                    n0 = max(0, j - 4 * c) * 128
                    diag = j >= 4 * c
                    sbk = i % 4
                    rd = [(qn, c), (kn, j // 4), (qn, "aug"), (kn, "aug")]
                    kk = 128 if (USE_PAIR and hb == 1) else 68
                    mm(PS[sbk][:, n0:CH], ("ps", sbk), ka[0:kk, j * 128:(j + 1) * 128],
                       qa[0:kk, c * CH + n0:(c + 1) * CH], True, not diag, rd)
                    if diag:
                        mm(PS[sbk][:, n0:n0 + 128], ("ps", sbk), ident[:, :], maskb[:, :], False, True,
                           [("ident",), ("maskb",)])
                    P.add("act", lambda e: e.activation(out=PT[sbk][:, n0:CH], in_=PS[sbk][:, n0:CH], func=AF.Exp),
                          reads=[("ps", sbk)], writes=[("PT%d" % sbk,)])

                def emit_PV(i):
                    c, j = tiles[i]
                    n0 = max(0, j - 4 * c) * 128
                    sbk = i % 4
                    lastj = 4 * c + 3
                    if j == 0:
                        state["ob"] = (4, 5, 7)[ocnt[0] % 3]
                        ocnt[0] += 1
                    ob = state["ob"]
                    mm(PS[ob][0:65, n0:CH], ("ps", ob), VA[:, j, h, 0:65], PT[sbk][:, n0:CH], j == 0, j == lastj,
                       [("VA", j), ("VA", "ones"), ("PT%d" % sbk,)])
                    if j == lastj:
                        u = c % 2
                        P.add("dve", lambda e: e.tensor_copy(out=rdh[64:65, :], in_=PS[ob][64:65, :]),
                              reads=[("ps", ob)], writes=[("rdh",)])
                        P.add("dve", lambda e: e.tensor_tensor(out=rdl[64:65, :], in0=PS[ob][64:65, :], in1=rdh[64:65, :],
                                                               op=ALU.subtract),
                              reads=[("ps", ob), ("rdh",)], writes=[("rdl",)])

                        def part2():
                            mm(PS[6][0:64, :], ("ps", 6), sel[:, :], rdh[:, :], True, False, [("sel",), ("rdh",)])
                            mm(PS[6][0:64, :], ("ps", 6), sel[:, :], rdl[:, :], False, True, [("sel",), ("rdl",)])
                            P.add("act", lambda e: e.activation(out=bcs[u], in_=PS[6][0:64, :], func=AF.Copy),
                                  reads=[("ps", 6)], writes=[("bcs%d" % u,)])
                            P.add("dve", lambda e: e.reciprocal(out=bcs[u], in_=bcs[u]),
                                  reads=[("bcs%d" % u,)], writes=[("bcs%d" % u,)])
                            if h % 2 == 0:
                                P.add("dve", lambda e: e.tensor_tensor(out=aT[0:64, h // 2, c * CH:(c + 1) * CH],
                                                                       in0=PS[ob][0:64, :], in1=bcs[u], op=ALU.mult),
                                      reads=[("ps", ob), ("bcs%d" % u,)], writes=[("aT", h // 2, c, 0)])
                            else:
                                P.add("dve", lambda e: e.tensor_tensor(out=atm[u], in0=PS[ob][0:64, :], in1=bcs[u], op=ALU.mult),
                                      reads=[("ps", ob), ("bcs%d" % u,)], writes=[("atm%d" % u,)])
                                P.add("sp", lambda e: e.dma_start(out=aT[64:128, h // 2, c * CH:(c + 1) * CH], in_=atm[u]),
                                      reads=[("atm%d" % u,)], writes=[("aT", h // 2, c, 1)], dma="atm%d" % u)
                        pending.append((i + 4, part2))

                LA = 3
                for i in range(min(LA, n)):
                    emit_S(i)
                for i in range(n):
                    if i + LA < n:
                        emit_S(i + LA)
                    emit_PV(i)
                    while pending and pending[0][0] <= i:
                        pending.pop(0)[1]()
                while pending:
                    pending.pop(0)[1]()

            pscnt[0] = 0
            load_wqk(0)
            for p in range(4):
                if p + 1 < 4:
                    load_wqk(p + 1)
                if USE_PAIR:
                    proj_pair(p)
                else:
                    aug_rows(p)
                    proj_head(2 * p, p % 2)
                    proj_head(2 * p + 1, p % 2)
                attn_head(2 * p)
                attn_head(2 * p + 1)
            if l == 0 and "d_aT" in dbg_out:
                P.add("sp", lambda e: e.dma_start(out=dbg_out["d_aT"].rearrange("(k p) n -> p k n", p=128), in_=aT),
                      reads=[("aT", k, c, q) for k in range(4) for c in range(NCH) for q in range(2)],
                      writes=[("d_aT",)], dma="dbg")

            chk('P2d')
            NPB = 16448
            ub = V("ub", OFF_D, NPB, dt=F32)
            pa = V("pa", OFF_D + NPB, NPB, dt=F32)
            pb = V("pb", OFF_D + 2 * NPB, NPB, dt=F32)
            wpx = [V("wpx%d" % i, OFF_D + 3 * NPB + i * 2048, 2048, pat="p (k n) -> p k n", n=128) for i in range(2)]
            dT = V("dT", OFF_C + 16384, 16384, pat="p (k n) -> p k n", n=S)
            cbz = V("cbz", OFF_C, 16384, pat="p (k n) -> p k n", n=S)
            assert OFF_D + 3 * NPB + 4096 <= AR_BYTES
            P.recycle(["ub", "pa", "pb", "wpx0", "wpx1", "dT", "cbz"])
            for t_, nm in ((ub, "ub"), (pa, "pa"), (pb, "pb")):
                P.add("pool", (lambda t_: lambda e: e.memset(t_[:, 0:16], 0.0))(t_), writes=[(nm, "halo")])
            H = 16
            for blk in range(2):
                wdma(wpx[blk], w_in[l, :, OPX + blk * 128:OPX + (blk + 1) * 128].rearrange("(k p) n -> p k n", p=128),
                     ("wpx%d" % blk,), "wpx%d" % blk)
            for blk in range(2):
                for c in range(NCH):
                    b = bank()
                    gemm(PS[b][:, :], b, [(wpx[blk][:, k, :], hT[:, k, c * CH:(c + 1) * CH]) for k in range(KC)],
                         [("wpx%d" % blk,)] + hk(c))
                    P.add("act", (lambda b, c: lambda e: e.activation(out=ub[:, H + c * CH:H + (c + 1) * CH], in_=PS[b][:, :],
                                                                       func=AF.Copy))(b, c),
                          reads=[("ps", b)], writes=[("ub", c)])
                allu = [("ub", c) for c in range(NCH)] + [("ub", "halo")]
                P.add("dve", lambda e: e.tensor_tensor(out=pa[:, H:H + S], in0=ub[:, H:H + S], in1=ub[:, H - 1:H - 1 + S], op=ALU.add),
                      reads=allu + [("pa", "halo")], writes=[("pa", "w")])
                if blk == 0:
                    P.add("dve", lambda e: e.tensor_tensor(out=pb[64:128, H:H + S], in0=pa[64:128, H:H + S],
                                                           in1=pa[64:128, H - 2:H - 2 + S], op=ALU.add),
                          reads=[("pa", "w"), ("pa", "halo"), ("pb", "halo")], writes=[("pb", "w")])
                    resA, resB = pa, pb
                    rk = [("pa", "w"), ("pb", "w")]
                else:
                    P.add("dve", lambda e: e.tensor_tensor(out=pb[:, H:H + S], in0=pa[:, H:H + S], in1=pa[:, H - 2:H - 2 + S], op=ALU.add),
                          reads=[("pa", "w"), ("pa", "halo"), ("pb", "halo")], writes=[("pb", "w")])
                    P.add("dve", lambda e: e.tensor_tensor(out=pa[:, H:H + S], in0=pb[:, H:H + S], in1=pb[:, H - 4:H - 4 + S], op=ALU.add),
                          reads=[("pb", "w"), ("pb", "halo"), ("pa", "halo")], writes=[("pa", "w")])
                    P.add("dve", lambda e: e.tensor_tensor(out=pb[64:128, H:H + S], in0=pa[64:128, H:H + S],
                                                           in1=pa[64:128, H - 8:H - 8 + S], op=ALU.add),
                          reads=[("pa", "w"), ("pa", "halo"), ("pb", "w")], writes=[("pb", "w2")])
                    resA, resB = pa, pb
                    rk = [("pa", "w"), ("pb", "w2"), ("pb", "w")]
                for (r_, p0, p1) in ((resA, 0, 64), (resB, 64, 128)):
                    P.add("dve", (lambda r_, p0, p1, blk: lambda e: e.scalar_tensor_tensor(
                        out=dT[p0:p1, blk, :], in0=r_[p0:p1, H:H + S], scalar=invw[p0:p1, blk:blk + 1],
                        in1=ub[p0:p1, H:H + S], op0=ALU.mult, op1=ALU.subtract))(r_, p0, p1, blk),
                        reads=rk + allu + [("cst",)], writes=[("dT", blk, p0)])
                    P.add("dve", (lambda r_, p0, p1, blk: lambda e: e.tensor_tensor(
                        out=r_[p0:p1, H:H + 16], in0=r_[p0:p1, H:H + 16], in1=invtab[blk][p0:p1, :], op=ALU.mult))(r_, p0, p1, blk),
                        reads=rk + [("dT", blk, p0), ("cst",)], writes=[("ptmp", blk, p0)])
                    P.add("dve", (lambda r_, p0, p1, blk: lambda e: e.tensor_tensor(
                        out=dT[p0:p1, blk, 0:16], in0=r_[p0:p1, H:H + 16], in1=ub[p0:p1, H:H + 16], op=ALU.subtract))(r_, p0, p1, blk),
                        reads=[("ptmp", blk, p0)] + allu, writes=[("dT", blk, p0)])
                if blk == 0:
                    for nm in ("pa", "pb"):
                        pass
            if l == 0 and "d_dT" in dbg_out:
                P.add("sp", lambda e: e.dma_start(out=dbg_out["d_dT"].rearrange("(k p) n -> p k n", p=128), in_=dT),
                      reads=[("dT", b_, p_) for b_ in range(2) for p_ in (0, 64)], writes=[("d_dT",)], dma="dbg")

            chk('P2e')
            NZ = 16400
            zb = V("zb", OFF_D, NZ, dt=F32)
            cxs = [V("cxs%d" % i, OFF_D + NZ + i * 2048, 2048, dt=F32) for i in range(2)]
            acc = [V("acc%d" % i, OFF_D + NZ + 4096 + i * 2048, 2048, dt=F32) for i in range(2)]
            wc3 = [V("wc3%d" % i, OFF_D + NZ + 8192 + i * 6144, 6144, pat="p (k a n) -> p k a n", a=3, n=128) for i in range(2)]
            P.recycle(["zb", "cxs0", "cxs1", "acc0", "acc1", "wc30", "wc31"])
            P.add("pool", lambda e: e.memset(zb[:, 0:2], 0.0), writes=[("zb", "halo")])
            for blk in range(2):
                for a, off in enumerate((OCX, OCB, OCC)):
                    wdma(wc3[blk][:, :, a, :], w_in[l, :, off + blk * 128:off + (blk + 1) * 128].rearrange("(k p) n -> p k n", p=128),
                         ("wc3%d" % blk, a), "wc3%d_%d" % (blk, a))
            for blk in range(2):
                for c in range(NCH):
                    u = c % 2
                    bx, bb, bc_ = bank(), bank(), bank()
                    for a, b in ((0, bx), (1, bb), (2, bc_)):
                        gemm(PS[b][:, :], b, [(wc3[blk][:, k, a, :], hT[:, k, c * CH:(c + 1) * CH]) for k in range(KC)],
                             [("wc3%d" % blk, a)] + hk(c))
                    P.add("act", (lambda bx, u: lambda e: e.activation(out=cxs[u], in_=PS[bx][:, :], func=AF.Copy))(bx, u),
                          reads=[("ps", bx)], writes=[("cxs%d" % u,)])
                    P.add("dve", (lambda bc_, u, c: lambda e: e.tensor_tensor(out=zb[:, 2 + c * CH:2 + (c + 1) * CH], in0=PS[bc_][:, :],
                                                                              in1=cxs[u], op=ALU.mult))(bc_, u, c),
                          reads=[("ps", bc_), ("cxs%d" % u,)], writes=[("zb", c)])
                    zr = [("zb", c), ("zb", "halo")] + ([("zb", c - 1)] if c > 0 else [])
                    P.add("dve", (lambda u, c, blk: lambda e: e.tensor_scalar(out=acc[u], in0=zb[:, c * CH:(c + 1) * CH],
                                                                              scalar1=cw[:, l, 0, blk:blk + 1], scalar2=None, op0=ALU.mult))(u, c, blk),
                          reads=zr + [("vec",)], writes=[("acc%d" % u,)])
                    for jj in (1, 2):
                        P.add("dve", (lambda u, c, blk, jj: lambda e: e.scalar_tensor_tensor(
                            out=acc[u], in0=zb[:, jj + c * CH:jj + (c + 1) * CH], scalar=cw[:, l, jj, blk:blk + 1],
                            in1=acc[u], op0=ALU.mult, op1=ALU.add))(u, c, blk, jj),
                            reads=zr + [("acc%d" % u,), ("vec",)], writes=[("acc%d" % u,)])
                    P.add("dve", (lambda u, c, blk, bb: lambda e: e.tensor_tensor(out=cbz[:, blk, c * CH:(c + 1) * CH], in0=acc[u],
                                                                                  in1=PS[bb][:, :], op=ALU.mult))(u, c, blk, bb),
                          reads=[("acc%d" % u,), ("ps", bb)], writes=[("cbz", blk, c)])
            if l == 0 and "d_cbz" in dbg_out:
                P.add("sp", lambda e: e.dma_start(out=dbg_out["d_cbz"].rearrange("(k p) n -> p k n", p=128), in_=cbz),
                      reads=[("cbz", b_, c_) for b_ in range(2) for c_ in range(NCH)], writes=[("d_cbz",)], dma="dbg")

            chk('P3a')
            WS = 7936
            wg = [V("wg%d" % i, OFF_D + i * WS, 6144, pat="p (k a n) -> p k a n", a=3, n=128) for i in range(2)]
            wao = [V("wao%d" % i, OFF_D + i * WS + 6144, 1024, pat="p (k n) -> p k n", n=128) for i in range(2)]
            wco = [V("wco%d" % i, OFF_D + i * WS + 7168, 512, pat="p (k n) -> p k n", n=128) for i in range(2)]
            wpo = [V("wpo%d" % i, OFF_D + i * WS + 7680, 256) for i in range(2)]
            T0 = OFF_D + 2 * WS
            sg = [V("sg%d" % i, T0 + i * 2048, 2048, dt=F32) for i in range(2)]
            mt = [V("mt%d" % i, T0 + 4096 + i * 2048, 2048, dt=F32) for i in range(2)]
            tt_ = [V("tt%d" % i, T0 + 8192 + i * 2048, 2048, dt=F32) for i in range(2)]
            mb = [V("mb%d" % i, T0 + 12288 + i * 1024, 1024) for i in range(2)]
            wo = V("wo", T0 + 14336, 16384, pat="p (k n) -> p k n", n=D)
            sd2 = [V("sd2%d" % i, T0 + 30720 + i * 2048, 2048, dt=F32) for i in range(2)]
            rs2 = [V("rs2%d" % i, T0 + 34816 + i * 2048, 2048, dt=F32) for i in range(2)]
            assert T0 + 38912 <= AR_BYTES
            P.recycle(["wg0", "wg1", "wao0", "wao1", "wco0", "wco1", "wpo0", "wpo1", "sg0", "sg1", "mt0", "mt1",
                       "tt0", "tt1", "mb0", "mb1", "wo", "sd20", "sd21", "rs20", "rs21"])

            def load_wset(j):
                i = j % 2
                for a in range(3):
                    wdma(wg[i][:, :, a, :],
                         w_in[l, :, OG + a * D + j * 128:OG + a * D + (j + 1) * 128].rearrange("(k p) n -> p k n", p=128),
                         ("wg%d" % i, a), "wg%d_%d" % (i, a))
                wdma(wao[i], w_ao[l, :, j * 128:(j + 1) * 128].rearrange("(k p) n -> p k n", p=128), ("wao%d" % i,), "wao%d" % i)
                wdma(wco[i], w_co[l, :, j * 128:(j + 1) * 128].rearrange("(k p) n -> p k n", p=128), ("wco%d" % i,), "wco%d" % i)
                g = j // 2
                r0 = (g % 2) * 64
                wdma(wpo[i][r0:r0 + 64, :], w_po[l, g, :, (j % 2) * 128:(j % 2 + 1) * 128], ("wpo%d" % i,), "wpo%d" % i)

            load_wset(0)
            ucnt2 = [0]
            for j in range(8):
                i = j % 2
                if j + 1 < 8:
                    load_wset(j + 1)
                if j == 1:
                    wdma(wo, w_o[l].rearrange("(k p) n -> p k n", p=128), ("wo",), "wo")
                g = j // 2
                r0 = (g % 2) * 64
                for c in range(NCH):
                    u = ucnt2[0] % 2
                    ucnt2[0] += 1
                    cs = slice(c * CH, (c + 1) * CH)
                    bg, by = bank(), bank()
                    gemm(PS[bg][:, :], bg, [(wg[i][:, k, 0, :], hT[:, k, cs]) for k in range(KC)], [("wg%d" % i, 0)] + hk(c))
                    gemm(PS[by][:, :], by, [(wao[i][:, k, :], aT[:, k, cs]) for k in range(4)],
                         [("wao%d" % i,)] + [("aT", k, c, q) for k in range(4) for q in range(2)])
                    P.add("act", (lambda bg, u: lambda e: e.activation(out=sg[u], in_=PS[bg][:, :], func=AF.Sigmoid))(bg, u),
                          reads=[("ps", bg)], writes=[("sg%d" % u,)])
                    P.add("dve", (lambda by, u: lambda e: e.tensor_tensor(out=mt[u], in0=sg[u], in1=PS[by][:, :], op=ALU.mult))(by, u),
                          reads=[("ps", by), ("sg%d" % u,)], writes=[("mt%d" % u,)])
                    bg, by = bank(), bank()
                    gemm(PS[bg][:, :], bg, [(wg[i][:, k, 1, :], hT[:, k, cs]) for k in range(KC)], [("wg%d" % i, 1)] + hk(c))
                    gemm(PS[by][:, :], by, [(wco[i][:, k, :], cbz[:, k, cs]) for k in range(2)],
                         [("wco%d" % i,), ("cbz", 0, c), ("cbz", 1, c)])
                    P.add("act", (lambda bg, u: lambda e: e.activation(out=sg[u], in_=PS[bg][:, :], func=AF.Sigmoid))(bg, u),
                          reads=[("ps", bg)], writes=[("sg%d" % u,)])
                    P.add("dve", (lambda by, u: lambda e: e.tensor_tensor(out=tt_[u], in0=sg[u], in1=PS[by][:, :], op=ALU.mult))(by, u),
                          reads=[("ps", by), ("sg%d" % u,)], writes=[("tt%d" % u,)])
                    P.add("pool", (lambda u: lambda e: e.tensor_tensor(out=mt[u], in0=mt[u], in1=tt_[u], op=ALU.add))(u),
                          reads=[("mt%d" % u,), ("tt%d" % u,)], writes=[("mt%d" % u,)])
                    bg, by = bank(), bank()
                    gemm(PS[bg][:, :], bg, [(wg[i][:, k, 2, :], hT[:, k, cs]) for k in range(KC)], [("wg%d" % i, 2)] + hk(c))
                    mm(PS[by][:, :], ("ps", by), wpo[i][r0:r0 + 64, :], dT[r0:r0 + 64, g // 2, cs], True, True,
                       [("wpo%d" % i,), ("dT", g // 2, r0)])
                    P.add("act", (lambda bg, u: lambda e: e.activation(out=sg[u], in_=PS[bg][:, :], func=AF.Sigmoid))(bg, u),
                          reads=[("ps", bg)], writes=[("sg%d" % u,)])
                    P.add("dve", (lambda by, u, j: lambda e: e.scalar_tensor_tensor(out=tt_[u], in0=PS[by][:, :], scalar=psc[:, l, j:j + 1],
                                                                                    in1=sg[u], op0=ALU.mult, op1=ALU.mult))(by, u, j),
                          reads=[("ps", by), ("sg%d" % u,), ("vec",)], writes=[("tt%d" % u,)])
                    P.add("pool", (lambda u: lambda e: e.tensor_tensor(out=mb[u], in0=mt[u], in1=tt_[u], op=ALU.add))(u),
                          reads=[("mt%d" % u,), ("tt%d" % u,)], writes=[("mb%d" % u,)])
                    P.add("sp", (lambda u, j, c: lambda e: e.dma_start(out=mT[c, :, j, :], in_=mb[u]))(u, j, c),
                          reads=[("mb%d" % u,)], writes=[("mT", j, c)], dma="mb%d" % u)

            chk('P3b')
            mc = [V("mc%d" % i, OFF_AT + 49152 + i * 8192, 8192, pat="p (k n) -> p k n", n=CH) for i in range(2)]
            assert OFF_AT + 49152 + 16384 <= OFF_D
            P.recycle(["xc0", "xc1", "sqx0", "sqx1", "mc0", "mc1"])
            def p3b_tail(c):
                i = c % 2
                xk = [("xc%d" % i, o) for o in range(KC)]
                norm_b(l, c, xc[i], xk, sqx[i], ("sqx%d" % i,), g2, sd2[i], ("sd2%d" % i,))

            def p3b_load(c):
                i = c % 2
                P.add("sp", (lambda c, i: lambda e: e.dma_start(out=xc[i], in_=src[c]))(c, i),
                      reads=[(srckey, c)], writes=[("xc%d" % i, o) for o in range(KC)], dma="xc%d" % i)
                P.add("sp", (lambda c, i: lambda e: e.dma_start(
                    out=mc[i], in_=mT[c]))(c, i),
                    reads=[("mT", j, c) for j in range(8)], writes=[("mc%d" % i,)], dma="mc%d" % i)

            p3b_load(0)
            for c in range(NCH):
                i = c % 2
                xk = [("xc%d" % i, o) for o in range(KC)]
                for o in range(8):
                    b = bank()
                    gemm(PS[b][:, :], b, [(wo[:, k, o * 128:(o + 1) * 128], mc[i][:, k, :]) for k in range(KC)],
                         [("wo",), ("mc%d" % i,)])
                    P.add("dve", (lambda b, i, o: lambda e: e.tensor_tensor(out=xc[i][:, o, :], in0=xc[i][:, o, :], in1=PS[b][:, :],
                                                                            op=ALU.add))(b, i, o),
                          reads=[("ps", b), ("xc%d" % i, o)], writes=[("xc%d" % i, o)])
                    if o == 1:
                        if c >= 1:
                            p3b_tail(c - 1)
                        if c + 1 < NCH:
                            p3b_load(c + 1)
                P.add("sp", (lambda c, i: lambda e: e.dma_start(out=xs[c], in_=xc[i]))(c, i),
                      reads=xk, writes=[("xs", c)], dma="xst%d" % i)
                norm_a(xc[i], xk, sqx[i], ("sqx%d" % i,))
            p3b_tail(NCH - 1)

            chk('P4')
            HS = S // 2
            uT = V("uT", OFF_AT, NHB * HS * 2, pat="p (k n) -> p k n", n=HS)
            F0 = OFF_AT + NHB * HS * 2
            wgu = [V("wgu%d" % i, F0 + i * 4096, 4096, pat="p (k a n) -> p k a n", a=2, n=128) for i in range(2)]
            wfo = [V("wfo%d" % i, F0 + 8192 + i * 5632, 5632, pat="p (k n) -> p k n", n=128) for i in range(2)]
            xo = [V("xo%d" % i, F0 + 19456 + i * 8192, 8192, dt=F32) for i in range(2)]
            sgl = [V("sgl%d" % i, F0 + 35840 + i * 2048, 2048, dt=F32) for i in range(2)]
            assert F0 + 39936 <= AR_BYTES
            P.recycle(["uT", "wgu0", "wgu1", "wfo0", "wfo1", "xo0", "xo1", "sgl0", "sgl1"])
            wcnt = [0]
            ocn = [0]
            for half in range(2):
                for hbk in range(NHB):
                    i = wcnt[0] % 2
                    wcnt[0] += 1
                    wdma(wgu[i][:, :, 0, :], w_fi[l, :, hbk * 128:(hbk + 1) * 128].rearrange("(k p) n -> p k n", p=128),
                         ("wgu%d" % i, 0), "wgu%d_0" % i)
                    wdma(wgu[i][:, :, 1, :], w_fi[l, :, D_FF + hbk * 128:D_FF + (hbk + 1) * 128].rearrange("(k p) n -> p k n", p=128),
                         ("wgu%d" % i, 1), "wgu%d_1" % i)
                    for cc in range(4):
                        c = half * 4 + cc
                        cs = slice(c * CH, (c + 1) * CH)
                        u = (hbk * 4 + cc) % 2
                        bg, bu = bank(), bank()
                        gemm(PS[bg][:, :], bg, [(wgu[i][:, k, 0, :], hT[:, k, cs]) for k in range(KC)], [("wgu%d" % i, 0)] + hk(c))
                        gemm(PS[bu][:, :], bu, [(wgu[i][:, k, 1, :], hT[:, k, cs]) for k in range(KC)], [("wgu%d" % i, 1)] + hk(c))
                        P.add("act", (lambda bg, u: lambda e: e.activation(out=sgl[u], in_=PS[bg][:, :], func=AF.Sigmoid))(bg, u),
                              reads=[("ps", bg)], writes=[("sgl%d" % u,)])
                        P.add("dve", (lambda bg, u: lambda e: e.tensor_tensor(out=sgl[u], in0=sgl[u], in1=PS[bg][:, :], op=ALU.mult))(bg, u),
                              reads=[("ps", bg), ("sgl%d" % u,)], writes=[("sgl%d" % u,)])
                        P.add("dve", (lambda bu, u, hbk, cc: lambda e: e.tensor_tensor(out=uT[:, hbk, cc * CH:(cc + 1) * CH], in0=sgl[u],
                                                                                       in1=PS[bu][:, :], op=ALU.mult))(bu, u, hbk, cc),
                              reads=[("ps", bu), ("sgl%d" % u,)], writes=[("uT", hbk, cc)])
                for o in range(8):
                    i = ocn[0] % 2
                    ocn[0] += 1
                    wdma(wfo[i], w_fo[l, :, o * 128:(o + 1) * 128].rearrange("(k p) n -> p k n", p=128), ("wfo%d" % i,), "wfo%d" % i)
                    P.add("sp", (lambda o, i, half: lambda e: e.dma_start(
                        out=xo[i].rearrange("p (c n) -> p c n", n=CH),
                        in_=xs[half * 4:(half + 1) * 4, :, o, :].rearrange("c p n -> p c n")))(o, i, half),
                        reads=[("xs", half * 4 + cc) for cc in range(4)], writes=[("xo%d" % i, cc) for cc in range(4)], dma="xo%d" % i)
                    for cc in range(4):
                        b = bank()
                        gemm(PS[b][:, :], b, [(wfo[i][:, k, :], uT[:, k, cc * CH:(cc + 1) * CH]) for k in range(NHB)],
                             [("wfo%d" % i,)] + [("uT", k, cc) for k in range(NHB)])
                        P.add("dve", (lambda b, i, cc: lambda e: e.tensor_tensor(out=xo[i][:, cc * CH:(cc + 1) * CH],
                                                                                 in0=xo[i][:, cc * CH:(cc + 1) * CH], in1=PS[b][:, :],
                                                                                 op=ALU.add))(b, i, cc),
                              reads=[("ps", b), ("xo%d" % i, cc)], writes=[("xo%d" % i, cc)])
                    P.add("sp", (lambda o, i, half: lambda e: e.dma_start(
                        out=dst[half * 4:(half + 1) * 4, :, o, :].rearrange("c p n -> p c n"),
                        in_=xo[i].rearrange("p (c n) -> p c n", n=CH)))(o, i, half),
                        reads=[("xo%d" % i, cc) for cc in range(4)], writes=[(dstkey + "_o", o, half)], dma="xot%d" % i)
                P.marker(reads=[(dstkey + "_o", o, half) for o in range(8)],
                         writes=[(dstkey, half * 4 + cc) for cc in range(4)])
        try:
            for l_ in range(depth):
                layer(l_)
        except _Stop:
            pass
        P.add("sp", None, reads=[("yT_o", o, half) for o in range(8) for half in range(2)] + [(k,) for k in ("d_hT", "d_aT", "d_cbz", "d_dT", "d_c", "d_v")])
        P.emit(nc, st)
    return nc, P


def _consts():
    c = np.zeros((128, NCST), np.float32)
    c[:, 0:128] = np.eye(128, dtype=np.float32)
    s = np.arange(128)[:, None]
    t = np.arange(128)[None, :]
    c[:, 128:256] = np.where(s <= t, 0.0, -30000.0)
    tt = np.arange(16, dtype=np.float32)
    for blk, (wa, wb) in enumerate(((2, 4), (8, 16))):
        c[0:64, 256 + blk * 16:272 + blk * 16] = 1.0 / np.minimum(tt + 1.0, wa)
        c[64:128, 256 + blk * 16:272 + blk * 16] = 1.0 / np.minimum(tt + 1.0, wb)
        c[0:64, 288 + blk] = 1.0 / wa
        c[64:128, 288 + blk] = 1.0 / wb
    return c


def _vecs(norm_mix_g, norm_ffn_g, pool_scale, conv_w, q_norm_g, k_norm_g, forget_b):
    v = np.zeros((128, NVEC), np.float32)
    f = lambda a: np.asarray(a, np.float32).reshape(4, 8, 128).transpose(2, 0, 1).reshape(128, 32)
    v[:, 0:32] = f(norm_mix_g)
    v[:, 32:64] = f(norm_ffn_g)
    v[:, 64:96] = f(pool_scale)
    v[:, 96:120] = np.asarray(conv_w, np.float32).reshape(4, 3, 2, 128).transpose(3, 0, 1, 2).reshape(128, 24)
    v[0:64, 120:124] = np.asarray(q_norm_g, np.float32).T
    v[64:128, 120:124] = np.asarray(q_norm_g, np.float32).T
    v[0:64, 124:128] = np.asarray(k_norm_g, np.float32).T
    v[64:128, 124:128] = np.asarray(k_norm_g, np.float32).T
    v[0:8, 128:132] = np.asarray(forget_b, np.float32).T
    return v


_NC_CACHE = {}


def make_in_maps(inputs, cores):
    x = np.asarray(inputs["x"], np.float32)
    shared = {
        "w_in": np.ascontiguousarray(inputs["w_in"], np.float32),
        "w_attn_out": np.ascontiguousarray(inputs["w_attn_out"], np.float32),
        "w_conv_out": np.ascontiguousarray(inputs["w_conv_out"], np.float32),
        "pool_w": np.ascontiguousarray(inputs["pool_w"], np.float32),
        "w_o": np.ascontiguousarray(inputs["w_o"], np.float32),
        "w_ffn_in": np.ascontiguousarray(inputs["w_ffn_in"], np.float32),
        "w_ffn_out": np.ascontiguousarray(inputs["w_ffn_out"], np.float32),
        "vecs": _vecs(inputs["norm_mix_g"], inputs["norm_ffn_g"], inputs["pool_scale"], inputs["conv_w"],
                      inputs["q_norm_g"], inputs["k_norm_g"], inputs["forget_b"]),
        "cst": _consts(),
    }
    maps = []
    for b in cores:
        m = dict(shared)
        m["xT"] = np.ascontiguousarray(x[b].reshape(NCH, CH, KC, 128).transpose(0, 3, 2, 1))
        maps.append(m)
    return maps


def kernel(**inputs):
    inputs = {k: np.asarray(v) for k, v in inputs.items()}
    if "nc" not in _NC_CACHE:
        _NC_CACHE["nc"] = build(4)[0]
    nc = _NC_CACHE["nc"]
    cores = list(range(8))
    res = run_bass_kernel_spmd(nc, make_in_maps(inputs, cores), core_ids=cores)
    out = np.stack([np.ascontiguousarray(r["yT"].transpose(0, 3, 2, 1)).reshape(S, D) for r in res.results], axis=0)
    return out.astype(np.float32)
```
